# Optimizing a Trainium2 kernel written in Bass

```python
import math
import jax, jax.numpy as jnp
from jax import lax
import numpy as np

D_MODEL = 4096
BATCH = 2
SEQ = 4096
DEPTH = 2

CHUNK = 64
RMS_EPS = 1e-6
GDN_HEAD_DIM = 128
GDN_HEADS = D_MODEL // GDN_HEAD_DIM
GDN_CONV = 4
GDN_IN = 4 * GDN_HEADS * GDN_HEAD_DIM + 2 * GDN_HEADS
DIFF_HEAD_DIM = 128
DIFF_HEADS = D_MODEL // (2 * DIFF_HEAD_DIM)
DIFF_QK = DIFF_HEADS * 2 * DIFF_HEAD_DIM
DIFF_V = DIFF_HEADS * 2 * DIFF_HEAD_DIM
Q_BLOCK = 128
FFN_HIDDEN = -(-8 * D_MODEL // (3 * 256)) * 256

kernel_name = "yoco_gated_deltanet_diff_attention_hybrid"


def rms_norm(x, gain):
    xf = x.astype(jnp.float32)
    y = xf * lax.rsqrt(jnp.mean(xf * xf, axis=-1, keepdims=True) + RMS_EPS)
    return (y * gain.astype(jnp.float32)).astype(x.dtype)


def l2_normalize(x):
    xf = x.astype(jnp.float32)
    return xf * lax.rsqrt(jnp.sum(xf * xf, axis=-1, keepdims=True) + 1e-6)


def causal_depthwise_conv(x, w):
    k_width, channels = w.shape
    return lax.conv_general_dilated(
        x, w[:, None, :].astype(x.dtype), window_strides=(1,), padding=[(k_width - 1, 0)],
        dimension_numbers=('NWC', 'WIO', 'NWC'), feature_group_count=channels)


def gated_delta_rule(q, k, v, g, beta):
    B, S, H, dk = q.shape
    dv = v.shape[-1]
    n = S // CHUNK
    f32 = jnp.float32

    def to_chunks(t):
        t = t.astype(f32).reshape((B, n, CHUNK, H) + t.shape[3:])
        return jnp.moveaxis(t, 3, 1)

    q = to_chunks(q) * (dk ** -0.5)
    k = to_chunks(k)
    v = to_chunks(v)
    beta = to_chunks(beta)
    g = jnp.cumsum(to_chunks(g), axis=-1)
    causal = jnp.tril(jnp.ones((CHUNK, CHUNK), bool))
    strict = jnp.tril(jnp.ones((CHUNK, CHUNK), bool), -1)
    diff = g[..., :, None] - g[..., None, :]
    decay = jnp.where(causal, jnp.exp(jnp.where(causal, diff, 0.0)), 0.0)
    kb = k * beta[..., None]
    a_low = jnp.where(strict, jnp.einsum('bhnid,bhnjd->bhnij', kb, k) * decay, 0.0)
    eye = jnp.eye(CHUNK, dtype=f32)
    t_inv = lax.linalg.triangular_solve(eye + a_low, jnp.broadcast_to(eye, a_low.shape),
                                        left_side=True, lower=True)
    u = t_inv @ (v * beta[..., None])
    w = t_inv @ (kb * jnp.exp(g)[..., None])
    intra = jnp.where(causal, jnp.einsum('bhnid,bhnjd->bhnij', q, k) * decay, 0.0)
    g_last = g[..., -1:]
    q_dec = q * jnp.exp(g)[..., None]
    k_dec = k * jnp.exp(g_last - g)[..., None]
    state_decay = jnp.exp(g_last[..., 0])
    xs = tuple(jnp.moveaxis(t, 2, 0) for t in (q_dec, k_dec, u, w, intra, state_decay))

    def step(state, inp):
        qd, kd, u_c, w_c, a_c, sd = inp
        v_new = u_c - w_c @ state
        o = qd @ state + a_c @ v_new
        state = state * sd[..., None, None] + jnp.swapaxes(kd, -1, -2) @ v_new
        return state, o

    state0 = jnp.zeros((B, H, dk, dv), f32)
    _, o = lax.scan(step, state0, xs)
    return o.transpose(1, 0, 3, 2, 4).reshape(B, S, H, dv)


def gdn_mixer(h, w_in, conv_w, a_log, dt_bias, out_gain, w_out):
    B, S, _ = h.shape
    H, d = GDN_HEADS, GDN_HEAD_DIM
    proj = h @ w_in
    qkv = proj[..., :3 * H * d]
    z = proj[..., 3 * H * d:4 * H * d]
    b = proj[..., 4 * H * d:4 * H * d + H]
    a = proj[..., 4 * H * d + H:]
    qkv = jax.nn.silu(causal_depthwise_conv(qkv, conv_w))
    q = l2_normalize(qkv[..., :H * d].reshape(B, S, H, d))
    k = l2_normalize(qkv[..., H * d:2 * H * d].reshape(B, S, H, d))
    v = qkv[..., 2 * H * d:].reshape(B, S, H, d)
    beta = jax.nn.sigmoid(b.astype(jnp.float32))
    g = -jnp.exp(a_log.astype(jnp.float32)) * jax.nn.softplus(a.astype(jnp.float32) + dt_bias.astype(jnp.float32))
    o = gated_delta_rule(q, k, v, g, beta)
    o = rms_norm(o, out_gain) * jax.nn.silu(z.reshape(B, S, H, d).astype(jnp.float32))
    return o.reshape(B, S, H * d).astype(h.dtype) @ w_out


def shared_kv(x, kv_norm, w_kv, k_norm):
    B, S, _ = x.shape
    kv = rms_norm(x, kv_norm) @ w_kv
    k = rms_norm(kv[..., :DIFF_QK].reshape(B, S, DIFF_HEADS, 2, DIFF_HEAD_DIM), k_norm)
    v = kv[..., DIFF_QK:].reshape(B, S, DIFF_HEADS, 2 * DIFF_HEAD_DIM)
    return k, v


def diff_attention(h, k, v, w_q, q_gain, lam_params, sub_gain, w_out, lam_init):
    B, S, _ = h.shape
    H, d = DIFF_HEADS, DIFF_HEAD_DIM
    q = rms_norm((h @ w_q).reshape(B, S, H, 2, d), q_gain)
    lp = lam_params.astype(jnp.float32)
    lam = jnp.exp(jnp.sum(lp[0] * lp[1])) - jnp.exp(jnp.sum(lp[2] * lp[3])) + lam_init
    nb = S // Q_BLOCK
    qb = q.reshape(B, nb, Q_BLOCK, H, 2, d).transpose(1, 0, 2, 3, 4, 5)
    key_chunk = jnp.arange(S) // CHUNK
    scale = d ** -0.5

    def block(args):
        q_blk, start = args
        s = jnp.einsum('bqhmd,bkhmd->bhmqk', q_blk, k, preferred_element_type=jnp.float32) * scale
        q_chunk = (start + jnp.arange(Q_BLOCK)) // CHUNK
        mask = key_chunk[None, :] <= q_chunk[:, None]
        p = jax.nn.softmax(jnp.where(mask, s, -jnp.inf), axis=-1)
        attn = p[:, :, 0] - lam * p[:, :, 1]
        return jnp.einsum('bhqk,bkhe->bqhe', attn.astype(v.dtype), v)

    o = lax.map(block, (qb, jnp.arange(nb) * Q_BLOCK))
    o = o.transpose(1, 0, 2, 3, 4).reshape(B, S, H, 2 * d)
    o = rms_norm(o, sub_gain) * (1.0 - lam_init)
    return o.reshape(B, S, H * 2 * d).astype(h.dtype) @ w_out


def swiglu(h, w_gate_up, w_down):
    gu = h @ w_gate_up
    gate, up = gu[..., :FFN_HIDDEN], gu[..., FFN_HIDDEN:]
    return (jax.nn.silu(gate) * up) @ w_down


def setup_inputs(seed: int = 0) -> dict:
    key = jax.random.key(seed)
    ks = iter(jax.random.split(key, 32))
    n_a = DEPTH // 2
    n_b = DEPTH - n_a
    D = D_MODEL

    def normal(shape, scale):
        return jax.random.normal(next(ks), shape, jnp.float32) * scale

    def gain(shape):
        return 1.0 + normal(shape, 0.02)

    x = normal((BATCH, SEQ, D), 1.0)
    a_norm = gain((n_a, D))
    a_w_in = normal((n_a, D, GDN_IN), D ** -0.5)
    a_conv = normal((n_a, GDN_CONV, 3 * GDN_HEADS * GDN_HEAD_DIM), GDN_CONV ** -0.5)
    a_A_log = jnp.log(jax.random.uniform(next(ks), (n_a, GDN_HEADS), jnp.float32, 1.0, 16.0))
    dt = jnp.exp(jax.random.uniform(next(ks), (n_a, GDN_HEADS), jnp.float32, math.log(1e-3), math.log(1e-1)))
    a_dt_bias = dt + jnp.log(-jnp.expm1(-dt))
    a_out_norm = gain((n_a, GDN_HEAD_DIM))
    a_w_out = normal((n_a, GDN_HEADS * GDN_HEAD_DIM, D), (GDN_HEADS * GDN_HEAD_DIM) ** -0.5)
    kv_norm = gain((D,))
    w_kv = normal((D, DIFF_QK + DIFF_V), D ** -0.5)
    k_norm = gain((DIFF_HEAD_DIM,))
    b_norm = gain((n_b, D))
    b_w_q = normal((n_b, D, DIFF_QK), D ** -0.5)
    b_q_norm = gain((n_b, DIFF_HEAD_DIM))
    b_lambda = normal((n_b, 4, DIFF_HEAD_DIM), 0.1)
    b_sub_norm = gain((n_b, 2 * DIFF_HEAD_DIM))
    b_w_out = normal((n_b, DIFF_V, D), DIFF_V ** -0.5)
    ffn_norm = gain((DEPTH, D))
    ffn_w_gate_up = normal((DEPTH, D, 2 * FFN_HIDDEN), D ** -0.5)
    ffn_w_down = normal((DEPTH, FFN_HIDDEN, D), FFN_HIDDEN ** -0.5)
    return {"x": x, "a_norm": a_norm, "a_w_in": a_w_in, "a_conv": a_conv, "a_A_log": a_A_log,
            "a_dt_bias": a_dt_bias, "a_out_norm": a_out_norm, "a_w_out": a_w_out,
            "kv_norm": kv_norm, "w_kv": w_kv, "k_norm": k_norm,
            "b_norm": b_norm, "b_w_q": b_w_q, "b_q_norm": b_q_norm, "b_lambda": b_lambda,
            "b_sub_norm": b_sub_norm, "b_w_out": b_w_out,
            "ffn_norm": ffn_norm, "ffn_w_gate_up": ffn_w_gate_up, "ffn_w_down": ffn_w_down}


def reference(x, a_norm, a_w_in, a_conv, a_A_log, a_dt_bias, a_out_norm, a_w_out,
              kv_norm, w_kv, k_norm, b_norm, b_w_q, b_q_norm, b_lambda, b_sub_norm, b_w_out,
              ffn_norm, ffn_w_gate_up, ffn_w_down):
    n_a = DEPTH // 2
    k_sh = None
    v_sh = None
    for layer in range(DEPTH):
        if layer < n_a:
            i = layer
            x = x + gdn_mixer(rms_norm(x, a_norm[i]), a_w_in[i], a_conv[i], a_A_log[i],
                              a_dt_bias[i], a_out_norm[i], a_w_out[i])
        else:
            if layer == n_a:
                k_sh, v_sh = shared_kv(x, kv_norm, w_kv, k_norm)
            j = layer - n_a
            lam_init = 0.8 - 0.6 * math.exp(-0.3 * layer)
            x = x + diff_attention(rms_norm(x, b_norm[j]), k_sh, v_sh, b_w_q[j], b_q_norm[j],
                                   b_lambda[j], b_sub_norm[j], b_w_out[j], lam_init)
        x = x + swiglu(rms_norm(x, ffn_norm[layer]), ffn_w_gate_up[layer], ffn_w_down[layer])
    return x
```

```python
import math
import contextlib
import numpy as np
import concourse.bass as bass
import concourse.mybir as mybir
from concourse.bass_utils import run_bass_kernel_spmd

F32 = mybir.dt.float32
BF16 = mybir.dt.bfloat16
AF = mybir.ActivationFunctionType
ALU = mybir.AluOpType
AX = mybir.AxisListType

RMS_EPS = 1e-6
ARENA_WORDS = 41500


class Cfg:
    def __init__(s, D, S, F):
        s.D = D; s.S = S; s.F = F
        s.H = D // 128; s.GIN = 4 * D + 2 * s.H; s.DH = D // 256
        s.NT = S // 128; s.KC = D // 128


class _Op:
    __slots__ = ("eng", "fn", "deps", "dma", "signals", "sem", "val", "prev")

    def __init__(s, eng, fn, deps, dma):
        s.eng = eng; s.fn = fn; s.deps = deps; s.dma = dma
        s.signals = False; s.sem = None; s.val = 0; s.prev = 0


class _Rec:
    def __init__(s):
        s.calls = []

    def __getattr__(s, name):
        def f(*a, **k):
            s.calls.append((name, a, k))
            return s
        return f


class Sched:
    CE = ("pe", "act", "dve", "pool")
    EPOCH = 12000
    NSLOT = 12
    USES = 1500

    def __init__(s):
        s.ops = []
        s.last_w = {}
        s.readers = {}
        s.barrier_deps = {}
        s.last_on = {}
        s.dmas_since = []

    def op(s, eng, fn, r=(), w=(), dma=False):
        if fn is not None:
            rec = _Rec()
            fn(rec)
            assert len(rec.calls) == 1
            fn = rec.calls[0]
        i = len(s.ops)
        deps = {}
        for b in r:
            j = s.last_w.get(b)
            if j is not None:
                deps[j] = True
        for b in w:
            j = s.last_w.get(b)
            if j is not None and j not in deps:
                deps[j] = False
            for x in s.readers.get(b, ()):
                if x not in deps:
                    deps[x] = False
        bd = s.barrier_deps.pop(eng, None)
        if bd:
            for j in bd:
                deps[j] = True
        s.ops.append(_Op(eng, fn, deps, dma))
        for b in r:
            s.readers.setdefault(b, []).append(i)
        for b in w:
            s.last_w[b] = i
            s.readers[b] = []
        if dma:
            s.dmas_since.append(i)
        else:
            s.last_on[eng] = i
        return i

    def pe(s, fn, r=(), w=()): return s.op("pe", fn, r, w)
    def act(s, fn, r=(), w=()): return s.op("act", fn, r, w)
    def dve(s, fn, r=(), w=()): return s.op("dve", fn, r, w)
    def pool(s, fn, r=(), w=()): return s.op("pool", fn, r, w)

    def dma(s, q, out, in_, r=(), w=(), **kw):
        return s.op(q, lambda e: e.dma_start(out=out, in_=in_, **kw), r, w, dma=True)

    def barrier(s):
        deps = list(s.last_on.values()) + list(s.dmas_since)
        s.dmas_since = []
        s.last_w = {}
        s.readers = {}
        for e in ("pe", "act", "dve", "pool", "sp"):
            s.barrier_deps[e] = list(deps)

    def finish(s):
        s.barrier()
        s.op("sp", None)

    def emit(s, nc, stack):
        ops = s.ops
        for o in ops:
            keep = {}
            for j, raw in o.deps.items():
                p = ops[j]
                if p.dma or o.dma:
                    keep[j] = raw
                elif p.eng == o.eng:
                    if o.eng != "pe" and raw:
                        keep[j] = raw
                else:
                    keep[j] = raw
            o.deps = keep
            for j in keep:
                ops[j].signals = True
        cnt = {e: 0 for e in s.CE}
        dcnt = {"sp": 0, "pool": 0}
        for o in ops:
            if o.dma:
                n = dcnt[o.eng]; dcnt[o.eng] += 1
                slot = n % s.NSLOT; use = n // s.NSLOT
                o.sem = ("d", o.eng, (use // s.USES) * s.NSLOT + slot)
                o.prev = 16 * (use % s.USES)
                o.val = o.prev + 16
            elif o.signals:
                c = cnt[o.eng]; cnt[o.eng] += 1
                o.sem = ("c", o.eng, c // s.EPOCH)
                o.val = c % s.EPOCH + 1
        names = sorted({o.sem for o in ops if o.sem is not None}, key=str)
        sems = {}
        for k, nm in enumerate(names):
            sems[nm] = stack.enter_context(nc.semaphore("s%d" % k))
        assert len(sems) < 140, len(sems)
        per_eng = {e: [] for e in ("pe", "act", "dve", "pool", "sp")}
        for i, o in enumerate(ops):
            per_eng[o.eng].append(o)
        engmap = {"pe": "tensor", "act": "scalar", "dve": "vector", "pool": "gpsimd", "sp": "sync"}
        with nc.Block() as b0:
            def clr(g):
                for nm in names:
                    g.sem_clear(sems[nm])
            b0.gpsimd(clr)
        with nc.Block() as blk:
            for ename, lst in per_eng.items():
                if not lst:
                    continue

                def body(e, lst=lst):
                    known = {}
                    for o in lst:
                        for j in o.deps:
                            p = ops[j]
                            if known.get(p.sem, 0) < p.val:
                                e.wait_ge(sems[p.sem], p.val)
                                known[p.sem] = p.val
                        if o.dma and o.prev > 0 and known.get(o.sem, 0) < o.prev:
                            e.wait_ge(sems[o.sem], o.prev)
                            known[o.sem] = o.prev
                        if o.fn is None:
                            continue
                        nm_, a_, k_ = o.fn
                        inst = getattr(e, nm_)(*a_, **k_)
                        if o.dma:
                            inst.then_inc(sems[o.sem], 16)
                        elif o.signals:
                            inst.then_inc(sems[o.sem], 1)
                getattr(blk, engmap[ename])(body)


def make_consts():
    c = np.zeros((128, 6, 128), np.float32)
    i = np.arange(128)
    J, I = np.meshgrid(i, i, indexing="ij")
    same = (J // 64) == (I // 64)
    c[:, 0, :] = np.eye(128, dtype=np.float32)
    c[:, 1, :] = np.where(same & (I >= J), 0.0, -30000.0)
    c[:, 2, :] = np.where(same & (I > J), 1.0, 0.0)
    c[:, 3, :] = np.where(same & (J <= I), 1.0, 0.0)
    c[:, 4, :] = np.where((J >= 64) & (I < 64), 0.0, 1.0)
    c[:, 5, :] = 1.0
    return c.reshape(128, 6 * 128)


def build(cfg, debug=False):
    D, S_, F, H, DH, NT, KC, GIN = cfg.D, cfg.S, cfg.F, cfg.H, cfg.DH, cfg.NT, cfg.KC, cfg.GIN
    nc = bass.Bass("TRN2", target_bir_lowering=False)
    stack = contextlib.ExitStack()

    def din(name, shape, dt=F32):
        return nc.dram_tensor(name, list(shape), dt, kind="ExternalInput").ap()

    def dscr(name, shape, dt=F32):
        return nc.dram_tensor(name, list(shape), dt, kind="ExternalOutput" if debug else "Internal").ap()

    x_in = din("x", [S_, D])
    a_norm = din("a_norm", [1, D]); a_w_in = din("a_w_in", [D, GIN]); a_conv_t = din("a_conv_t", [3 * D, 4])
    a_A_log = din("a_A_log", [1, H]); a_dt_bias = din("a_dt_bias", [1, H])
    a_out_norm_t = din("a_out_norm_t", [1, D]); a_w_out = din("a_w_out", [D, D])
    kv_norm = din("kv_norm", [1, D]); w_kv = din("w_kv", [D, 2 * D]); k_norm = din("k_norm", [1, 128])
    b_norm = din("b_norm", [1, D]); b_w_q = din("b_w_q", [D, D]); b_q_norm = din("b_q_norm", [1, 128])
    b_lambda = din("b_lambda", [1, 4 * 128]); b_sub_norm_t = din("b_sub_norm_t", [1, D]); b_w_out = din("b_w_out", [D, D])
    ffn_norm = din("ffn_norm", [2, D]); ffn_w_gu = din("ffn_w_gate_up", [2, D, 2 * F]); ffn_w_down = din("ffn_w_down", [2, F, D])
    consts_d = din("consts", [128, 6 * 128])
    out_d = nc.dram_tensor("out", [S_, D], F32, kind="ExternalOutput").ap()

    q_tm = dscr("q_tm", [S_, D]); k_tm = dscr("k_tm", [S_, D]); v_tm = dscr("v_tm", [S_, D]); z_tm = dscr("z_tm", [S_, D])
    gb_tm = dscr("gb_tm", [S_, 2 * H]); gbT = dscr("gbT", [2 * H, S_])
    o_gdn = dscr("o_gdn", [S_, D]); x1 = dscr("x1", [S_, D]); x2 = dscr("x2", [S_, D]); x3 = dscr("x3", [S_, D])
    actT = dscr("actT", [F, S_], BF16)
    kT_d = dscr("kT_d", [2 * DH, 128, S_], BF16); qT_d = dscr("qT_d", [2 * DH, 128, S_], BF16)
    vb_d = dscr("vb_d", [S_, D], BF16); o_att = dscr("o_att", [S_, D])

    arena_t = stack.enter_context(nc.sbuf_tensor("arena", [128, ARENA_WORDS], F32))
    cst_t = stack.enter_context(nc.sbuf_tensor("cst", [128, 6 * 128], F32))
    cstb_t = stack.enter_context(nc.sbuf_tensor("cstb", [128, 128], BF16))
    banks = [stack.enter_context(nc.psum_tensor("bank%d" % i, [128, 512], F32)) for i in range(8)]

    Sc = Sched()
    cst = cst_t[:]
    IDENT = cst[:, 0:128]; MASKNEG = cst[:, 128:256]; STRICT = cst[:, 256:384]
    LCUM = cst[:, 384:512]; AMASK = cst[:, 512:640]; ONES = cst[:, 640:768]
    IDENTB = cstb_t[:]
    Sc.dma("sp", cst, consts_d, w=["cst"])
    Sc.dve(lambda e: e.tensor_copy(out=IDENTB, in_=IDENT), r=["cst"], w=["cstb"])
    Sc.barrier()

    class Arena:
        def __init__(s): s.off = 0; s.gen = 0
        def reset(s): s.off = 0; s.gen += 1
        def f32(s, n, name):
            a = arena_t[:, s.off:s.off + n]; s.off += n
            assert s.off <= ARENA_WORDS, (name, s.off)
            return a, (name, s.gen)
        def bf16(s, n, name):
            w = (n + 1) // 2
            a = arena_t[:, s.off:s.off + w].bitcast(BF16)[:, 0:n]; s.off += w
            assert s.off <= ARENA_WORDS, (name, s.off)
            return a, (name, s.gen)
    AR = Arena()

    def bank_f32(i): return banks[i][:]
    def bank_bf16(i): return banks[i][:].bitcast(BF16)

    rot = {"acc": 0, "aux": 0}
    def next_acc():
        i = rot["acc"] % 6; rot["acc"] += 1; return i
    def next_aux():
        i = 6 + rot["aux"] % 2; rot["aux"] += 1; return i

    def bcast_row(ap_row, n):
        return ap_row.partition_broadcast(128).rearrange("p o n -> p (o n)")

    def make_prov_tm(src, K, TG, prologue, extra_src=None):
        kc_n = K // 128
        hT, hT_tok = AR.bf16(kc_n * TG, "hT")
        hT3 = hT.rearrange("p (k t) -> p k t", t=TG)
        xin, xin_tok = AR.f32(K, "xin")
        xb, xb_tok = AR.bf16(K, "xb")
        zin = None
        if extra_src is not None:
            zin, zin_tok = AR.f32(K, "zin")
        ntt = TG // 128

        def prov(tg):
            for tt in range(ntt):
                r0 = tg * TG + tt * 128
                Sc.dma("sp", xin, src[r0:r0 + 128, :], w=[xin_tok])
                rd = [xin_tok]
                if zin is not None:
                    Sc.dma("sp", zin, extra_src[r0:r0 + 128, :], w=[zin_tok])
                    rd.append(zin_tok)
                prologue(xin, zin, xb, rd, xb_tok)
                for k0 in range(0, kc_n, 4):
                    nk = min(4, kc_n - k0)
                    bi = next_aux()
                    pb = bank_bf16(bi)
                    for kk in range(nk):
                        Sc.pe(lambda e, kk=kk, k0=k0, pb=pb: e.transpose(
                            out=pb[:, kk * 128:(kk + 1) * 128], in_=xb[:, (k0 + kk) * 128:(k0 + kk + 1) * 128], identity=IDENTB),
                            r=[xb_tok, "cstb"], w=[("ps", bi)])
                    src_ap = pb[:, 0:nk * 128].rearrange("p (k t) -> p k t", t=128)
                    dst_ap = hT3[:, k0:k0 + nk, tt * 128:(tt + 1) * 128]
                    if (k0 // 4) % 2 == 0:
                        Sc.dve(lambda e, d=dst_ap, s_=src_ap: e.tensor_copy(out=d, in_=s_), r=[("ps", bi)], w=[hT_tok])
                    else:
                        Sc.act(lambda e, d=dst_ap, s_=src_ap: e.activation(out=d, in_=s_, func=AF.Copy), r=[("ps", bi)], w=[hT_tok])
        return prov, hT3, hT_tok

    def make_prov_fm(src, K, TG):
        kc_n = K // 128
        hT, hT_tok = AR.bf16(kc_n * TG, "hT")
        hT3 = hT.rearrange("p (k t) -> p k t", t=TG)

        def prov(tg):
            step = 16
            for k0 in range(0, kc_n, step):
                nk = min(step, kc_n - k0)
                Sc.dma("sp", hT3[:, k0:k0 + nk, :],
                       src[k0 * 128:(k0 + nk) * 128, tg * TG:(tg + 1) * TG].rearrange("(k p) t -> p k t", p=128),
                       w=[hT_tok])
        return prov, hT3, hT_tok

    def rms_prologue_factory(gain_row_ap, K, group=None, post_scale=1.0):
        gain_b, gain_tok = AR.f32(K, "gain")
        sq, sq_tok = AR.f32(K, "sq")
        ng = 1 if group is None else K // group
        st, st_tok = AR.f32(4 * ng + 4, "stat")
        Sc.dma("sp", gain_b, bcast_row(gain_row_ap, K), w=[gain_tok])
        gsz = K if group is None else group

        def prologue(xin, zin, xb, rd, xb_tok):
            Sc.act(lambda e: e.activation(out=sq, in_=xin, func=AF.Square), r=rd[:1], w=[sq_tok])
            ss = st[:, 0:ng]; rs = st[:, ng:2 * ng]
            Sc.dve(lambda e: e.tensor_reduce(out=ss, in_=sq.rearrange("p (g c) -> p g c", c=gsz), axis=AX.X, op=ALU.add),
                   r=[sq_tok], w=[st_tok])
            ps2 = post_scale * post_scale
            Sc.dve(lambda e: e.tensor_scalar(out=rs, in0=ss, scalar1=1.0 / (gsz * ps2), scalar2=RMS_EPS / ps2, op0=ALU.mult, op1=ALU.add),
                   r=[st_tok], w=[st_tok])
            Sc.act(lambda e: e.activation(out=rs, in_=rs, func=AF.Sqrt), r=[st_tok], w=[st_tok])
            Sc.dve(lambda e: e.reciprocal(out=rs, in_=rs), r=[st_tok], w=[st_tok])
            if group is None:
                Sc.dve(lambda e: e.scalar_tensor_tensor(out=xb, in0=xin, scalar=rs[:, 0:1], in1=gain_b, op0=ALU.mult, op1=ALU.mult),
                       r=[rd[0], st_tok, gain_tok], w=[xb_tok])
            else:
                if zin is not None:
                    Sc.act(lambda e: e.activation(out=zin, in_=zin, func=AF.Silu), r=[rd[1]], w=[rd[1]])
                x3_ = xin.rearrange("p (g c) -> p g c", c=gsz)
                rsb = rs.unsqueeze(2).broadcast_to([128, ng, gsz])
                sq3 = sq.rearrange("p (g c) -> p g c", c=gsz)
                Sc.dve(lambda e: e.tensor_tensor(out=sq3, in0=x3_, in1=rsb, op=ALU.mult), r=[rd[0], st_tok, sq_tok], w=[sq_tok])
                if zin is not None:
                    Sc.dve(lambda e: e.tensor_tensor(out=sq, in0=sq, in1=gain_b, op=ALU.mult), r=[sq_tok, gain_tok], w=[sq_tok])
                    Sc.dve(lambda e: e.tensor_tensor(out=xb, in0=sq, in1=zin, op=ALU.mult), r=[sq_tok, rd[1]], w=[xb_tok])
                else:
                    Sc.dve(lambda e: e.tensor_tensor(out=xb, in0=sq, in1=gain_b, op=ALU.mult), r=[sq_tok, gain_tok], w=[xb_tok])
        return prologue

    def linear(K, w_ap, blocks, prov, hT3, hT_tok, TG, mode, epilogue, ntg):
        kc_n = K // 128
        KS = 16
        nks = (kc_n + KS - 1) // KS
        NSL = 2
        slots = []
        for i in range(NSL):
            a, tok = AR.bf16(KS * 512, "wslot%d" % i)
            slots.append((a.rearrange("p (k c) -> p k c", c=512), tok))
        wcnt = [0]
        ntt = TG // 128
        for tg in range(ntg):
            prov(tg)
            for bi_, segs in enumerate(blocks):
                ncols = sum(n for _, n in segs)
                if mode == "fm":
                    nacc = (ncols + 127) // 128
                else:
                    nacc = ntt
                accs = [next_acc() for _ in range(nacc)]
                for ks in range(nks):
                    k0 = ks * KS; nk = min(KS, kc_n - k0)
                    sl, sl_tok = slots[wcnt[0] % NSL]; wcnt[0] += 1
                    off = 0
                    for (c0, n) in segs:
                        Sc.dma("pool", sl[:, 0:nk, off:off + n],
                               w_ap[k0 * 128:(k0 + nk) * 128, c0:c0 + n].rearrange("(k p) c -> p k c", p=128),
                               w=[sl_tok], max_dma_last_dim=2048)
                        off += n
                    for kk in range(nk):
                        first = (ks == 0 and kk == 0); last = (ks == nks - 1 and kk == nk - 1)
                        for ai in range(nacc):
                            if mode == "fm":
                                cw = min(128, ncols - ai * 128)
                                Sc.pe(lambda e, ai=ai, kk=kk, k0=k0, sl=sl, cw=cw, first=first, last=last, b=accs[ai]: e.matmul(
                                    out=bank_f32(b)[0:cw, 0:TG], lhsT=sl[:, kk, ai * 128:ai * 128 + cw], rhs=hT3[:, k0 + kk, :],
                                    start=first, stop=last), r=[sl_tok, hT_tok], w=[("ps", accs[ai])])
                            else:
                                Sc.pe(lambda e, ai=ai, kk=kk, k0=k0, sl=sl, first=first, last=last, b=accs[ai]: e.matmul(
                                    out=bank_f32(b)[:, 0:ncols], lhsT=hT3[:, k0 + kk, ai * 128:(ai + 1) * 128], rhs=sl[:, kk, 0:ncols],
                                    start=first, stop=last), r=[sl_tok, hT_tok], w=[("ps", accs[ai])])
                epilogue(tg, bi_, segs, accs)

    def phase_A():
        AR.reset()
        TG = 512 if S_ >= 512 else S_
        ntg = S_ // TG
        prol = rms_prologue_factory(a_norm, D)
        prov, hT3, hT_tok = make_prov_tm(x_in, D, TG, prol)
        NCH = 3 * H
        cw_t, cw_tok = AR.f32(NCH * 4, "convw")
        cw3 = cw_t.rearrange("p (c k) -> p c k", k=4)
        Sc.dma("sp", cw3, a_conv_t.rearrange("(c p) k -> p c k", p=128), w=[cw_tok])
        halo, halo_tok = AR.f32(NCH * 3, "halo")
        halo3 = halo.rearrange("p (c k) -> p c k", k=3)
        Sc.dve(lambda e: e.memset(halo, 0.0), w=[halo_tok])
        pbufs = [AR.f32(3 + TG, "pbuf%d" % i) for i in range(2)]
        ybufs = [AR.f32(TG, "ybuf%d" % i) for i in range(2)]
        obufs = [AR.f32(TG, "obuf%d" % i) for i in range(2)]
        sqb, sqb_tok = AR.f32(TG, "sqb")
        stt, stt_tok = AR.f32(16, "stt")
        zbufs = [AR.f32(512, "zbuf%d" % i) for i in range(2)]
        negA, negA_tok = AR.f32(H, "negA")
        dtb, dtb_tok = AR.f32(H, "dtb")
        Sc.dma("sp", negA, bcast_row(a_A_log, H), w=[negA_tok])
        Sc.dma("sp", dtb, bcast_row(a_dt_bias, H), w=[dtb_tok])
        Sc.act(lambda e: e.activation(out=negA, in_=negA, func=AF.Exp), r=[negA_tok], w=[negA_tok])
        Sc.dve(lambda e: e.tensor_scalar(out=negA, in0=negA, scalar1=-1.0, scalar2=None, op0=ALU.mult), r=[negA_tok], w=[negA_tok])
        gbs = [AR.f32(2 * H, "gb%d" % i) for i in range(2)]
        gtmp = [AR.f32(2 * H, "gtmp%d" % i) for i in range(2)]
        gbTs = [AR.f32(128, "gbTs%d" % i) for i in range(2)]
        cnt = [0]
        ntt = TG // 128

        def epi(tg, bi_, segs, accs):
            c0 = segs[0][0]
            if c0 < 3 * D:
                for ai, b in enumerate(accs):
                    ch = c0 // 128 + ai
                    which = ch // H; head = ch % H
                    i2 = cnt[0] % 2; cnt[0] += 1
                    (pb, pb_tok), (yb, yb_tok), (ob, ob_tok) = pbufs[i2], ybufs[i2], obufs[i2]
                    Sc.dve(lambda e, pb=pb, ch=ch: e.tensor_copy(out=pb[:, 0:3], in_=halo3[:, ch, :]), r=[halo_tok], w=[pb_tok])
                    Sc.act(lambda e, pb=pb, b=b: e.activation(out=pb[:, 3:3 + TG], in_=bank_f32(b)[:, 0:TG], func=AF.Copy),
                           r=[("ps", b)], w=[pb_tok])
                    Sc.dve(lambda e, pb=pb, ch=ch: e.tensor_copy(out=halo3[:, ch, :], in_=pb[:, TG:TG + 3]), r=[pb_tok], w=[halo_tok])
                    Sc.dve(lambda e, pb=pb, yb=yb, ch=ch: e.tensor_scalar(out=yb, in0=pb[:, 0:TG], scalar1=cw3[:, ch, 0:1], scalar2=None, op0=ALU.mult),
                           r=[pb_tok, cw_tok], w=[yb_tok])
                    for k in range(1, 4):
                        Sc.dve(lambda e, pb=pb, yb=yb, ch=ch, k=k: e.scalar_tensor_tensor(
                            out=yb, in0=pb[:, k:k + TG], scalar=cw3[:, ch, k:k + 1], in1=yb, op0=ALU.mult, op1=ALU.add),
                            r=[pb_tok, cw_tok, yb_tok], w=[yb_tok])
                    Sc.act(lambda e, yb=yb: e.activation(out=yb, in_=yb, func=AF.Silu), r=[yb_tok], w=[yb_tok])
                    xb_ = next_aux()
                    for tt in range(ntt):
                        Sc.pe(lambda e, yb=yb, tt=tt, xb_=xb_: e.transpose(out=bank_f32(xb_)[:, tt * 128:(tt + 1) * 128],
                              in_=yb[:, tt * 128:(tt + 1) * 128], identity=IDENT), r=[yb_tok, "cst"], w=[("ps", xb_)])
                    if which < 2:
                        Sc.act(lambda e, xb_=xb_: e.activation(out=sqb, in_=bank_f32(xb_)[:, 0:TG], func=AF.Square), r=[("ps", xb_)], w=[sqb_tok])
                        Sc.dve(lambda e: e.tensor_reduce(out=stt[:, 0:ntt], in_=sqb.rearrange("p (t c) -> p t c", c=128), axis=AX.X, op=ALU.add),
                               r=[sqb_tok], w=[stt_tok])
                        Sc.dve(lambda e: e.tensor_scalar(out=stt[:, 4:4 + ntt], in0=stt[:, 0:ntt], scalar1=1e-6, scalar2=None, op0=ALU.add),
                               r=[stt_tok], w=[stt_tok])
                        Sc.act(lambda e: e.activation(out=stt[:, 4:4 + ntt], in_=stt[:, 4:4 + ntt], func=AF.Sqrt), r=[stt_tok], w=[stt_tok])
                        Sc.dve(lambda e: e.reciprocal(out=stt[:, 4:4 + ntt], in_=stt[:, 4:4 + ntt]), r=[stt_tok], w=[stt_tok])
                        for tt in range(ntt):
                            Sc.dve(lambda e, ob=ob, tt=tt, xb_=xb_: e.tensor_scalar(out=ob[:, tt * 128:(tt + 1) * 128],
                                   in0=bank_f32(xb_)[:, tt * 128:(tt + 1) * 128], scalar1=stt[:, 4 + tt:5 + tt], scalar2=None, op0=ALU.mult),
                                   r=[("ps", xb_), stt_tok], w=[ob_tok])
                    else:
                        Sc.act(lambda e, ob=ob, xb_=xb_: e.activation(out=ob, in_=bank_f32(xb_)[:, 0:TG], func=AF.Copy), r=[("ps", xb_)], w=[ob_tok])
                    dst = (q_tm, k_tm, v_tm)[which]
                    Sc.dma("sp", dst[tg * TG:(tg + 1) * TG, head * 128:(head + 1) * 128].rearrange("(t p) c -> p t c", p=128),
                           ob.rearrange("p (t c) -> p t c", c=128), r=[ob_tok])
            elif c0 < 4 * D:
                ncols = sum(n for _, n in segs)
                for tt, b in enumerate(accs):
                    i2 = cnt[0] % 2; cnt[0] += 1
                    zb, zb_tok = zbufs[i2]
                    Sc.act(lambda e, zb=zb, b=b: e.activation(out=zb[:, 0:ncols], in_=bank_f32(b)[:, 0:ncols], func=AF.Copy), r=[("ps", b)], w=[zb_tok])
                    r0 = tg * TG + tt * 128
                    Sc.dma("sp", z_tm[r0:r0 + 128, c0 - 3 * D:c0 - 3 * D + ncols], zb[:, 0:ncols], r=[zb_tok])
            else:
                for tt, b in enumerate(accs):
                    i2 = cnt[0] % 2; cnt[0] += 1
                    (gb, gb_tok), (gt, gt_tok), (gT, gT_tok) = gbs[i2], gtmp[i2], gbTs[i2]
                    pb = bank_f32(b)
                    Sc.act(lambda e, gb=gb, pb=pb: e.activation(out=gb[:, H:2 * H], in_=pb[:, 0:H], func=AF.Sigmoid), r=[("ps", b)], w=[gb_tok])
                    Sc.dve(lambda e, gt=gt, pb=pb: e.tensor_tensor(out=gt[:, 0:H], in0=pb[:, H:2 * H], in1=dtb, op=ALU.add), r=[("ps", b), dtb_tok], w=[gt_tok])
                    Sc.act(lambda e, gt=gt: e.activation(out=gt[:, 0:H], in_=gt[:, 0:H], func=AF.Exp), r=[gt_tok], w=[gt_tok])
                    Sc.dve(lambda e, gt=gt: e.tensor_scalar(out=gt[:, 0:H], in0=gt[:, 0:H], scalar1=1.0, scalar2=None, op0=ALU.add), r=[gt_tok], w=[gt_tok])
                    Sc.act(lambda e, gt=gt: e.activation(out=gt[:, 0:H], in_=gt[:, 0:H], func=AF.Ln), r=[gt_tok], w=[gt_tok])
                    Sc.dve(lambda e, gt=gt: e.tensor_tensor(out=gt[:, H:2 * H], in0=gt[:, 0:H], in1=negA, op=ALU.mult), r=[gt_tok, negA_tok], w=[gt_tok])
                    xb_ = next_aux()
                    Sc.pe(lambda e, gt=gt, xb_=xb_: e.matmul(out=bank_f32(xb_)[:, 0:H], lhsT=LCUM, rhs=gt[:, H:2 * H], start=True, stop=True),
                          r=[gt_tok, "cst"], w=[("ps", xb_)])
                    Sc.dve(lambda e, gb=gb, xb_=xb_: e.tensor_copy(out=gb[:, 0:H], in_=bank_f32(xb_)[:, 0:H]), r=[("ps", xb_)], w=[gb_tok])
                    r0 = tg * TG + tt * 128
                    Sc.dma("sp", gb_tm[r0:r0 + 128, :], gb, r=[gb_tok])
                    xc_ = next_aux()
                    Sc.pe(lambda e, gb=gb, xc_=xc_: e.transpose(out=bank_f32(xc_)[0:2 * H, 0:128], in_=gb, identity=IDENT), r=[gb_tok, "cst"], w=[("ps", xc_)])
                    Sc.act(lambda e, gT=gT, xc_=xc_: e.activation(out=gT[0:2 * H, :], in_=bank_f32(xc_)[0:2 * H, 0:128], func=AF.Copy), r=[("ps", xc_)], w=[gT_tok])
                    Sc.dma("sp", gbT[:, r0:r0 + 128], gT[0:2 * H, :], r=[gT_tok])

        blocks_fm = [[(c, 512)] for c in range(0, 3 * D, 512)]
        blocks_z = [[(3 * D + c, 512)] for c in range(0, D, 512)]
        blocks_ba = [[(4 * D, 2 * H)]]
        linear_multi(D, a_w_in, [("fm", blocks_fm), ("tm", blocks_z), ("tm", blocks_ba)], prov, hT3, hT_tok, TG, epi, ntg)

    def linear_multi(K, w_ap, groups, prov, hT3, hT_tok, TG, epilogue, ntg):
        state = {"first": True}
        for tg in range(ntg):
            prov(tg)
            for mode, blocks in groups:
                linear_one_tg(K, w_ap, blocks, hT3, hT_tok, TG, mode, epilogue, tg)

    lin_state = {}

    def linear_one_tg(K, w_ap, blocks, hT3, hT_tok, TG, mode, epilogue, tg):
        kc_n = K // 128
        KS = 16
        nks = (kc_n + KS - 1) // KS
        NSL = 2
        key = AR.gen
        if lin_state.get("gen") != key:
            slots = []
            for i in range(NSL):
                a, tok = AR.bf16(KS * 512, "wslot%d" % i)
                slots.append((a.rearrange("p (k c) -> p k c", c=512), tok))
            lin_state["gen"] = key; lin_state["slots"] = slots; lin_state["cnt"] = 0
        slots = lin_state["slots"]
        ntt = TG // 128
        for bi_, segs in enumerate(blocks):
            ncols = sum(n for _, n in segs)
            nacc = (ncols + 127) // 128 if mode == "fm" else ntt
            accs = [next_acc() for _ in range(nacc)]
            for ks in range(nks):
                k0 = ks * KS; nk = min(KS, kc_n - k0)
                sl, sl_tok = slots[lin_state["cnt"] % NSL]; lin_state["cnt"] += 1
                off = 0
                for (c0, n) in segs:
                    Sc.dma("pool", sl[:, 0:nk, off:off + n],
                           w_ap[k0 * 128:(k0 + nk) * 128, c0:c0 + n].rearrange("(k p) c -> p k c", p=128),
                           w=[sl_tok], max_dma_last_dim=2048)
                    off += n
                for kk in range(nk):
                    first = (ks == 0 and kk == 0); last = (ks == nks - 1 and kk == nk - 1)
                    for ai in range(nacc):
                        if mode == "fm":
                            cw = min(128, ncols - ai * 128)
                            Sc.pe(lambda e, ai=ai, kk=kk, k0=k0, sl=sl, cw=cw, first=first, last=last, b=accs[ai]: e.matmul(
                                out=bank_f32(b)[0:cw, 0:TG], lhsT=sl[:, kk, ai * 128:ai * 128 + cw], rhs=hT3[:, k0 + kk, :],
                                start=first, stop=last), r=[sl_tok, hT_tok], w=[("ps", accs[ai])])
                        else:
                            Sc.pe(lambda e, ai=ai, kk=kk, k0=k0, sl=sl, ncols=ncols, first=first, last=last, b=accs[ai]: e.matmul(
                                out=bank_f32(b)[:, 0:ncols], lhsT=hT3[:, k0 + kk, ai * 128:(ai + 1) * 128], rhs=sl[:, kk, 0:ncols],
                                start=first, stop=last), r=[sl_tok, hT_tok], w=[("ps", accs[ai])])
            epilogue(tg, bi_, segs, accs)

    def phase_B():
        AR.reset()
        scale = 128.0 ** -0.5
        gbcol, gbcol_tok = AR.f32(NT * 2 * H, "gbcol")
        gbcol3 = gbcol.rearrange("p (t c) -> p t c", c=2 * H)
        Sc.dma("sp", gbcol3, gb_tm.rearrange("(t p) c -> p t c", p=128), w=[gbcol_tok])
        bgcol, bgcol_tok = AR.f32(NT * H, "bgcol")
        bgcol3 = bgcol.rearrange("p (t c) -> p t c", c=H)
        Sc.act(lambda e: e.activation(out=bgcol3, in_=gbcol3[:, :, 0:H], func=AF.Exp), r=[gbcol_tok], w=[bgcol_tok])
        Sc.dve(lambda e: e.tensor_tensor(out=bgcol3, in0=bgcol3, in1=gbcol3[:, :, H:2 * H], op=ALU.mult), r=[bgcol_tok, gbcol_tok], w=[bgcol_tok])
        G = 2 if H >= 2 else 1
        streams = []
        for gi in range(G):
            st = {}
            def T(n, name, gi=gi):
                return AR.f32(n, "%s_%d" % (name, gi))
            st["Grow"] = T(S_, "Grow"); st["Brow"] = T(S_, "Brow")
            st["dl"] = T(NT, "dl")
            for nm in ("q0", "k0", "v0", "q1", "k1", "v1", "kT", "qT", "DT", "DTs", "X", "A", "R", "P", "Q", "P2", "Q2", "AcT", "vb", "kbg",
                       "u", "wT", "qdT", "EG", "kdA", "kdB", "vnew", "osb0", "osb1", "Sst", "tmp"):
                st[nm] = T(128, nm)
            streams.append(st)
        bank_rot = [0]
        def nb():
            i = bank_rot[0] % 8; bank_rot[0] += 1; return i

        def head_gen(st, h):
            Grow, Grow_tok = st["Grow"]; Brow, Brow_tok = st["Brow"]; dl, dl_tok = st["dl"]
            Sc.dma("sp", Grow, gbT[h:h + 1, :].partition_broadcast(128).rearrange("p o n -> p (o n)"), w=[Grow_tok])
            Sc.dma("sp", Brow, gbT[H + h:H + h + 1, :].partition_broadcast(128).rearrange("p o n -> p (o n)"), w=[Brow_tok])
            Sc.dve(lambda e: e.tensor_tensor(out=dl[0:64, :], in0=Grow[0:64, 63:S_:128], in1=gbcol3[0:64, :, h], op=ALU.subtract),
                   r=[Grow_tok, gbcol_tok], w=[dl_tok])
            Sc.dve(lambda e: e.tensor_tensor(out=dl[64:128, :], in0=Grow[64:128, 127:S_:128], in1=gbcol3[64:128, :, h], op=ALU.subtract),
                   r=[Grow_tok, gbcol_tok], w=[dl_tok])
            Sc.act(lambda e: e.activation(out=dl, in_=dl, func=AF.Exp), r=[dl_tok], w=[dl_tok])
            Sst, Sst_tok = st["Sst"]
            Sc.dve(lambda e: e.memset(Sst, 0.0), w=[Sst_tok])
            vnew, vnew_tok = st["vnew"]
            Sc.dve(lambda e: e.memset(vnew, 0.0), w=[vnew_tok])
            kdA, kdA_tok = st["kdA"]; kdB, kdB_tok = st["kdB"]
            Sc.dve(lambda e: e.memset(kdA, 0.0), w=[kdA_tok])
            Sc.dve(lambda e: e.memset(kdB, 0.0), w=[kdB_tok])
            yield
            for t in range(NT):
                par = t % 2
                (qt, qt_tok), (kt, kt_tok), (vt, vt_tok) = st["q%d" % par], st["k%d" % par], st["v%d" % par]
                r0 = t * 128
                hs = slice(h * 128, (h + 1) * 128)
                Sc.dma("sp", qt, q_tm[r0:r0 + 128, hs], w=[qt_tok])
                Sc.dma("sp", kt, k_tm[r0:r0 + 128, hs], w=[kt_tok])
                Sc.dma("sp", vt, v_tm[r0:r0 + 128, hs], w=[vt_tok])
                gcol = gbcol3[:, t, h:h + 1]; bcol = gbcol3[:, t, H + h:H + h + 1]; bgc = bgcol3[:, t, h:h + 1]
                Gr = Grow[:, r0:r0 + 128]; Br = Brow[:, r0:r0 + 128]
                kT, kT_tok = st["kT"]; qT, qT_tok = st["qT"]
                b1 = nb()
                Sc.pe(lambda e, b1=b1: e.transpose(out=bank_f32(b1)[:, 0:128], in_=kt, identity=IDENT), r=[kt_tok, "cst"], w=[("ps", b1)])
                Sc.act(lambda e, b1=b1: e.activation(out=kT, in_=bank_f32(b1)[:, 0:128], func=AF.Copy), r=[("ps", b1)], w=[kT_tok])
                b2 = nb()
                Sc.pe(lambda e, b2=b2: e.transpose(out=bank_f32(b2)[:, 0:128], in_=qt, identity=IDENT), r=[qt_tok, "cst"], w=[("ps", b2)])
                Sc.dve(lambda e, b2=b2: e.tensor_copy(out=qT, in_=bank_f32(b2)[:, 0:128]), r=[("ps", b2)], w=[qT_tok])
                yield
                DT, DT_tok = st["DT"]; DTs, DTs_tok = st["DTs"]; EG, EG_tok = st["EG"]
                Sc.dve(lambda e: e.scalar_tensor_tensor(out=DT, in0=Gr, scalar=gcol, in1=MASKNEG, op0=ALU.subtract, op1=ALU.add),
                       r=[Grow_tok, gbcol_tok, "cst"], w=[DT_tok])
                Sc.act(lambda e: e.activation(out=DT, in_=DT, func=AF.Exp), r=[DT_tok], w=[DT_tok])
                Sc.act(lambda e: e.activation(out=EG, in_=Gr, func=AF.Exp), r=[Grow_tok], w=[EG_tok])
                Sc.dve(lambda e: e.tensor_tensor(out=DTs, in0=DT, in1=STRICT, op=ALU.mult), r=[DT_tok, "cst"], w=[DTs_tok])
                bkk = nb()
                Sc.pe(lambda e, bkk=bkk: e.matmul(out=bank_f32(bkk)[:, 0:128], lhsT=kT, rhs=kT, start=True, stop=True), r=[kT_tok], w=[("ps", bkk)])
                bqk = nb()
                Sc.pe(lambda e, bqk=bqk: e.matmul(out=bank_f32(bqk)[:, 0:128], lhsT=kT, rhs=qT, start=True, stop=True), r=[kT_tok, qT_tok], w=[("ps", bqk)])
                X, X_tok = st["X"]; A, A_tok = st["A"]; R, R_tok = st["R"]; AcT, AcT_tok = st["AcT"]
                Sc.dve(lambda e, bkk=bkk: e.tensor_tensor(out=X, in0=bank_f32(bkk)[:, 0:128], in1=Br, op=ALU.mult), r=[("ps", bkk), Brow_tok], w=[X_tok])
                Sc.dve(lambda e: e.tensor_tensor(out=X, in0=X, in1=DTs, op=ALU.mult), r=[X_tok, DTs_tok], w=[X_tok])
                Sc.dve(lambda e, bqk=bqk: e.scalar_tensor_tensor(out=AcT, in0=bank_f32(bqk)[:, 0:128], scalar=scale, in1=DT, op0=ALU.mult, op1=ALU.mult),
                       r=[("ps", bqk), DT_tok], w=[AcT_tok])
                yield
                ba = nb()
                Sc.pe(lambda e, ba=ba: e.transpose(out=bank_f32(ba)[:, 0:128], in_=X, identity=IDENT), r=[X_tok, "cst"], w=[("ps", ba)])
                Sc.act(lambda e, ba=ba: e.activation(out=A, in_=bank_f32(ba)[:, 0:128], func=AF.Copy), r=[("ps", ba)], w=[A_tok])
                Sc.dve(lambda e: e.tensor_tensor(out=R, in0=IDENT, in1=X, op=ALU.subtract), r=["cst", X_tok], w=[R_tok])
                Pc, Pc_tok = X, X_tok
                Qc, Qc_tok = A, A_tok
                pq = [(st["P"], st["Q"]), (st["P2"], st["Q2"])]
                for kstage in range(1, 6):
                    (Pn, Pn_tok), (Qn, Qn_tok) = pq[kstage % 2]
                    bq = nb()
                    Sc.pe(lambda e, bq=bq, Pc=Pc, Qc=Qc: e.matmul(out=bank_f32(bq)[:, 0:128], lhsT=Pc, rhs=Qc, start=True, stop=True),
                          r=[Pc_tok, Qc_tok], w=[("ps", bq)])
                    Sc.dve(lambda e, bq=bq, Qn=Qn: e.tensor_copy(out=Qn, in_=bank_f32(bq)[:, 0:128]), r=[("ps", bq)], w=[Qn_tok])
                    if kstage < 5:
                        bp = nb()
                        Sc.pe(lambda e, bp=bp, Pc=Pc, Qc=Qc: e.matmul(out=bank_f32(bp)[:, 0:128], lhsT=Qc, rhs=Pc, start=True, stop=True),
                              r=[Pc_tok, Qc_tok], w=[("ps", bp)])
                        Sc.act(lambda e, bp=bp, Pn=Pn: e.activation(out=Pn, in_=bank_f32(bp)[:, 0:128], func=AF.Copy), r=[("ps", bp)], w=[Pn_tok])
                    bm = nb()
                    Sc.pe(lambda e, bm=bm, Qn=Qn: e.matmul(out=bank_f32(bm)[:, 0:128], lhsT=Qn, rhs=R, start=True, stop=True),
                          r=[Qn_tok, R_tok], w=[("ps", bm)])
                    Sc.dve(lambda e, bm=bm: e.tensor_tensor(out=R, in0=R, in1=bank_f32(bm)[:, 0:128], op=ALU.add), r=[R_tok, ("ps", bm)], w=[R_tok])
                    Pc, Pc_tok, Qc, Qc_tok = Pn, Pn_tok, Qn, Qn_tok
                    yield
                vb, vb_tok = st["vb"]; kbg, kbg_tok = st["kbg"]; u, u_tok = st["u"]; wT, wT_tok = st["wT"]; qdT, qdT_tok = st["qdT"]
                Sc.dve(lambda e: e.tensor_scalar(out=vb, in0=vt, scalar1=bcol, scalar2=None, op0=ALU.mult), r=[vt_tok, gbcol_tok], w=[vb_tok])
                Sc.dve(lambda e: e.tensor_scalar(out=kbg, in0=kt, scalar1=bgc, scalar2=None, op0=ALU.mult), r=[kt_tok, bgcol_tok], w=[kbg_tok])
                bu = nb()
                Sc.pe(lambda e, bu=bu: e.matmul(out=bank_f32(bu)[:, 0:128], lhsT=R, rhs=vb, start=True, stop=True), r=[R_tok, vb_tok], w=[("ps", bu)])
                Sc.act(lambda e, bu=bu: e.activation(out=u, in_=bank_f32(bu)[:, 0:128], func=AF.Copy), r=[("ps", bu)], w=[u_tok])
                bw = nb()
                Sc.pe(lambda e, bw=bw: e.matmul(out=bank_f32(bw)[:, 0:128], lhsT=kbg, rhs=R, start=True, stop=True), r=[R_tok, kbg_tok], w=[("ps", bw)])
                Sc.dve(lambda e, bw=bw: e.tensor_copy(out=wT, in_=bank_f32(bw)[:, 0:128]), r=[("ps", bw)], w=[wT_tok])
                Sc.dve(lambda e: e.scalar_tensor_tensor(out=qdT, in0=qT, scalar=scale, in1=EG, op0=ALU.mult, op1=ALU.mult), r=[qT_tok, EG_tok], w=[qdT_tok])
                Sc.dve(lambda e: e.tensor_scalar(out=kdA[0:64, :], in0=kt[0:64, :], scalar1=dl[0:64, t:t + 1], scalar2=None, op0=ALU.mult),
                       r=[kt_tok, dl_tok], w=[kdA_tok])
                Sc.dve(lambda e: e.tensor_scalar(out=kdB[64:128, :], in0=kt[64:128, :], scalar1=dl[64:128, t:t + 1], scalar2=None, op0=ALU.mult),
                       r=[kt_tok, dl_tok], w=[kdB_tok])
                yield
                osb, osb_tok = st["osb%d" % par]
                for c in range(2):
                    rs_ = slice(64 * c, 64 * c + 64)
                    kd, kd_tok = (kdA, kdA_tok) if c == 0 else (kdB, kdB_tok)
                    bws = nb()
                    Sc.pe(lambda e, bws=bws: e.matmul(out=bank_f32(bws)[:, 0:128], lhsT=wT, rhs=Sst, start=True, stop=True), r=[wT_tok, Sst_tok], w=[("ps", bws)])
                    Sc.dve(lambda e, bws=bws, rs_=rs_: e.tensor_tensor(out=vnew[rs_, :], in0=u[rs_, :], in1=bank_f32(bws)[rs_, 0:128], op=ALU.subtract),
                           r=[u_tok, ("ps", bws)], w=[vnew_tok])
                    bo = nb()
                    Sc.pe(lambda e, bo=bo: e.matmul(out=bank_f32(bo)[:, 0:128], lhsT=qdT, rhs=Sst, start=True, stop=False), r=[qdT_tok, Sst_tok], w=[("ps", bo)])
                    Sc.pe(lambda e, bo=bo: e.matmul(out=bank_f32(bo)[:, 0:128], lhsT=AcT, rhs=vnew, start=False, stop=True), r=[AcT_tok, vnew_tok], w=[("ps", bo)])
                    Sc.act(lambda e, bo=bo, rs_=rs_: e.activation(out=osb[rs_, :], in_=bank_f32(bo)[rs_, 0:128], func=AF.Copy), r=[("ps", bo)], w=[osb_tok])
                    bs = nb()
                    Sc.pe(lambda e, bs=bs, kd=kd: e.matmul(out=bank_f32(bs)[:, 0:128], lhsT=kd, rhs=vnew, start=True, stop=True), r=[kd_tok, vnew_tok], w=[("ps", bs)])
                    sdcol = EG[:, 64 * c + 63:64 * c + 64]
                    Sc.dve(lambda e, bs=bs, sdcol=sdcol: e.scalar_tensor_tensor(out=Sst, in0=Sst, scalar=sdcol, in1=bank_f32(bs)[:, 0:128], op0=ALU.mult, op1=ALU.add),
                           r=[Sst_tok, EG_tok, ("ps", bs)], w=[Sst_tok])
                    yield
                Sc.dma("sp", o_gdn[r0:r0 + 128, hs], osb, r=[osb_tok])

        for h0 in range(0, H, G):
            gens = [head_gen(streams[gi], h0 + gi) for gi in range(min(G, H - h0))]
            alive = list(gens)
            while alive:
                nxt = []
                for g in alive:
                    try:
                        next(g); nxt.append(g)
                    except StopIteration:
                        pass
                alive = nxt

    def resid_linear(K, w_ap, prov, hT3, hT_tok, TG, resid_src, dst, ntg):
        rbufs = [AR.f32(512, "rbuf%d" % i) for i in range(2)]
        cnt = [0]

        def epi(tg, bi_, segs, accs):
            c0 = segs[0][0]; ncols = segs[0][1]
            for tt, b in enumerate(accs):
                i2 = cnt[0] % 2; cnt[0] += 1
                rb, rb_tok = rbufs[i2]
                r0 = tg * TG + tt * 128
                Sc.dma("sp", rb[:, 0:ncols], resid_src[r0:r0 + 128, c0:c0 + ncols], w=[rb_tok])
                Sc.dve(lambda e, rb=rb, b=b: e.tensor_tensor(out=rb[:, 0:ncols], in0=rb[:, 0:ncols], in1=bank_f32(b)[:, 0:ncols], op=ALU.add),
                       r=[rb_tok, ("ps", b)], w=[rb_tok])
                Sc.dma("sp", dst[r0:r0 + 128, c0:c0 + ncols], rb[:, 0:ncols], r=[rb_tok])
        N = D
        blocks = [[(c, min(512, N - c))] for c in range(0, N, 512)]
        for tg in range(ntg):
            prov(tg)
            linear_one_tg(K, w_ap, blocks, hT3, hT_tok, TG, "tm", epi, tg)

    def phase_C():
        AR.reset()
        TG = 512 if S_ >= 512 else S_
        prol = rms_prologue_factory(a_out_norm_t, D, group=128)
        prov, hT3, hT_tok = make_prov_tm(o_gdn, D, TG, prol, extra_src=z_tm)
        resid_linear(D, a_w_out, prov, hT3, hT_tok, TG, x_in, x1, S_ // TG)

    def phase_FFN(layer, src, dst):
        AR.reset()
        TG = 512 if S_ >= 512 else S_
        ntg = S_ // TG
        prol = rms_prologue_factory(ffn_norm[layer:layer + 1, :], D)
        prov, hT3, hT_tok = make_prov_tm(src, D, TG, prol)
        sgb = [AR.f32(TG, "sg%d" % i) for i in range(2)]
        acb = [AR.bf16(TG, "ac%d" % i) for i in range(2)]
        cnt = [0]
        wgu = ffn_w_gu[layer]

        def epi(tg, bi_, segs, accs):
            f0 = segs[0][0]; nf = segs[0][1]
            nch = nf // 128
            for ci in range(nch):
                i2 = cnt[0] % 2; cnt[0] += 1
                (sg, sg_tok), (ac, ac_tok) = sgb[i2], acb[i2]
                bg_, bu_ = accs[ci], accs[nch + ci]
                Sc.act(lambda e, sg=sg, bg_=bg_: e.activation(out=sg, in_=bank_f32(bg_)[:, 0:TG], func=AF.Silu), r=[("ps", bg_)], w=[sg_tok])
                Sc.dve(lambda e, sg=sg, ac=ac, bu_=bu_: e.tensor_tensor(out=ac, in0=sg, in1=bank_f32(bu_)[:, 0:TG], op=ALU.mult),
                       r=[sg_tok, ("ps", bu_)], w=[ac_tok])
                fr = f0 + ci * 128
                Sc.dma("sp", actT[fr:fr + 128, tg * TG:(tg + 1) * TG], ac, r=[ac_tok])
        blocks = [[(f, 256), (F + f, 256)] for f in range(0, F, 256)]
        for tg in range(ntg):
            prov(tg)
            linear_one_tg(D, wgu, blocks, hT3, hT_tok, TG, "fm", epi, tg)
        Sc.barrier()
        AR.reset()
        TG2 = 256 if S_ >= 256 else S_
        prov2, hT3b, hT_tokb = make_prov_fm(actT, F, TG2)
        resid_linear(F, ffn_w_down[layer], prov2, hT3b, hT_tokb, TG2, src, dst, S_ // TG2)

    def phase_E():
        TG = 512 if S_ >= 512 else S_
        ntg = S_ // TG

        def run(norm_row, w_ap, ncols_total, gain_row, dstT, do_v):
            AR.reset()
            prol = rms_prologue_factory(norm_row, D)
            prov, hT3, hT_tok = make_prov_tm(x2, D, TG, prol)
            gn, gn_tok = AR.f32(128, "gn")
            Sc.dma("sp", gn, bcast_row(gain_row, 128), w=[gn_tok])
            sqb, sqb_tok = AR.f32(512, "sqb")
            stt, stt_tok = AR.f32(16, "stt")
            knb = [AR.bf16(512, "kn%d" % i) for i in range(2)]
            kTb = [AR.bf16(512, "kTb%d" % i) for i in range(2)]
            vbb = [AR.bf16(512, "vbb%d" % i) for i in range(2)]
            cnt = [0]

            def epi(tg, bi_, segs, accs):
                c0 = segs[0][0]
                for tt, b in enumerate(accs):
                    i2 = cnt[0] % 2; cnt[0] += 1
                    r0 = tg * TG + tt * 128
                    pb = bank_f32(b)
                    if c0 < D:
                        (kn, kn_tok), (kTs, kTs_tok) = knb[i2], kTb[i2]
                        Sc.act(lambda e, pb=pb: e.activation(out=sqb, in_=pb[:, 0:512], func=AF.Square), r=[("ps", b)], w=[sqb_tok])
                        Sc.dve(lambda e: e.tensor_reduce(out=stt[:, 0:4], in_=sqb.rearrange("p (g c) -> p g c", c=128), axis=AX.X, op=ALU.add), r=[sqb_tok], w=[stt_tok])
                        Sc.dve(lambda e: e.tensor_scalar(out=stt[:, 4:8], in0=stt[:, 0:4], scalar1=1.0 / 128, scalar2=RMS_EPS, op0=ALU.mult, op1=ALU.add), r=[stt_tok], w=[stt_tok])
                        Sc.act(lambda e: e.activation(out=stt[:, 4:8], in_=stt[:, 4:8], func=AF.Sqrt), r=[stt_tok], w=[stt_tok])
                        Sc.dve(lambda e: e.reciprocal(out=stt[:, 4:8], in_=stt[:, 4:8]), r=[stt_tok], w=[stt_tok])
                        for g in range(4):
                            Sc.dve(lambda e, kn=kn, pb=pb, g=g: e.scalar_tensor_tensor(out=kn[:, g * 128:(g + 1) * 128], in0=pb[:, g * 128:(g + 1) * 128],
                                   scalar=stt[:, 4 + g:5 + g], in1=gn, op0=ALU.mult, op1=ALU.mult), r=[("ps", b), stt_tok, gn_tok], w=[kn_tok])
                        xb_ = next_aux()
                        for g in range(4):
                            Sc.pe(lambda e, kn=kn, g=g, xb_=xb_: e.transpose(out=bank_bf16(xb_)[:, g * 128:(g + 1) * 128], in_=kn[:, g * 128:(g + 1) * 128], identity=IDENTB),
                                  r=[kn_tok, "cstb"], w=[("ps", xb_)])
                        Sc.act(lambda e, kTs=kTs, xb_=xb_: e.activation(out=kTs, in_=bank_bf16(xb_)[:, 0:512], func=AF.Copy), r=[("ps", xb_)], w=[kTs_tok])
                        g0 = c0 // 128
                        Sc.dma("sp", dstT[g0:g0 + 4, :, r0:r0 + 128].rearrange("g p t -> p g t"), kTs.rearrange("p (g t) -> p g t", t=128), r=[kTs_tok])
                    else:
                        vbt, vbt_tok = vbb[i2]
                        Sc.act(lambda e, vbt=vbt, pb=pb: e.activation(out=vbt, in_=pb[:, 0:512], func=AF.Copy), r=[("ps", b)], w=[vbt_tok])
                        Sc.dma("sp", vb_d[r0:r0 + 128, c0 - D:c0 - D + 512], vbt, r=[vbt_tok])
            blocks = [[(c, 512)] for c in range(0, ncols_total, 512)]
            for tg in range(ntg):
                prov(tg)
                linear_one_tg(D, w_ap, blocks, hT3, hT_tok, TG, "tm", epi, tg)
            Sc.barrier()
        run(kv_norm, w_kv, 2 * D, k_norm, kT_d, True)
        run(b_norm, b_w_q, D, b_q_norm, qT_d, False)

    def phase_F(lam_init):
        AR.reset()
        scale = 128.0 ** -0.5
        QG = 512 if S_ >= 512 else S_
        nqg = S_ // QG
        ntq = QG // 128
        lp, lp_tok = AR.f32(512, "lp")
        Sc.dma("sp", lp, bcast_row(b_lambda, 512), w=[lp_tok])
        lt, lt_tok = AR.f32(272, "lt")
        Sc.dve(lambda e: e.tensor_tensor(out=lt[:, 0:128], in0=lp[:, 0:128], in1=lp[:, 128:256], op=ALU.mult), r=[lp_tok], w=[lt_tok])
        Sc.dve(lambda e: e.tensor_tensor(out=lt[:, 128:256], in0=lp[:, 256:384], in1=lp[:, 384:512], op=ALU.mult), r=[lp_tok], w=[lt_tok])
        Sc.dve(lambda e: e.tensor_reduce(out=lt[:, 256:258], in_=lt[:, 0:256].rearrange("p (g c) -> p g c", c=128), axis=AX.X, op=ALU.add), r=[lt_tok], w=[lt_tok])
        Sc.act(lambda e: e.activation(out=lt[:, 258:260], in_=lt[:, 256:258], func=AF.Exp), r=[lt_tok], w=[lt_tok])
        Sc.dve(lambda e: e.tensor_tensor(out=lt[:, 260:261], in0=lt[:, 259:260], in1=lt[:, 258:259], op=ALU.subtract), r=[lt_tok], w=[lt_tok])
        Sc.dve(lambda e: e.tensor_scalar(out=lt[:, 261:262], in0=lt[:, 260:261], scalar1=-lam_init, scalar2=None, op0=ALU.add), r=[lt_tok], w=[lt_tok])
        neglam = lt[:, 261:262]
        kTs = [AR.bf16(S_, "kTs%d" % m) for m in range(2)]
        qTs = [AR.bf16(S_, "qTs%d" % m) for m in range(2)]
        vx, vx_tok = AR.bf16(NT * 258, "vx")
        vx3 = vx.rearrange("p (t c) -> p t c", c=258)
        Sc.dve(lambda e: e.memset(vx, 1.0), w=[vx_tok])
        Eb = [AR.bf16(QG, "E%d" % i) for i in range(2)]
        Em = [AR.bf16(128, "Em%d" % i) for i in range(2)]
        AMB, AMB_tok = AR.bf16(128, "AMB")
        Sc.dve(lambda e: e.tensor_copy(out=AMB, in_=AMASK), r=["cst"], w=[AMB_tok])
        Osb = [[AR.f32(257, "O%d_%d" % (m, tt)) for tt in range(ntq)] for m in range(2)]
        outb = [AR.f32(256, "ob%d" % i) for i in range(2)]
        rcp, rcp_tok = AR.f32(8, "rcp")
        ecnt = [0]; ocnt = [0]
        for h in range(DH):
            for m in range(2):
                Sc.dma("sp", kTs[m][0], kT_d[2 * h + m], w=[kTs[m][1]])
                Sc.dma("sp", qTs[m][0], qT_d[2 * h + m], w=[qTs[m][1]])
            Sc.dma("sp", vx3[:, :, 0:256], vb_d[:, h * 256:(h + 1) * 256].rearrange("(t p) c -> p t c", p=128), w=[vx_tok])
            for qg in range(nqg):
                for m in range(2):
                    kTm, kT_tok = kTs[m]; qTm, qT_tok = qTs[m]
                    obanks = [0, 1, 2, 3][:ntq]
                    nkt = (qg + 1) * ntq
                    for kt_ in range(nkt):
                        sb_ = 4 + (ecnt[0] % 2)
                        E, E_tok = Eb[ecnt[0] % 2]; ecnt[0] += 1
                        Sc.pe(lambda e, sb_=sb_, kt_=kt_, kTm=kTm, qTm=qTm, qg=qg: e.matmul(out=bank_f32(sb_)[:, 0:QG], lhsT=kTm[:, kt_ * 128:(kt_ + 1) * 128],
                              rhs=qTm[:, qg * QG:(qg + 1) * QG], start=True, stop=True), r=[kT_tok, qT_tok], w=[("ps", sb_)])
                        Sc.act(lambda e, sb_=sb_, E=E: e.activation(out=E, in_=bank_f32(sb_)[:, 0:QG], func=AF.Exp, scale=scale), r=[("ps", sb_)], w=[E_tok])
                        for tt in range(ntq):
                            qt_ = qg * ntq + tt
                            if qt_ < kt_:
                                continue
                            lhs = E[:, tt * 128:(tt + 1) * 128]; lhs_tok = E_tok
                            if qt_ == kt_:
                                Emm, Emm_tok = Em[tt % 2]
                                Sc.dve(lambda e, Emm=Emm, lhs=lhs: e.tensor_tensor(out=Emm, in0=lhs, in1=AMB, op=ALU.mult), r=[E_tok, AMB_tok], w=[Emm_tok])
                                lhs, lhs_tok = Emm, Emm_tok
                            Sc.pe(lambda e, tt=tt, lhs=lhs, kt_=kt_, qt_=qt_: e.matmul(out=bank_f32(obanks[tt])[:, 0:257], lhsT=lhs, rhs=vx3[:, kt_, 0:257],
                                  start=(kt_ == 0), stop=(kt_ == qt_)), r=[lhs_tok, vx_tok], w=[("ps", obanks[tt])])
                    for tt in range(ntq):
                        O, O_tok = Osb[m][tt]
                        if tt % 2 == 0:
                            Sc.act(lambda e, O=O, tt=tt: e.activation(out=O, in_=bank_f32(obanks[tt])[:, 0:257], func=AF.Copy), r=[("ps", obanks[tt])], w=[O_tok])
                        else:
                            Sc.dve(lambda e, O=O, tt=tt: e.tensor_copy(out=O, in_=bank_f32(obanks[tt])[:, 0:257]), r=[("ps", obanks[tt])], w=[O_tok])
                for tt in range(ntq):
                    (O0, O0_tok), (O1, O1_tok) = Osb[0][tt], Osb[1][tt]
                    ob, ob_tok = outb[ocnt[0] % 2]; ocnt[0] += 1
                    Sc.dve(lambda e, O0=O0: e.reciprocal(out=rcp[:, 0:1], in_=O0[:, 256:257]), r=[O0_tok], w=[rcp_tok])
                    Sc.dve(lambda e, O1=O1: e.reciprocal(out=rcp[:, 1:2], in_=O1[:, 256:257]), r=[O1_tok], w=[rcp_tok])
                    Sc.dve(lambda e: e.tensor_tensor(out=rcp[:, 2:3], in0=rcp[:, 1:2], in1=neglam, op=ALU.mult), r=[rcp_tok, lt_tok], w=[rcp_tok])
                    Sc.dve(lambda e, ob=ob, O0=O0: e.tensor_scalar(out=ob, in0=O0[:, 0:256], scalar1=rcp[:, 0:1], scalar2=None, op0=ALU.mult), r=[O0_tok, rcp_tok], w=[ob_tok])
                    Sc.dve(lambda e, ob=ob, O1=O1: e.scalar_tensor_tensor(out=ob, in0=O1[:, 0:256], scalar=rcp[:, 2:3], in1=ob, op0=ALU.mult, op1=ALU.add),
                           r=[O1_tok, rcp_tok, ob_tok], w=[ob_tok])
                    r0 = qg * QG + tt * 128
                    Sc.dma("sp", o_att[r0:r0 + 128, h * 256:(h + 1) * 256], ob, r=[ob_tok])

    def phase_G(lam_init):
        AR.reset()
        TG = 512 if S_ >= 512 else S_
        prol = rms_prologue_factory(b_sub_norm_t, D, group=256, post_scale=(1.0 - lam_init))
        prov, hT3, hT_tok = make_prov_tm(o_att, D, TG, prol)
        resid_linear(D, b_w_out, prov, hT3, hT_tok, TG, x2, x3, S_ // TG)

    lam_init = 0.8 - 0.6 * math.exp(-0.3 * 1)
    phase_A(); Sc.barrier()
    phase_B(); Sc.barrier()
    phase_C(); Sc.barrier()
    phase_FFN(0, x1, x2); Sc.barrier()
    phase_E()
    phase_F(lam_init); Sc.barrier()
    phase_G(lam_init); Sc.barrier()
    phase_FFN(1, x3, out_d)
    Sc.finish()
    Sc.emit(nc, stack)
    stack.close()
    return nc


def prep_inputs(inputs, cfg, b):
    f = lambda a: np.ascontiguousarray(np.asarray(a, dtype=np.float32))
    H = cfg.H
    m = {
        "x": f(inputs["x"][b]),
        "a_norm": f(inputs["a_norm"]).reshape(1, -1),
        "a_w_in": f(inputs["a_w_in"][0]),
        "a_conv_t": f(np.asarray(inputs["a_conv"][0]).T),
        "a_A_log": f(inputs["a_A_log"]).reshape(1, -1),
        "a_dt_bias": f(inputs["a_dt_bias"]).reshape(1, -1),
        "a_out_norm_t": f(np.tile(np.asarray(inputs["a_out_norm"][0]), H)).reshape(1, -1),
        "a_w_out": f(inputs["a_w_out"][0]),
        "kv_norm": f(inputs["kv_norm"]).reshape(1, -1),
        "w_kv": f(inputs["w_kv"]),
        "k_norm": f(inputs["k_norm"]).reshape(1, -1),
        "b_norm": f(inputs["b_norm"]).reshape(1, -1),
        "b_w_q": f(inputs["b_w_q"][0]),
        "b_q_norm": f(inputs["b_q_norm"]).reshape(1, -1),
        "b_lambda": f(inputs["b_lambda"][0]).reshape(1, -1),
        "b_sub_norm_t": f(np.tile(np.asarray(inputs["b_sub_norm"][0]), cfg.DH)).reshape(1, -1),
        "b_w_out": f(inputs["b_w_out"][0]),
        "ffn_norm": f(inputs["ffn_norm"]),
        "ffn_w_gate_up": f(inputs["ffn_w_gate_up"]),
        "ffn_w_down": f(inputs["ffn_w_down"]),
        "consts": make_consts(),
    }
    return m


def run(inputs, cfg, debug=False, trace=False):
    nc = build(cfg, debug=debug)
    B = np.asarray(inputs["x"]).shape[0]
    in_maps = [prep_inputs(inputs, cfg, c) for c in range(B)]
    res = run_bass_kernel_spmd(nc, in_maps, core_ids=list(range(B)), **({"trace": True} if trace else {}))
    return res


def kernel(**inputs):
    cfg = Cfg(4096, 4096, 11008)
    res = run(inputs, cfg)
    B = np.asarray(inputs["x"]).shape[0]
    return np.stack([np.asarray(res.results[b]["out"], dtype=np.float32) for b in range(B)], axis=0)
```

```python
import math
import contextlib
import numpy as np
import concourse.bass as bass
import concourse.mybir as mybir
from concourse.bass_utils import run_bass_kernel_spmd

F32 = mybir.dt.float32
BF16 = mybir.dt.bfloat16
AF = mybir.ActivationFunctionType
ALU = mybir.AluOpType
AX = mybir.AxisListType

RMS_EPS = 1e-6
ARENA_WORDS = 41500
INV_F32R = False


class Cfg:
    def __init__(s, D, S, F):
        s.D = D; s.S = S; s.F = F
        s.H = D // 128; s.GIN = 4 * D + 2 * s.H; s.DH = D // 256
        s.NT = S // 128; s.KC = D // 128


class _Op:
    __slots__ = ("eng", "fn", "deps", "dma", "signals", "sem", "val", "prev")

    def __init__(s, eng, fn, deps, dma):
        s.eng = eng; s.fn = fn; s.deps = deps; s.dma = dma
        s.signals = False; s.sem = None; s.val = 0; s.prev = 0


class _Rec:
    def __init__(s):
        s.calls = []

    def __getattr__(s, name):
        def f(*a, **k):
            s.calls.append((name, a, k))
            return s
        return f


class Sched:
    CE = ("pe", "act", "dve", "pool")
    EPOCH = 12000
    NSLOT = 12
    USES = 1500

    def __init__(s):
        s.ops = []
        s.last_w = {}
        s.readers = {}
        s.barrier_deps = {}
        s.last_on = {}
        s.dmas_since = []

    def op(s, eng, fn, r=(), w=(), dma=False, bg=False):
        if fn is not None:
            rec = _Rec()
            fn(rec)
            assert len(rec.calls) == 1
            fn = rec.calls[0]
        i = len(s.ops)
        deps = {}
        for b in r:
            j = s.last_w.get(b)
            if j is not None:
                deps[j] = True
        for b in w:
            j = s.last_w.get(b)
            if j is not None and j not in deps:
                deps[j] = False
            for x in s.readers.get(b, ()):
                if x not in deps:
                    deps[x] = False
        bd = s.barrier_deps.pop(eng, None)
        if bd:
            for j in bd:
                deps[j] = True
        s.ops.append(_Op(eng, fn, deps, dma))
        for b in r:
            s.readers.setdefault(b, []).append(i)
        for b in w:
            s.last_w[b] = i
            s.readers[b] = []
        if dma:
            if not bg:
                s.dmas_since.append(i)
        else:
            s.last_on[eng] = i
        return i

    def pe(s, fn, r=(), w=()): return s.op("pe", fn, r, w)
    def act(s, fn, r=(), w=()): return s.op("act", fn, r, w)
    def dve(s, fn, r=(), w=()): return s.op("dve", fn, r, w)
    def pool(s, fn, r=(), w=()): return s.op("pool", fn, r, w)

    def dma(s, q, out, in_, r=(), w=(), bg=False, **kw):
        return s.op(q, lambda e: e.dma_start(out=out, in_=in_, **kw), r, w, dma=True, bg=bg)

    def barrier(s):
        deps = list(s.last_on.values()) + list(s.dmas_since)
        s.dmas_since = []
        s.last_w = {k: v for k, v in s.last_w.items() if isinstance(k, tuple) and k[0] == "W2"}
        s.readers = {}
        for e in ("pe", "act", "dve", "pool", "sp"):
            s.barrier_deps[e] = list(deps)

    def finish(s):
        s.barrier()
        s.op("sp", None)

    def emit(s, nc, stack):
        ops = s.ops
        for o in ops:
            keep = {}
            for j, raw in o.deps.items():
                p = ops[j]
                if p.dma or o.dma:
                    keep[j] = raw
                elif p.eng == o.eng:
                    if o.eng != "pe" and raw:
                        keep[j] = raw
                else:
                    keep[j] = raw
            o.deps = keep
            for j in keep:
                ops[j].signals = True
        cnt = {e: 0 for e in s.CE}
        dcnt = {"sp": 0, "pool": 0}
        for o in ops:
            if o.dma:
                n = dcnt[o.eng]; dcnt[o.eng] += 1
                slot = n % s.NSLOT; use = n // s.NSLOT
                o.sem = ("d", o.eng, (use // s.USES) * s.NSLOT + slot)
                o.prev = 16 * (use % s.USES)
                o.val = o.prev + 16
            elif o.signals:
                c = cnt[o.eng]; cnt[o.eng] += 1
                o.sem = ("c", o.eng, c // s.EPOCH)
                o.val = c % s.EPOCH + 1
        names = sorted({o.sem for o in ops if o.sem is not None}, key=str)
        sems = {}
        for k, nm in enumerate(names):
            sems[nm] = stack.enter_context(nc.semaphore("s%d" % k))
        assert len(sems) < 140, len(sems)
        per_eng = {e: [] for e in ("pe", "act", "dve", "pool", "sp")}
        for i, o in enumerate(ops):
            per_eng[o.eng].append(o)
        engmap = {"pe": "tensor", "act": "scalar", "dve": "vector", "pool": "gpsimd", "sp": "sync"}
        with nc.Block() as b0:
            def clr(g):
                for nm in names:
                    g.sem_clear(sems[nm])
            b0.gpsimd(clr)
        with nc.Block() as blk:
            for ename, lst in per_eng.items():
                if not lst:
                    continue

                def body(e, lst=lst):
                    known = {}
                    for o in lst:
                        for j in o.deps:
                            p = ops[j]
                            if known.get(p.sem, 0) < p.val:
                                e.wait_ge(sems[p.sem], p.val)
                                known[p.sem] = p.val
                        if o.dma and o.prev > 0 and known.get(o.sem, 0) < o.prev:
                            e.wait_ge(sems[o.sem], o.prev)
                            known[o.sem] = o.prev
                        if o.fn is None:
                            continue
                        nm_, a_, k_ = o.fn
                        inst = getattr(e, nm_)(*a_, **k_)
                        if o.dma:
                            inst.then_inc(sems[o.sem], 16)
                        elif o.signals:
                            inst.then_inc(sems[o.sem], 1)
                getattr(blk, engmap[ename])(body)


def make_consts():
    c = np.zeros((128, 6, 128), np.float32)
    i = np.arange(128)
    J, I = np.meshgrid(i, i, indexing="ij")
    same = (J // 64) == (I // 64)
    c[:, 0, :] = np.eye(128, dtype=np.float32)
    c[:, 1, :] = np.where(same & (I >= J), 0.0, -30000.0)
    c[:, 2, :] = np.where(same & (I > J), 1.0, 0.0)
    c[:, 3, :] = np.where(same & (J <= I), 1.0, 0.0)
    c[:, 4, :] = np.where((J >= 64) & (I < 64), 0.0, 1.0)
    c[:, 5, :] = 1.0
    return c.reshape(128, 6 * 128)


def build(cfg, debug=False):
    D, S_, F, H, DH, NT, KC, GIN = cfg.D, cfg.S, cfg.F, cfg.H, cfg.DH, cfg.NT, cfg.KC, cfg.GIN
    nc = bass.Bass("TRN2", target_bir_lowering=False)
    stack = contextlib.ExitStack()

    def din(name, shape, dt=F32):
        return nc.dram_tensor(name, list(shape), dt, kind="ExternalInput").ap()

    def dscr(name, shape, dt=F32):
        return nc.dram_tensor(name, list(shape), dt, kind="ExternalOutput" if debug else "Internal").ap()

    x_in = din("x", [S_, D])
    a_norm = din("a_norm", [1, D]); a_w_in = din("a_w_in", [D, GIN]); a_conv_t = din("a_conv_t", [3 * D, 4])
    a_A_log = din("a_A_log", [1, H]); a_dt_bias = din("a_dt_bias", [1, H])
    a_out_norm_t = din("a_out_norm_t", [1, D]); a_w_out = din("a_w_out", [D, D])
    kv_norm = din("kv_norm", [1, D]); w_kv = din("w_kv", [D, 2 * D]); k_norm = din("k_norm", [1, 128])
    b_norm = din("b_norm", [1, D]); b_w_q = din("b_w_q", [D, D]); b_q_norm = din("b_q_norm", [1, 128])
    b_lambda = din("b_lambda", [1, 4 * 128]); b_sub_norm_t = din("b_sub_norm_t", [1, D]); b_w_out = din("b_w_out", [D, D])
    ffn_norm = din("ffn_norm", [2, D]); ffn_w_gu = din("ffn_w_gate_up", [2, D, 2 * F]); ffn_w_down = din("ffn_w_down", [2, F, D])
    consts_d = din("consts", [128, 6 * 128])
    out_d = nc.dram_tensor("out", [S_, D], F32, kind="ExternalOutput").ap()

    q_tm = dscr("q_tm", [S_, D]); k_tm = dscr("k_tm", [S_, D]); v_tm = dscr("v_tm", [S_, D]); z_tm = dscr("z_tm", [S_, D])
    gb_tm = dscr("gb_tm", [S_, 2 * H]); gbT = dscr("gbT", [2 * H, S_])
    o_gdn = dscr("o_gdn", [S_, D]); x1 = dscr("x1", [S_, D]); x2 = dscr("x2", [S_, D]); x3 = dscr("x3", [S_, D])
    actT = dscr("actT", [F, S_], BF16)
    kT_d = dscr("kT_d", [2 * DH, 128, S_], BF16); qT_d = dscr("qT_d", [2 * DH, 128, S_], BF16)
    vb_d = dscr("vb_d", [S_, D], BF16); o_att = dscr("o_att", [S_, D])

    arena_t = stack.enter_context(nc.sbuf_tensor("arena", [128, ARENA_WORDS], F32))
    cst_t = stack.enter_context(nc.sbuf_tensor("cst", [128, 6 * 128], F32))
    cstb_t = stack.enter_context(nc.sbuf_tensor("cstb", [128, 128], BF16))
    banks = [stack.enter_context(nc.psum_tensor("bank%d" % i, [128, 512], F32)) for i in range(8)]

    Sc = Sched()
    cst = cst_t[:]
    IDENT = cst[:, 0:128]; MASKNEG = cst[:, 128:256]; STRICT = cst[:, 256:384]
    LCUM = cst[:, 384:512]; AMASK = cst[:, 512:640]; ONES = cst[:, 640:768]
    IDENTB = cstb_t[:]
    Sc.dma("sp", cst, consts_d, w=["cst"])
    Sc.dve(lambda e: e.tensor_copy(out=IDENTB, in_=IDENT), r=["cst"], w=["cstb"])
    Sc.barrier()

    class Arena:
        def __init__(s): s.off = 0; s.gen = 0
        def reset(s): s.off = 0; s.gen += 1
        def f32(s, n, name):
            a = arena_t[:, s.off:s.off + n]; s.off += n
            assert s.off <= ARENA_WORDS, (name, s.off)
            return a, (name, s.gen)
        def bf16(s, n, name):
            w = (n + 1) // 2
            a = arena_t[:, s.off:s.off + w].bitcast(BF16)[:, 0:n]; s.off += w
            assert s.off <= ARENA_WORDS, (name, s.off)
            return a, (name, s.gen)
    AR = Arena()

    def bank_f32(i): return banks[i][:]
    def bank_bf16(i): return banks[i][:].bitcast(BF16)

    rot = {"acc": 0, "aux": 0}
    def next_acc():
        i = rot["acc"] % 6; rot["acc"] += 1; return i
    def next_aux():
        i = 6 + rot["aux"] % 2; rot["aux"] += 1; return i

    def bcast_row(ap_row, n):
        return ap_row.partition_broadcast(128).rearrange("p o n -> p (o n)")

    def make_prov_tm(src, K, TG, prologue, extra_src=None):
        kc_n = K // 128
        hT, hT_tok = AR.bf16(kc_n * TG, "hT")
        hT3 = hT.rearrange("p (k t) -> p k t", t=TG)
        xin, xin_tok = AR.f32(K, "xin")
        xb, xb_tok = AR.bf16(K, "xb")
        zin = None
        if extra_src is not None:
            zin, zin_tok = AR.f32(K, "zin")
        ntt = TG // 128

        def prov(tg):
            for tt in range(ntt):
                r0 = tg * TG + tt * 128
                Sc.dma("sp", xin, src[r0:r0 + 128, :], w=[xin_tok])
                rd = [xin_tok]
                if zin is not None:
                    Sc.dma("sp", zin, extra_src[r0:r0 + 128, :], w=[zin_tok])
                    rd.append(zin_tok)
                prologue(xin, zin, xb, rd, xb_tok)
                for k0 in range(0, kc_n, 4):
                    nk = min(4, kc_n - k0)
                    bi = next_aux()
                    pb = bank_bf16(bi)
                    for kk in range(nk):
                        Sc.pe(lambda e, kk=kk, k0=k0, pb=pb: e.transpose(
                            out=pb[:, kk * 128:(kk + 1) * 128], in_=xb[:, (k0 + kk) * 128:(k0 + kk + 1) * 128], identity=IDENTB),
                            r=[xb_tok, "cstb"], w=[("ps", bi)])
                    src_ap = pb[:, 0:nk * 128].rearrange("p (k t) -> p k t", t=128)
                    dst_ap = hT3[:, k0:k0 + nk, tt * 128:(tt + 1) * 128]
                    if (k0 // 4) % 2 == 0:
                        Sc.dve(lambda e, d=dst_ap, s_=src_ap: e.tensor_copy(out=d, in_=s_), r=[("ps", bi)], w=[hT_tok])
                    else:
                        Sc.act(lambda e, d=dst_ap, s_=src_ap: e.activation(out=d, in_=s_, func=AF.Copy), r=[("ps", bi)], w=[hT_tok])
        return prov, hT3, hT_tok

    def make_prov_fm(src, K, TG):
        kc_n = K // 128
        hT, hT_tok = AR.bf16(kc_n * TG, "hT")
        hT3 = hT.rearrange("p (k t) -> p k t", t=TG)

        def prov(tg):
            step = 16
            for k0 in range(0, kc_n, step):
                nk = min(step, kc_n - k0)
                Sc.dma("sp", hT3[:, k0:k0 + nk, :],
                       src[k0 * 128:(k0 + nk) * 128, tg * TG:(tg + 1) * TG].rearrange("(k p) t -> p k t", p=128),
                       w=[hT_tok])
        return prov, hT3, hT_tok

    def rms_prologue_factory(gain_row_ap, K, group=None, post_scale=1.0):
        gain_b, gain_tok = AR.f32(K, "gain")
        sq, sq_tok = AR.f32(K, "sq")
        ng = 1 if group is None else K // group
        st, st_tok = AR.f32(4 * ng + 4, "stat")
        Sc.dma("sp", gain_b, bcast_row(gain_row_ap, K), w=[gain_tok])
        gsz = K if group is None else group

        def prologue(xin, zin, xb, rd, xb_tok):
            Sc.act(lambda e: e.activation(out=sq, in_=xin, func=AF.Square), r=rd[:1], w=[sq_tok])
            ss = st[:, 0:ng]; rs = st[:, ng:2 * ng]
            Sc.dve(lambda e: e.tensor_reduce(out=ss, in_=sq.rearrange("p (g c) -> p g c", c=gsz), axis=AX.X, op=ALU.add),
                   r=[sq_tok], w=[st_tok])
            ps2 = post_scale * post_scale
            Sc.dve(lambda e: e.tensor_scalar(out=rs, in0=ss, scalar1=1.0 / (gsz * ps2), scalar2=RMS_EPS / ps2, op0=ALU.mult, op1=ALU.add),
                   r=[st_tok], w=[st_tok])
            Sc.act(lambda e: e.activation(out=rs, in_=rs, func=AF.Sqrt), r=[st_tok], w=[st_tok])
            Sc.dve(lambda e: e.reciprocal(out=rs, in_=rs), r=[st_tok], w=[st_tok])
            if group is None:
                Sc.dve(lambda e: e.scalar_tensor_tensor(out=xb, in0=xin, scalar=rs[:, 0:1], in1=gain_b, op0=ALU.mult, op1=ALU.mult),
                       r=[rd[0], st_tok, gain_tok], w=[xb_tok])
            else:
                if zin is not None:
                    Sc.act(lambda e: e.activation(out=zin, in_=zin, func=AF.Silu), r=[rd[1]], w=[rd[1]])
                x3_ = xin.rearrange("p (g c) -> p g c", c=gsz)
                rsb = rs.unsqueeze(2).broadcast_to([128, ng, gsz])
                sq3 = sq.rearrange("p (g c) -> p g c", c=gsz)
                Sc.dve(lambda e: e.tensor_tensor(out=sq3, in0=x3_, in1=rsb, op=ALU.mult), r=[rd[0], st_tok, sq_tok], w=[sq_tok])
                if zin is not None:
                    Sc.dve(lambda e: e.tensor_tensor(out=sq, in0=sq, in1=gain_b, op=ALU.mult), r=[sq_tok, gain_tok], w=[sq_tok])
                    Sc.dve(lambda e: e.tensor_tensor(out=xb, in0=sq, in1=zin, op=ALU.mult), r=[sq_tok, rd[1]], w=[xb_tok])
                else:
                    Sc.dve(lambda e: e.tensor_tensor(out=xb, in0=sq, in1=gain_b, op=ALU.mult), r=[sq_tok, gain_tok], w=[xb_tok])
        return prologue

    def phase_A():
        AR.reset()
        TG = 512 if S_ >= 512 else S_
        ntg = S_ // TG
        prol = rms_prologue_factory(a_norm, D)
        prov, hT3, hT_tok = make_prov_tm(x_in, D, TG, prol)
        NCH = 3 * H
        cw_t, cw_tok = AR.f32(NCH * 4, "convw")
        cw3 = cw_t.rearrange("p (c k) -> p c k", k=4)
        Sc.dma("sp", cw3, a_conv_t.rearrange("(c p) k -> p c k", p=128), w=[cw_tok])
        halo, halo_tok = AR.f32(NCH * 3, "halo")
        halo3 = halo.rearrange("p (c k) -> p c k", k=3)
        Sc.dve(lambda e: e.memset(halo, 0.0), w=[halo_tok])
        pbufs = [AR.f32(3 + TG, "pbuf%d" % i) for i in range(2)]
        ybufs = [AR.f32(TG, "ybuf%d" % i) for i in range(2)]
        obufs = [AR.f32(TG, "obuf%d" % i) for i in range(2)]
        sqb, sqb_tok = AR.f32(TG, "sqb")
        stt, stt_tok = AR.f32(16, "stt")
        zbufs = [AR.f32(512, "zbuf%d" % i) for i in range(2)]
        negA, negA_tok = AR.f32(H, "negA")
        dtb, dtb_tok = AR.f32(H, "dtb")
        Sc.dma("sp", negA, bcast_row(a_A_log, H), w=[negA_tok])
        Sc.dma("sp", dtb, bcast_row(a_dt_bias, H), w=[dtb_tok])
        Sc.act(lambda e: e.activation(out=negA, in_=negA, func=AF.Exp), r=[negA_tok], w=[negA_tok])
        Sc.dve(lambda e: e.tensor_scalar(out=negA, in0=negA, scalar1=-1.0, scalar2=None, op0=ALU.mult), r=[negA_tok], w=[negA_tok])
        gbs = [AR.f32(2 * H, "gb%d" % i) for i in range(2)]
        gtmp = [AR.f32(2 * H, "gtmp%d" % i) for i in range(2)]
        gbTs = [AR.f32(128, "gbTs%d" % i) for i in range(2)]
        cnt = [0]
        ntt = TG // 128

        def epi(tg, bi_, segs, accs):
            c0 = segs[0][0]
            if c0 < 3 * D:
                for ai, b in enumerate(accs):
                    ch = c0 // 128 + ai
                    which = ch // H; head = ch % H
                    i2 = cnt[0] % 2; cnt[0] += 1
                    (pb, pb_tok), (yb, yb_tok), (ob, ob_tok) = pbufs[i2], ybufs[i2], obufs[i2]
                    Sc.dve(lambda e, pb=pb, ch=ch: e.tensor_copy(out=pb[:, 0:3], in_=halo3[:, ch, :]), r=[halo_tok], w=[pb_tok])
                    Sc.act(lambda e, pb=pb, b=b: e.activation(out=pb[:, 3:3 + TG], in_=bank_f32(b)[:, 0:TG], func=AF.Copy),
                           r=[("ps", b)], w=[pb_tok])
                    Sc.dve(lambda e, pb=pb, ch=ch: e.tensor_copy(out=halo3[:, ch, :], in_=pb[:, TG:TG + 3]), r=[pb_tok], w=[halo_tok])
                    Sc.dve(lambda e, pb=pb, yb=yb, ch=ch: e.tensor_scalar(out=yb, in0=pb[:, 0:TG], scalar1=cw3[:, ch, 0:1], scalar2=None, op0=ALU.mult),
                           r=[pb_tok, cw_tok], w=[yb_tok])
                    for k in range(1, 4):
                        Sc.dve(lambda e, pb=pb, yb=yb, ch=ch, k=k: e.scalar_tensor_tensor(
                            out=yb, in0=pb[:, k:k + TG], scalar=cw3[:, ch, k:k + 1], in1=yb, op0=ALU.mult, op1=ALU.add),
                            r=[pb_tok, cw_tok, yb_tok], w=[yb_tok])
                    Sc.act(lambda e, yb=yb: e.activation(out=yb, in_=yb, func=AF.Silu), r=[yb_tok], w=[yb_tok])
                    xb_ = next_aux()
                    for tt in range(ntt):
                        Sc.pe(lambda e, yb=yb, tt=tt, xb_=xb_: e.transpose(out=bank_f32(xb_)[:, tt * 128:(tt + 1) * 128],
                              in_=yb[:, tt * 128:(tt + 1) * 128], identity=IDENT), r=[yb_tok, "cst"], w=[("ps", xb_)])
                    if which < 2:
                        Sc.act(lambda e, xb_=xb_: e.activation(out=sqb, in_=bank_f32(xb_)[:, 0:TG], func=AF.Square), r=[("ps", xb_)], w=[sqb_tok])
                        Sc.dve(lambda e: e.tensor_reduce(out=stt[:, 0:ntt], in_=sqb.rearrange("p (t c) -> p t c", c=128), axis=AX.X, op=ALU.add),
                               r=[sqb_tok], w=[stt_tok])
                        Sc.dve(lambda e: e.tensor_scalar(out=stt[:, 4:4 + ntt], in0=stt[:, 0:ntt], scalar1=1e-6, scalar2=None, op0=ALU.add),
                               r=[stt_tok], w=[stt_tok])
                        Sc.act(lambda e: e.activation(out=stt[:, 4:4 + ntt], in_=stt[:, 4:4 + ntt], func=AF.Sqrt), r=[stt_tok], w=[stt_tok])
                        Sc.dve(lambda e: e.reciprocal(out=stt[:, 4:4 + ntt], in_=stt[:, 4:4 + ntt]), r=[stt_tok], w=[stt_tok])
                        for tt in range(ntt):
                            Sc.dve(lambda e, ob=ob, tt=tt, xb_=xb_: e.tensor_scalar(out=ob[:, tt * 128:(tt + 1) * 128],
                                   in0=bank_f32(xb_)[:, tt * 128:(tt + 1) * 128], scalar1=stt[:, 4 + tt:5 + tt], scalar2=None, op0=ALU.mult),
                                   r=[("ps", xb_), stt_tok], w=[ob_tok])
                    else:
                        Sc.act(lambda e, ob=ob, xb_=xb_: e.activation(out=ob, in_=bank_f32(xb_)[:, 0:TG], func=AF.Copy), r=[("ps", xb_)], w=[ob_tok])
                    dst = (q_tm, k_tm, v_tm)[which]
                    Sc.dma("sp", dst[tg * TG:(tg + 1) * TG, head * 128:(head + 1) * 128].rearrange("(t p) c -> p t c", p=128),
                           ob.rearrange("p (t c) -> p t c", c=128), r=[ob_tok])
            elif c0 < 4 * D:
                ncols = sum(n for _, n in segs)
                for tt, b in enumerate(accs):
                    i2 = cnt[0] % 2; cnt[0] += 1
                    zb, zb_tok = zbufs[i2]
                    Sc.act(lambda e, zb=zb, b=b: e.activation(out=zb[:, 0:ncols], in_=bank_f32(b)[:, 0:ncols], func=AF.Copy), r=[("ps", b)], w=[zb_tok])
                    r0 = tg * TG + tt * 128
                    Sc.dma("sp", z_tm[r0:r0 + 128, c0 - 3 * D:c0 - 3 * D + ncols], zb[:, 0:ncols], r=[zb_tok])
            else:
                for tt, b in enumerate(accs):
                    i2 = cnt[0] % 2; cnt[0] += 1
                    (gb, gb_tok), (gt, gt_tok), (gT, gT_tok) = gbs[i2], gtmp[i2], gbTs[i2]
                    pb = bank_f32(b)
                    Sc.act(lambda e, gb=gb, pb=pb: e.activation(out=gb[:, H:2 * H], in_=pb[:, 0:H], func=AF.Sigmoid), r=[("ps", b)], w=[gb_tok])
                    Sc.dve(lambda e, gt=gt, pb=pb: e.tensor_tensor(out=gt[:, 0:H], in0=pb[:, H:2 * H], in1=dtb, op=ALU.add), r=[("ps", b), dtb_tok], w=[gt_tok])
                    Sc.act(lambda e, gt=gt: e.activation(out=gt[:, 0:H], in_=gt[:, 0:H], func=AF.Exp), r=[gt_tok], w=[gt_tok])
                    Sc.dve(lambda e, gt=gt: e.tensor_scalar(out=gt[:, 0:H], in0=gt[:, 0:H], scalar1=1.0, scalar2=None, op0=ALU.add), r=[gt_tok], w=[gt_tok])
                    Sc.act(lambda e, gt=gt: e.activation(out=gt[:, 0:H], in_=gt[:, 0:H], func=AF.Ln), r=[gt_tok], w=[gt_tok])
                    Sc.dve(lambda e, gt=gt: e.tensor_tensor(out=gt[:, H:2 * H], in0=gt[:, 0:H], in1=negA, op=ALU.mult), r=[gt_tok, negA_tok], w=[gt_tok])
                    xb_ = next_aux()
                    Sc.pe(lambda e, gt=gt, xb_=xb_: e.matmul(out=bank_f32(xb_)[:, 0:H], lhsT=LCUM, rhs=gt[:, H:2 * H], start=True, stop=True),
                          r=[gt_tok, "cst"], w=[("ps", xb_)])
                    Sc.dve(lambda e, gb=gb, xb_=xb_: e.tensor_copy(out=gb[:, 0:H], in_=bank_f32(xb_)[:, 0:H]), r=[("ps", xb_)], w=[gb_tok])
                    r0 = tg * TG + tt * 128
                    Sc.dma("sp", gb_tm[r0:r0 + 128, :], gb, r=[gb_tok])
                    xc_ = next_aux()
                    Sc.pe(lambda e, gb=gb, xc_=xc_: e.transpose(out=bank_f32(xc_)[0:2 * H, 0:128], in_=gb, identity=IDENT), r=[gb_tok, "cst"], w=[("ps", xc_)])
                    Sc.act(lambda e, gT=gT, xc_=xc_: e.activation(out=gT[0:2 * H, :], in_=bank_f32(xc_)[0:2 * H, 0:128], func=AF.Copy), r=[("ps", xc_)], w=[gT_tok])
                    Sc.dma("sp", gbT[:, r0:r0 + 128], gT[0:2 * H, :], r=[gT_tok])

        n_fm = 3 * D // 512; n_z = D // 512
        for tg in range(ntg):
            prov(tg)
            linear_one_tg("a_w_in", range(0, n_fm), hT3, hT_tok, TG, "fm", epi, tg)
            linear_one_tg("a_w_in", range(n_fm, n_fm + n_z + 1), hT3, hT_tok, TG, "tm", epi, tg)

    WREG = {}
    conv_q = []

    def reg_weight(name, w_ap, K, blocks):
        kc_n = K // 128
        W2 = nc.dram_tensor("W2_" + name, [len(blocks), 128, kc_n, 512], BF16, kind="Internal").ap()
        WREG[name] = dict(K=K, blocks=blocks, W2=W2, name=name)
        for b, segs in enumerate(blocks):
            off = 0
            for (c0, n) in segs:
                conv_q.append((W2[b, :, :, off:off + n], w_ap[:, c0:c0 + n].rearrange("(k p) c -> p k c", p=128), ("W2", name, b)))
                off += n

    def issue_conv(n):
        for _ in range(n):
            if not conv_q:
                return
            o_, i_, tok = conv_q.pop(0)
            Sc.dma("pool", o_, i_, w=[tok], bg=True, max_dma_last_dim=2048)
            conv_last[tok] = len(conv_q)

    conv_last = {}
    conv_need = {}

    def ensure_conv(tok):
        while any(t == tok for (_, _, t) in conv_q[:4]) or (tok not in conv_last):
            if not conv_q:
                break
            issue_conv(1)

    lin_state = {}
    NSL = 3
    KS = 16

    def linear_one_tg(wname, blk_idx, hT3, hT_tok, TG, mode, epilogue, tg):
        wr = WREG[wname]
        K = wr["K"]; W2 = wr["W2"]
        kc_n = K // 128
        nks = (kc_n + KS - 1) // KS
        key = AR.gen
        if lin_state.get("gen") != key:
            slots = []
            for i in range(NSL):
                a, tok = AR.bf16(KS * 512, "wslot%d" % i)
                slots.append((a.rearrange("p (k c) -> p k c", c=512), tok))
            lin_state["gen"] = key; lin_state["slots"] = slots; lin_state["cnt"] = 0
        slots = lin_state["slots"]
        ntt = TG // 128
        for bi_ in blk_idx:
            segs = wr["blocks"][bi_]
            ncols = sum(n for _, n in segs)
            nacc = (ncols + 127) // 128 if mode == "fm" else ntt
            accs = [next_acc() for _ in range(nacc)]
            for ks in range(nks):
                k0 = ks * KS; nk = min(KS, kc_n - k0)
                sl, sl_tok = slots[lin_state["cnt"] % NSL]; lin_state["cnt"] += 1
                ensure_conv(("W2", wname, bi_))
                Sc.dma("pool", sl[:, 0:nk, 0:ncols], W2[bi_, :, k0:k0 + nk, 0:ncols], r=[("W2", wname, bi_)], w=[sl_tok])
                issue_conv(1)
                for kk in range(nk):
                    first = (ks == 0 and kk == 0); last = (ks == nks - 1 and kk == nk - 1)
                    for ai in range(nacc):
                        if mode == "fm":
                            cw = min(128, ncols - ai * 128)
                            Sc.pe(lambda e, ai=ai, kk=kk, k0=k0, sl=sl, cw=cw, first=first, last=last, b=accs[ai]: e.matmul(
                                out=bank_f32(b)[0:cw, 0:TG], lhsT=sl[:, kk, ai * 128:ai * 128 + cw], rhs=hT3[:, k0 + kk, :],
                                start=first, stop=last), r=[sl_tok, hT_tok], w=[("ps", accs[ai])])
                        else:
                            Sc.pe(lambda e, ai=ai, kk=kk, k0=k0, sl=sl, ncols=ncols, first=first, last=last, b=accs[ai]: e.matmul(
                                out=bank_f32(b)[:, 0:ncols], lhsT=hT3[:, k0 + kk, ai * 128:(ai + 1) * 128], rhs=sl[:, kk, 0:ncols],
                                start=first, stop=last), r=[sl_tok, hT_tok], w=[("ps", accs[ai])])
            epilogue(tg, bi_, segs, accs)

    def phase_B():
        AR.reset()
        scale = 128.0 ** -0.5
        gbcol, gbcol_tok = AR.f32(NT * 2 * H, "gbcol")
        gbcol3 = gbcol.rearrange("p (t c) -> p t c", c=2 * H)
        Sc.dma("sp", gbcol3, gb_tm.rearrange("(t p) c -> p t c", p=128), w=[gbcol_tok])
        bgcol, bgcol_tok = AR.f32(NT * H, "bgcol")
        bgcol3 = bgcol.rearrange("p (t c) -> p t c", c=H)
        Sc.act(lambda e: e.activation(out=bgcol3, in_=gbcol3[:, :, 0:H], func=AF.Exp), r=[gbcol_tok], w=[bgcol_tok])
        Sc.dve(lambda e: e.tensor_tensor(out=bgcol3, in0=bgcol3, in1=gbcol3[:, :, H:2 * H], op=ALU.mult), r=[bgcol_tok, gbcol_tok], w=[bgcol_tok])
        G = 3 if H >= 3 else 1
        streams = []
        for gi in range(G):
            st = {}
            def T(n, name, gi=gi):
                return AR.f32(n, "%s_%d" % (name, gi))
            def TB(n, name, gi=gi):
                return AR.bf16(n, "%s_%d" % (name, gi))
            st["Grow"] = T(S_, "Grow"); st["Brow"] = T(S_, "Brow")
            st["dl"] = T(NT, "dl")
            for nm in ("q0", "k0", "v0", "q1", "k1", "v1", "DT", "DTs", "X", "A", "R", "P", "Q", "P2", "Q2",
                       "u", "EG", "osb0", "osb1", "Sst"):
                st[nm] = T(128, nm)
            for nm in ("kT", "qT", "AcT", "vb", "kbg", "wT", "qdT", "kdA", "kdB", "vnew", "Sb", "Rb"):
                st[nm] = TB(128, nm)
            streams.append(st)
        F32R = mybir.dt.float32r
        def fr(ap):
            return ap.bitcast(F32R) if INV_F32R else ap
        bank_rot = [0]
        def nb():
            i = bank_rot[0] % 8; bank_rot[0] += 1; return i

        def head_gen(st, h):
            Grow, Grow_tok = st["Grow"]; Brow, Brow_tok = st["Brow"]; dl, dl_tok = st["dl"]
            Sc.dma("sp", Grow, gbT[h:h + 1, :].partition_broadcast(128).rearrange("p o n -> p (o n)"), w=[Grow_tok])
            Sc.dma("sp", Brow, gbT[H + h:H + h + 1, :].partition_broadcast(128).rearrange("p o n -> p (o n)"), w=[Brow_tok])
            Sc.dve(lambda e: e.tensor_tensor(out=dl[0:64, :], in0=Grow[0:64, 63:S_:128], in1=gbcol3[0:64, :, h], op=ALU.subtract),
                   r=[Grow_tok, gbcol_tok], w=[dl_tok])
            Sc.dve(lambda e: e.tensor_tensor(out=dl[64:128, :], in0=Grow[64:128, 127:S_:128], in1=gbcol3[64:128, :, h], op=ALU.subtract),
                   r=[Grow_tok, gbcol_tok], w=[dl_tok])
            Sc.act(lambda e: e.activation(out=dl, in_=dl, func=AF.Exp), r=[dl_tok], w=[dl_tok])
            Sst, Sst_tok = st["Sst"]
            Sc.dve(lambda e: e.memset(Sst, 0.0), w=[Sst_tok])
            Sb, Sb_tok = st["Sb"]
            Sc.dve(lambda e: e.memset(Sb, 0.0), w=[Sb_tok])
            vnew, vnew_tok = st["vnew"]
            Sc.dve(lambda e: e.memset(vnew, 0.0), w=[vnew_tok])
            kdA, kdA_tok = st["kdA"]; kdB, kdB_tok = st["kdB"]
            Sc.dve(lambda e: e.memset(kdA, 0.0), w=[kdA_tok])
            Sc.dve(lambda e: e.memset(kdB, 0.0), w=[kdB_tok])
            yield
            for t in range(NT):
                par = t % 2
                (qt, qt_tok), (kt, kt_tok), (vt, vt_tok) = st["q%d" % par], st["k%d" % par], st["v%d" % par]
                r0 = t * 128
                hs = slice(h * 128, (h + 1) * 128)
                Sc.dma("sp", qt, q_tm[r0:r0 + 128, hs], w=[qt_tok])
                Sc.dma("sp", kt, k_tm[r0:r0 + 128, hs], w=[kt_tok])
                Sc.dma("sp", vt, v_tm[r0:r0 + 128, hs], w=[vt_tok])
                gcol = gbcol3[:, t, h:h + 1]; bcol = gbcol3[:, t, H + h:H + h + 1]; bgc = bgcol3[:, t, h:h + 1]
                Gr = Grow[:, r0:r0 + 128]; Br = Brow[:, r0:r0 + 128]
                kT, kT_tok = st["kT"]; qT, qT_tok = st["qT"]
                b1 = nb()
                Sc.pe(lambda e, b1=b1: e.transpose(out=bank_f32(b1)[:, 0:128], in_=kt, identity=IDENT), r=[kt_tok, "cst"], w=[("ps", b1)])
                Sc.act(lambda e, b1=b1: e.activation(out=kT, in_=bank_f32(b1)[:, 0:128], func=AF.Copy), r=[("ps", b1)], w=[kT_tok])
                b2 = nb()
                Sc.pe(lambda e, b2=b2: e.transpose(out=bank_f32(b2)[:, 0:128], in_=qt, identity=IDENT), r=[qt_tok, "cst"], w=[("ps", b2)])
                Sc.dve(lambda e, b2=b2: e.tensor_copy(out=qT, in_=bank_f32(b2)[:, 0:128]), r=[("ps", b2)], w=[qT_tok])
                yield
                DT, DT_tok = st["DT"]; DTs, DTs_tok = st["DTs"]; EG, EG_tok = st["EG"]
                Sc.dve(lambda e: e.scalar_tensor_tensor(out=DT, in0=Gr, scalar=gcol, in1=MASKNEG, op0=ALU.subtract, op1=ALU.add),
                       r=[Grow_tok, gbcol_tok, "cst"], w=[DT_tok])
                Sc.act(lambda e: e.activation(out=DT, in_=DT, func=AF.Exp), r=[DT_tok], w=[DT_tok])
                Sc.act(lambda e: e.activation(out=EG, in_=Gr, func=AF.Exp), r=[Grow_tok], w=[EG_tok])
                Sc.dve(lambda e: e.tensor_tensor(out=DTs, in0=DT, in1=STRICT, op=ALU.mult), r=[DT_tok, "cst"], w=[DTs_tok])
                bkk = nb()
                Sc.pe(lambda e, bkk=bkk: e.matmul(out=bank_f32(bkk)[:, 0:128], lhsT=kT, rhs=kT, start=True, stop=True), r=[kT_tok], w=[("ps", bkk)])
                bqk = nb()
                Sc.pe(lambda e, bqk=bqk: e.matmul(out=bank_f32(bqk)[:, 0:128], lhsT=kT, rhs=qT, start=True, stop=True), r=[kT_tok, qT_tok], w=[("ps", bqk)])
                X, X_tok = st["X"]; A, A_tok = st["A"]; R, R_tok = st["R"]; AcT, AcT_tok = st["AcT"]
                Sc.dve(lambda e, bkk=bkk: e.tensor_tensor(out=X, in0=bank_f32(bkk)[:, 0:128], in1=Br, op=ALU.mult), r=[("ps", bkk), Brow_tok], w=[X_tok])
                Sc.dve(lambda e: e.tensor_tensor(out=X, in0=X, in1=DTs, op=ALU.mult), r=[X_tok, DTs_tok], w=[X_tok])
                Sc.dve(lambda e, bqk=bqk: e.scalar_tensor_tensor(out=AcT, in0=bank_f32(bqk)[:, 0:128], scalar=scale, in1=DT, op0=ALU.mult, op1=ALU.mult),
                       r=[("ps", bqk), DT_tok], w=[AcT_tok])
                yield
                ba = nb()
                Sc.pe(lambda e, ba=ba: e.transpose(out=bank_f32(ba)[:, 0:128], in_=X, identity=IDENT), r=[X_tok, "cst"], w=[("ps", ba)])
                Sc.act(lambda e, ba=ba: e.activation(out=A, in_=bank_f32(ba)[:, 0:128], func=AF.Copy), r=[("ps", ba)], w=[A_tok])
                Sc.dve(lambda e: e.tensor_tensor(out=R, in0=IDENT, in1=X, op=ALU.subtract), r=["cst", X_tok], w=[R_tok])
                Pc, Pc_tok = X, X_tok
                Qc, Qc_tok = A, A_tok
                pq = [(st["P"], st["Q"]), (st["P2"], st["Q2"])]
                for kstage in range(1, 6):
                    (Pn, Pn_tok), (Qn, Qn_tok) = pq[kstage % 2]
                    bq = nb()
                    Sc.pe(lambda e, bq=bq, Pc=Pc, Qc=Qc: e.matmul(out=bank_f32(bq)[:, 0:128], lhsT=fr(Pc), rhs=fr(Qc), start=True, stop=True),
                          r=[Pc_tok, Qc_tok], w=[("ps", bq)])
                    Sc.dve(lambda e, bq=bq, Qn=Qn: e.tensor_copy(out=Qn, in_=bank_f32(bq)[:, 0:128]), r=[("ps", bq)], w=[Qn_tok])
                    if kstage < 5:
                        bp = nb()
                        Sc.pe(lambda e, bp=bp, Pc=Pc, Qc=Qc: e.matmul(out=bank_f32(bp)[:, 0:128], lhsT=fr(Qc), rhs=fr(Pc), start=True, stop=True),
                              r=[Pc_tok, Qc_tok], w=[("ps", bp)])
                        Sc.act(lambda e, bp=bp, Pn=Pn: e.activation(out=Pn, in_=bank_f32(bp)[:, 0:128], func=AF.Copy), r=[("ps", bp)], w=[Pn_tok])
                    bm = nb()
                    Sc.pe(lambda e, bm=bm, Qn=Qn: e.matmul(out=bank_f32(bm)[:, 0:128], lhsT=fr(Qn), rhs=fr(R), start=True, stop=True),
                          r=[Qn_tok, R_tok], w=[("ps", bm)])
                    Sc.dve(lambda e, bm=bm: e.tensor_tensor(out=R, in0=R, in1=bank_f32(bm)[:, 0:128], op=ALU.add), r=[R_tok, ("ps", bm)], w=[R_tok])
                    Pc, Pc_tok, Qc, Qc_tok = Pn, Pn_tok, Qn, Qn_tok
                    yield
                vb, vb_tok = st["vb"]; kbg, kbg_tok = st["kbg"]; u, u_tok = st["u"]; wT, wT_tok = st["wT"]; qdT, qdT_tok = st["qdT"]
                Sc.dve(lambda e: e.tensor_scalar(out=vb, in0=vt, scalar1=bcol, scalar2=None, op0=ALU.mult), r=[vt_tok, gbcol_tok], w=[vb_tok])
                Sc.dve(lambda e: e.tensor_scalar(out=kbg, in0=kt, scalar1=bgc, scalar2=None, op0=ALU.mult), r=[kt_tok, bgcol_tok], w=[kbg_tok])
                Rb, Rb_tok = st["Rb"]
                Sc.act(lambda e: e.activation(out=Rb, in_=R, func=AF.Copy), r=[R_tok], w=[Rb_tok])
                bu = nb()
                Sc.pe(lambda e, bu=bu: e.matmul(out=bank_f32(bu)[:, 0:128], lhsT=Rb, rhs=vb, start=True, stop=True), r=[Rb_tok, vb_tok], w=[("ps", bu)])
                Sc.act(lambda e, bu=bu: e.activation(out=u, in_=bank_f32(bu)[:, 0:128], func=AF.Copy), r=[("ps", bu)], w=[u_tok])
                bw = nb()
                Sc.pe(lambda e, bw=bw: e.matmul(out=bank_f32(bw)[:, 0:128], lhsT=kbg, rhs=Rb, start=True, stop=True), r=[Rb_tok, kbg_tok], w=[("ps", bw)])
                Sc.dve(lambda e, bw=bw: e.tensor_copy(out=wT, in_=bank_f32(bw)[:, 0:128]), r=[("ps", bw)], w=[wT_tok])
                Sc.dve(lambda e: e.scalar_tensor_tensor(out=qdT, in0=qT, scalar=scale, in1=EG, op0=ALU.mult, op1=ALU.mult), r=[qT_tok, EG_tok], w=[qdT_tok])
                Sc.dve(lambda e: e.tensor_scalar(out=kdA[0:64, :], in0=kt[0:64, :], scalar1=dl[0:64, t:t + 1], scalar2=None, op0=ALU.mult),
                       r=[kt_tok, dl_tok], w=[kdA_tok])
                Sc.dve(lambda e: e.tensor_scalar(out=kdB[64:128, :], in0=kt[64:128, :], scalar1=dl[64:128, t:t + 1], scalar2=None, op0=ALU.mult),
                       r=[kt_tok, dl_tok], w=[kdB_tok])
                yield
                osb, osb_tok = st["osb%d" % par]
                for c in range(2):
                    rs_ = slice(64 * c, 64 * c + 64)
                    kd, kd_tok = (kdA, kdA_tok) if c == 0 else (kdB, kdB_tok)
                    bws = nb()
                    Sc.pe(lambda e, bws=bws: e.matmul(out=bank_f32(bws)[:, 0:128], lhsT=wT, rhs=Sb, start=True, stop=True), r=[wT_tok, Sb_tok], w=[("ps", bws)])
                    Sc.dve(lambda e, bws=bws, rs_=rs_: e.tensor_tensor(out=vnew[rs_, :], in0=u[rs_, :], in1=bank_f32(bws)[rs_, 0:128], op=ALU.subtract),
                           r=[u_tok, ("ps", bws)], w=[vnew_tok])
                    bo = nb()
                    Sc.pe(lambda e, bo=bo: e.matmul(out=bank_f32(bo)[:, 0:128], lhsT=qdT, rhs=Sb, start=True, stop=False), r=[qdT_tok, Sb_tok], w=[("ps", bo)])
                    Sc.pe(lambda e, bo=bo: e.matmul(out=bank_f32(bo)[:, 0:128], lhsT=AcT, rhs=vnew, start=False, stop=True), r=[AcT_tok, vnew_tok], w=[("ps", bo)])
                    Sc.act(lambda e, bo=bo, rs_=rs_: e.activation(out=osb[rs_, :], in_=bank_f32(bo)[rs_, 0:128], func=AF.Copy), r=[("ps", bo)], w=[osb_tok])
                    bs = nb()
                    Sc.pe(lambda e, bs=bs, kd=kd: e.matmul(out=bank_f32(bs)[:, 0:128], lhsT=kd, rhs=vnew, start=True, stop=True), r=[kd_tok, vnew_tok], w=[("ps", bs)])
                    sdcol = EG[:, 64 * c + 63:64 * c + 64]
                    Sc.dve(lambda e, bs=bs, sdcol=sdcol: e.scalar_tensor_tensor(out=Sb, in0=Sst, scalar=sdcol, in1=bank_f32(bs)[:, 0:128], op0=ALU.mult, op1=ALU.add),
                           r=[Sst_tok, EG_tok, ("ps", bs)], w=[Sb_tok])
                    Sc.dve(lambda e, bs=bs, sdcol=sdcol: e.scalar_tensor_tensor(out=Sst, in0=Sst, scalar=sdcol, in1=bank_f32(bs)[:, 0:128], op0=ALU.mult, op1=ALU.add),
                           r=[Sst_tok, EG_tok, ("ps", bs)], w=[Sst_tok])
                    yield
                Sc.dma("sp", o_gdn[r0:r0 + 128, hs], osb, r=[osb_tok])

        for h0 in range(0, H, G):
            gens = [head_gen(streams[gi], h0 + gi) for gi in range(min(G, H - h0))]
            alive = list(gens)
            while alive:
                nxt = []
                for g in alive:
                    try:
                        next(g); nxt.append(g)
                    except StopIteration:
                        pass
                alive = nxt

    def resid_linear(wname, prov, hT3, hT_tok, TG, resid_src, dst, ntg):
        rbufs = [AR.f32(512, "rbuf%d" % i) for i in range(2)]
        cnt = [0]

        def epi(tg, bi_, segs, accs):
            c0 = segs[0][0]; ncols = segs[0][1]
            for tt, b in enumerate(accs):
                i2 = cnt[0] % 2; cnt[0] += 1
                rb, rb_tok = rbufs[i2]
                r0 = tg * TG + tt * 128
                Sc.dma("sp", rb[:, 0:ncols], resid_src[r0:r0 + 128, c0:c0 + ncols], w=[rb_tok])
                Sc.dve(lambda e, rb=rb, b=b: e.tensor_tensor(out=rb[:, 0:ncols], in0=rb[:, 0:ncols], in1=bank_f32(b)[:, 0:ncols], op=ALU.add),
                       r=[rb_tok, ("ps", b)], w=[rb_tok])
                Sc.dma("sp", dst[r0:r0 + 128, c0:c0 + ncols], rb[:, 0:ncols], r=[rb_tok])
        nb_ = len(WREG[wname]["blocks"])
        for tg in range(ntg):
            prov(tg)
            linear_one_tg(wname, range(nb_), hT3, hT_tok, TG, "tm", epi, tg)

    def phase_C():
        AR.reset()
        TG = 512 if S_ >= 512 else S_
        prol = rms_prologue_factory(a_out_norm_t, D, group=128)
        prov, hT3, hT_tok = make_prov_tm(o_gdn, D, TG, prol, extra_src=z_tm)
        resid_linear("a_w_out", prov, hT3, hT_tok, TG, x_in, x1, S_ // TG)

    def phase_FFN(layer, src, dst):
        AR.reset()
        TG = 512 if S_ >= 512 else S_
        ntg = S_ // TG
        prol = rms_prologue_factory(ffn_norm[layer:layer + 1, :], D)
        prov, hT3, hT_tok = make_prov_tm(src, D, TG, prol)
        sgb = [AR.f32(TG, "sg%d" % i) for i in range(2)]
        acb = [AR.bf16(TG, "ac%d" % i) for i in range(2)]
        cnt = [0]

        def epi(tg, bi_, segs, accs):
            f0 = segs[0][0]; nf = segs[0][1]
            nch = nf // 128
            for ci in range(nch):
                i2 = cnt[0] % 2; cnt[0] += 1
                (sg, sg_tok), (ac, ac_tok) = sgb[i2], acb[i2]
                bg_, bu_ = accs[ci], accs[nch + ci]
                Sc.act(lambda e, sg=sg, bg_=bg_: e.activation(out=sg, in_=bank_f32(bg_)[:, 0:TG], func=AF.Silu), r=[("ps", bg_)], w=[sg_tok])
                Sc.dve(lambda e, sg=sg, ac=ac, bu_=bu_: e.tensor_tensor(out=ac, in0=sg, in1=bank_f32(bu_)[:, 0:TG], op=ALU.mult),
                       r=[sg_tok, ("ps", bu_)], w=[ac_tok])
                fr = f0 + ci * 128
                Sc.dma("sp", actT[fr:fr + 128, tg * TG:(tg + 1) * TG], ac, r=[ac_tok])
        nb_ = len(WREG["gu%d" % layer]["blocks"])
        for tg in range(ntg):
            prov(tg)
            linear_one_tg("gu%d" % layer, range(nb_), hT3, hT_tok, TG, "fm", epi, tg)
        Sc.barrier()
        AR.reset()
        TG2 = 512 if S_ >= 512 else S_
        prov2, hT3b, hT_tokb = make_prov_fm(actT, F, TG2)
        resid_linear("dn%d" % layer, prov2, hT3b, hT_tokb, TG2, src, dst, S_ // TG2)

    def phase_E():
        TG = 512 if S_ >= 512 else S_
        ntg = S_ // TG

        def run(norm_row, wname, ncols_total, gain_row, dstT, do_v):
            AR.reset()
            prol = rms_prologue_factory(norm_row, D)
            prov, hT3, hT_tok = make_prov_tm(x2, D, TG, prol)
            gn, gn_tok = AR.f32(128, "gn")
            Sc.dma("sp", gn, bcast_row(gain_row, 128), w=[gn_tok])
            sqb, sqb_tok = AR.f32(512, "sqb")
            stt, stt_tok = AR.f32(16, "stt")
            knb = [AR.bf16(512, "kn%d" % i) for i in range(2)]
            kTb = [AR.bf16(512, "kTb%d" % i) for i in range(2)]
            vbb = [AR.bf16(512, "vbb%d" % i) for i in range(2)]
            cnt = [0]

            def epi(tg, bi_, segs, accs):
                c0 = segs[0][0]
                for tt, b in enumerate(accs):
                    i2 = cnt[0] % 2; cnt[0] += 1
                    r0 = tg * TG + tt * 128
                    pb = bank_f32(b)
                    if c0 < D:
                        (kn, kn_tok), (kTs, kTs_tok) = knb[i2], kTb[i2]
                        Sc.act(lambda e, pb=pb: e.activation(out=sqb, in_=pb[:, 0:512], func=AF.Square), r=[("ps", b)], w=[sqb_tok])
                        Sc.dve(lambda e: e.tensor_reduce(out=stt[:, 0:4], in_=sqb.rearrange("p (g c) -> p g c", c=128), axis=AX.X, op=ALU.add), r=[sqb_tok], w=[stt_tok])
                        Sc.dve(lambda e: e.tensor_scalar(out=stt[:, 4:8], in0=stt[:, 0:4], scalar1=1.0 / 128, scalar2=RMS_EPS, op0=ALU.mult, op1=ALU.add), r=[stt_tok], w=[stt_tok])
                        Sc.act(lambda e: e.activation(out=stt[:, 4:8], in_=stt[:, 4:8], func=AF.Sqrt), r=[stt_tok], w=[stt_tok])
                        Sc.dve(lambda e: e.reciprocal(out=stt[:, 4:8], in_=stt[:, 4:8]), r=[stt_tok], w=[stt_tok])
                        for g in range(4):
                            Sc.dve(lambda e, kn=kn, pb=pb, g=g: e.scalar_tensor_tensor(out=kn[:, g * 128:(g + 1) * 128], in0=pb[:, g * 128:(g + 1) * 128],
                                   scalar=stt[:, 4 + g:5 + g], in1=gn, op0=ALU.mult, op1=ALU.mult), r=[("ps", b), stt_tok, gn_tok], w=[kn_tok])
                        xb_ = next_aux()
                        for g in range(4):
                            Sc.pe(lambda e, kn=kn, g=g, xb_=xb_: e.transpose(out=bank_bf16(xb_)[:, g * 128:(g + 1) * 128], in_=kn[:, g * 128:(g + 1) * 128], identity=IDENTB),
                                  r=[kn_tok, "cstb"], w=[("ps", xb_)])
                        Sc.act(lambda e, kTs=kTs, xb_=xb_: e.activation(out=kTs, in_=bank_bf16(xb_)[:, 0:512], func=AF.Copy), r=[("ps", xb_)], w=[kTs_tok])
                        g0 = c0 // 128
                        Sc.dma("sp", dstT[g0:g0 + 4, :, r0:r0 + 128].rearrange("g p t -> p g t"), kTs.rearrange("p (g t) -> p g t", t=128), r=[kTs_tok])
                    else:
                        vbt, vbt_tok = vbb[i2]
                        Sc.act(lambda e, vbt=vbt, pb=pb: e.activation(out=vbt, in_=pb[:, 0:512], func=AF.Copy), r=[("ps", b)], w=[vbt_tok])
                        Sc.dma("sp", vb_d[r0:r0 + 128, c0 - D:c0 - D + 512], vbt, r=[vbt_tok])
            nb_ = len(WREG[wname]["blocks"])
            for tg in range(ntg):
                prov(tg)
                linear_one_tg(wname, range(nb_), hT3, hT_tok, TG, "tm", epi, tg)
            Sc.barrier()
        run(kv_norm, "w_kv", 2 * D, k_norm, kT_d, True)
        run(b_norm, "b_w_q", D, b_q_norm, qT_d, False)

    def phase_F(lam_init):
        AR.reset()
        scale = 128.0 ** -0.5
        QG = 512 if S_ >= 512 else S_
        nqg = S_ // QG
        ntq = QG // 128
        lp, lp_tok = AR.f32(512, "lp")
        Sc.dma("sp", lp, bcast_row(b_lambda, 512), w=[lp_tok])
        lt, lt_tok = AR.f32(272, "lt")
        Sc.dve(lambda e: e.tensor_tensor(out=lt[:, 0:128], in0=lp[:, 0:128], in1=lp[:, 128:256], op=ALU.mult), r=[lp_tok], w=[lt_tok])
        Sc.dve(lambda e: e.tensor_tensor(out=lt[:, 128:256], in0=lp[:, 256:384], in1=lp[:, 384:512], op=ALU.mult), r=[lp_tok], w=[lt_tok])
        Sc.dve(lambda e: e.tensor_reduce(out=lt[:, 256:258], in_=lt[:, 0:256].rearrange("p (g c) -> p g c", c=128), axis=AX.X, op=ALU.add), r=[lt_tok], w=[lt_tok])
        Sc.act(lambda e: e.activation(out=lt[:, 258:260], in_=lt[:, 256:258], func=AF.Exp), r=[lt_tok], w=[lt_tok])
        Sc.dve(lambda e: e.tensor_tensor(out=lt[:, 260:261], in0=lt[:, 259:260], in1=lt[:, 258:259], op=ALU.subtract), r=[lt_tok], w=[lt_tok])
        Sc.dve(lambda e: e.tensor_scalar(out=lt[:, 261:262], in0=lt[:, 260:261], scalar1=-lam_init, scalar2=None, op0=ALU.add), r=[lt_tok], w=[lt_tok])
        neglam = lt[:, 261:262]
        kTs = [AR.bf16(S_, "kTs%d" % m) for m in range(2)]
        qTs = [AR.bf16(S_, "qTs%d" % m) for m in range(2)]
        vx, vx_tok = AR.bf16(NT * 258, "vx")
        vx3 = vx.rearrange("p (t c) -> p t c", c=258)
        Sc.dve(lambda e: e.memset(vx, 1.0), w=[vx_tok])
        Eb = [AR.bf16(QG, "E%d" % i) for i in range(3)]
        Em = [AR.bf16(128, "Em%d" % i) for i in range(2)]
        AMB, AMB_tok = AR.bf16(128, "AMB")
        Sc.dve(lambda e: e.tensor_copy(out=AMB, in_=AMASK), r=["cst"], w=[AMB_tok])
        Osb = [[AR.f32(257, "O%d_%d" % (m, tt)) for tt in range(ntq)] for m in range(2)]
        outb = [AR.f32(256, "ob%d" % i) for i in range(2)]
        rcp, rcp_tok = AR.f32(8, "rcp")
        ecnt = [0]; ocnt = [0]
        for h in range(DH):
            for m in range(2):
                Sc.dma("sp", kTs[m][0], kT_d[2 * h + m], w=[kTs[m][1]])
                Sc.dma("sp", qTs[m][0], qT_d[2 * h + m], w=[qTs[m][1]])
            Sc.dma("sp", vx3[:, :, 0:256], vb_d[:, h * 256:(h + 1) * 256].rearrange("(t p) c -> p t c", p=128), w=[vx_tok])
            for qg in range(nqg):
                for m in range(2):
                    kTm, kT_tok = kTs[m]; qTm, qT_tok = qTs[m]
                    obanks = [0, 1, 2, 3][:ntq]
                    nkt = (qg + 1) * ntq
                    def issue_st(kt_):
                        sb_ = 4 + (ecnt[0] % 3)
                        E, E_tok = Eb[ecnt[0] % 3]; ecnt[0] += 1
                        Sc.pe(lambda e, sb_=sb_, kt_=kt_, kTm=kTm, qTm=qTm, qg=qg: e.matmul(out=bank_f32(sb_)[:, 0:QG], lhsT=kTm[:, kt_ * 128:(kt_ + 1) * 128],
                              rhs=qTm[:, qg * QG:(qg + 1) * QG], start=True, stop=True), r=[kT_tok, qT_tok], w=[("ps", sb_)])
                        Sc.act(lambda e, sb_=sb_, E=E: e.activation(out=E, in_=bank_f32(sb_)[:, 0:QG], func=AF.Exp, scale=scale), r=[("ps", sb_)], w=[E_tok])
                        return E, E_tok
                    pend = [issue_st(0)]
                    if nkt > 1:
                        pend.append(issue_st(1))
                    for kt_ in range(nkt):
                        E, E_tok = pend.pop(0)
                        if kt_ + 2 < nkt:
                            pend.append(issue_st(kt_ + 2))
                        for tt in range(ntq):
                            qt_ = qg * ntq + tt
                            if qt_ < kt_:
                                continue
                            lhs = E[:, tt * 128:(tt + 1) * 128]; lhs_tok = E_tok
                            if qt_ == kt_:
                                Emm, Emm_tok = Em[tt % 2]
                                Sc.dve(lambda e, Emm=Emm, lhs=lhs: e.tensor_tensor(out=Emm, in0=lhs, in1=AMB, op=ALU.mult), r=[E_tok, AMB_tok], w=[Emm_tok])
                                lhs, lhs_tok = Emm, Emm_tok
                            Sc.pe(lambda e, tt=tt, lhs=lhs, kt_=kt_, qt_=qt_: e.matmul(out=bank_f32(obanks[tt])[:, 0:257], lhsT=lhs, rhs=vx3[:, kt_, 0:257],
                                  start=(kt_ == 0), stop=(kt_ == qt_)), r=[lhs_tok, vx_tok], w=[("ps", obanks[tt])])
                    for tt in range(ntq):
                        O, O_tok = Osb[m][tt]
                        if tt % 2 == 0:
                            Sc.act(lambda e, O=O, tt=tt: e.activation(out=O, in_=bank_f32(obanks[tt])[:, 0:257], func=AF.Copy), r=[("ps", obanks[tt])], w=[O_tok])
                        else:
                            Sc.dve(lambda e, O=O, tt=tt: e.tensor_copy(out=O, in_=bank_f32(obanks[tt])[:, 0:257]), r=[("ps", obanks[tt])], w=[O_tok])
                for tt in range(ntq):
                    (O0, O0_tok), (O1, O1_tok) = Osb[0][tt], Osb[1][tt]
                    ob, ob_tok = outb[ocnt[0] % 2]; ocnt[0] += 1
                    Sc.dve(lambda e, O0=O0: e.reciprocal(out=rcp[:, 0:1], in_=O0[:, 256:257]), r=[O0_tok], w=[rcp_tok])
                    Sc.dve(lambda e, O1=O1: e.reciprocal(out=rcp[:, 1:2], in_=O1[:, 256:257]), r=[O1_tok], w=[rcp_tok])
                    Sc.dve(lambda e: e.tensor_tensor(out=rcp[:, 2:3], in0=rcp[:, 1:2], in1=neglam, op=ALU.mult), r=[rcp_tok, lt_tok], w=[rcp_tok])
                    Sc.dve(lambda e, ob=ob, O0=O0: e.tensor_scalar(out=ob, in0=O0[:, 0:256], scalar1=rcp[:, 0:1], scalar2=None, op0=ALU.mult), r=[O0_tok, rcp_tok], w=[ob_tok])
                    Sc.dve(lambda e, ob=ob, O1=O1: e.scalar_tensor_tensor(out=ob, in0=O1[:, 0:256], scalar=rcp[:, 2:3], in1=ob, op0=ALU.mult, op1=ALU.add),
                           r=[O1_tok, rcp_tok, ob_tok], w=[ob_tok])
                    r0 = qg * QG + tt * 128
                    Sc.dma("sp", o_att[r0:r0 + 128, h * 256:(h + 1) * 256], ob, r=[ob_tok])

    def phase_G(lam_init):
        AR.reset()
        TG = 512 if S_ >= 512 else S_
        prol = rms_prologue_factory(b_sub_norm_t, D, group=256, post_scale=(1.0 - lam_init))
        prov, hT3, hT_tok = make_prov_tm(o_att, D, TG, prol)
        resid_linear("b_w_out", prov, hT3, hT_tok, TG, x2, x3, S_ // TG)

    lam_init = 0.8 - 0.6 * math.exp(-0.3 * 1)
    blkD = [[(c, 512)] for c in range(0, D, 512)]
    reg_weight("a_w_in", a_w_in, D, [[(c, 512)] for c in range(0, 4 * D, 512)] + [[(4 * D, 2 * H)]])
    reg_weight("a_w_out", a_w_out, D, blkD)
    reg_weight("gu0", ffn_w_gu[0], D, [[(f, 256), (F + f, 256)] for f in range(0, F, 256)])
    reg_weight("dn0", ffn_w_down[0], F, blkD)
    reg_weight("w_kv", w_kv, D, [[(c, 512)] for c in range(0, 2 * D, 512)])
    reg_weight("b_w_q", b_w_q, D, blkD)
    reg_weight("b_w_out", b_w_out, D, blkD)
    reg_weight("gu1", ffn_w_gu[1], D, [[(f, 256), (F + f, 256)] for f in range(0, F, 256)])
    reg_weight("dn1", ffn_w_down[1], F, blkD)
    issue_conv(len(WREG["a_w_in"]["blocks"]))
    phase_A(); Sc.barrier()
    phase_B(); Sc.barrier()
    phase_C(); Sc.barrier()
    phase_FFN(0, x1, x2); Sc.barrier()
    phase_E()
    phase_F(lam_init); Sc.barrier()
    phase_G(lam_init); Sc.barrier()
    phase_FFN(1, x3, out_d)
    Sc.finish()
    Sc.emit(nc, stack)
    stack.close()
    return nc


def prep_inputs(inputs, cfg, b):
    f = lambda a: np.ascontiguousarray(np.asarray(a, dtype=np.float32))
    H = cfg.H
    m = {
        "x": f(inputs["x"][b]),
        "a_norm": f(inputs["a_norm"]).reshape(1, -1),
        "a_w_in": f(inputs["a_w_in"][0]),
        "a_conv_t": f(np.asarray(inputs["a_conv"][0]).T),
        "a_A_log": f(inputs["a_A_log"]).reshape(1, -1),
        "a_dt_bias": f(inputs["a_dt_bias"]).reshape(1, -1),
        "a_out_norm_t": f(np.tile(np.asarray(inputs["a_out_norm"][0]), H)).reshape(1, -1),
        "a_w_out": f(inputs["a_w_out"][0]),
        "kv_norm": f(inputs["kv_norm"]).reshape(1, -1),
        "w_kv": f(inputs["w_kv"]),
        "k_norm": f(inputs["k_norm"]).reshape(1, -1),
        "b_norm": f(inputs["b_norm"]).reshape(1, -1),
        "b_w_q": f(inputs["b_w_q"][0]),
        "b_q_norm": f(inputs["b_q_norm"]).reshape(1, -1),
        "b_lambda": f(inputs["b_lambda"][0]).reshape(1, -1),
        "b_sub_norm_t": f(np.tile(np.asarray(inputs["b_sub_norm"][0]), cfg.DH)).reshape(1, -1),
        "b_w_out": f(inputs["b_w_out"][0]),
        "ffn_norm": f(inputs["ffn_norm"]),
        "ffn_w_gate_up": f(inputs["ffn_w_gate_up"]),
        "ffn_w_down": f(inputs["ffn_w_down"]),
        "consts": make_consts(),
    }
    return m


def run(inputs, cfg, debug=False, trace=False):
    nc = build(cfg, debug=debug)
    B = np.asarray(inputs["x"]).shape[0]
    in_maps = [prep_inputs(inputs, cfg, c) for c in range(B)]
    res = run_bass_kernel_spmd(nc, in_maps, core_ids=list(range(B)), **({"trace": True} if trace else {}))
    return res


def kernel(**inputs):
    cfg = Cfg(4096, 4096, 11008)
    res = run(inputs, cfg)
    B = np.asarray(inputs["x"]).shape[0]
    return np.stack([np.asarray(res.results[b]["out"], dtype=np.float32) for b in range(B)], axis=0)
```

```python
import math
import contextlib
import numpy as np
import concourse.bass as bass
import concourse.mybir as mybir
from concourse.bass_utils import run_bass_kernel_spmd

F32 = mybir.dt.float32
BF16 = mybir.dt.bfloat16
AF = mybir.ActivationFunctionType
ALU = mybir.AluOpType
AX = mybir.AxisListType

RMS_EPS = 1e-6
ARENA_WORDS = 41500
INV_F32R = False


class Cfg:
    def __init__(s, D, S, F):
        s.D = D; s.S = S; s.F = F
        s.H = D // 128; s.GIN = 4 * D + 2 * s.H; s.DH = D // 256
        s.NT = S // 128; s.KC = D // 128


class _Op:
    __slots__ = ("eng", "fn", "deps", "dma", "signals", "sem", "val", "prev")

    def __init__(s, eng, fn, deps, dma):
        s.eng = eng; s.fn = fn; s.deps = deps; s.dma = dma
        s.signals = False; s.sem = None; s.val = 0; s.prev = 0


class _Rec:
    def __init__(s):
        s.calls = []

    def __getattr__(s, name):
        def f(*a, **k):
            s.calls.append((name, a, k))
            return s
        return f


class Sched:
    CE = ("pe", "act", "dve", "pool")
    EPOCH = 12000
    NSLOT = 12
    USES = 1500

    def __init__(s):
        s.ops = []
        s.last_w = {}
        s.readers = {}
        s.barrier_deps = {}
        s.last_on = {}
        s.dmas_since = []

    def op(s, eng, fn, r=(), w=(), dma=False, bg=False):
        if fn is not None:
            rec = _Rec()
            fn(rec)
            assert len(rec.calls) == 1
            fn = rec.calls[0]
        i = len(s.ops)
        deps = {}
        for b in r:
            j = s.last_w.get(b)
            if j is not None:
                deps[j] = True
        for b in w:
            j = s.last_w.get(b)
            if j is not None and j not in deps:
                deps[j] = False
            for x in s.readers.get(b, ()):
                if x not in deps:
                    deps[x] = False
        bd = s.barrier_deps.pop(eng, None)
        if bd:
            for j in bd:
                deps[j] = True
        s.ops.append(_Op(eng, fn, deps, dma))
        for b in r:
            s.readers.setdefault(b, []).append(i)
        for b in w:
            s.last_w[b] = i
            s.readers[b] = []
        if dma:
            if not bg:
                s.dmas_since.append(i)
        else:
            s.last_on[eng] = i
        return i

    def pe(s, fn, r=(), w=()): return s.op("pe", fn, r, w)
    def act(s, fn, r=(), w=()): return s.op("act", fn, r, w)
    def dve(s, fn, r=(), w=()): return s.op("dve", fn, r, w)
    def pool(s, fn, r=(), w=()): return s.op("pool", fn, r, w)

    def dma(s, q, out, in_, r=(), w=(), bg=False, **kw):
        return s.op(q, lambda e: e.dma_start(out=out, in_=in_, **kw), r, w, dma=True, bg=bg)

    def barrier(s):
        deps = list(s.last_on.values()) + list(s.dmas_since)
        s.dmas_since = []
        s.last_w = {k: v for k, v in s.last_w.items() if isinstance(k, tuple) and k[0] == "W2"}
        s.readers = {}
        for e in ("pe", "act", "dve", "pool", "sp"):
            s.barrier_deps[e] = list(deps)

    def finish(s):
        s.barrier()
        s.op("sp", None)

    def emit(s, nc, stack):
        ops = s.ops
        for o in ops:
            keep = {}
            for j, raw in o.deps.items():
                p = ops[j]
                if p.dma or o.dma:
                    keep[j] = raw
                elif p.eng == o.eng:
                    if o.eng != "pe" and raw:
                        keep[j] = raw
                else:
                    keep[j] = raw
            o.deps = keep
            for j in keep:
                ops[j].signals = True
        cnt = {e: 0 for e in s.CE}
        dcnt = {"sp": 0, "pool": 0}
        for o in ops:
            if o.dma:
                n = dcnt[o.eng]; dcnt[o.eng] += 1
                slot = n % s.NSLOT; use = n // s.NSLOT
                o.sem = ("d", o.eng, (use // s.USES) * s.NSLOT + slot)
                o.prev = 16 * (use % s.USES)
                o.val = o.prev + 16
            elif o.signals:
                c = cnt[o.eng]; cnt[o.eng] += 1
                o.sem = ("c", o.eng, c // s.EPOCH)
                o.val = c % s.EPOCH + 1
        names = sorted({o.sem for o in ops if o.sem is not None}, key=str)
        sems = {}
        for k, nm in enumerate(names):
            sems[nm] = stack.enter_context(nc.semaphore("s%d" % k))
        assert len(sems) < 140, len(sems)
        per_eng = {e: [] for e in ("pe", "act", "dve", "pool", "sp")}
        for i, o in enumerate(ops):
            per_eng[o.eng].append(o)
        engmap = {"pe": "tensor", "act": "scalar", "dve": "vector", "pool": "gpsimd", "sp": "sync"}
        with nc.Block() as b0:
            def clr(g):
                for nm in names:
                    g.sem_clear(sems[nm])
            b0.gpsimd(clr)
        with nc.Block() as blk:
            for ename, lst in per_eng.items():
                if not lst:
                    continue

                def body(e, lst=lst):
                    known = {}
                    for o in lst:
                        for j in o.deps:
                            p = ops[j]
                            if known.get(p.sem, 0) < p.val:
                                e.wait_ge(sems[p.sem], p.val)
                                known[p.sem] = p.val
                        if o.dma and o.prev > 0 and known.get(o.sem, 0) < o.prev:
                            e.wait_ge(sems[o.sem], o.prev)
                            known[o.sem] = o.prev
                        if o.fn is None:
                            continue
                        nm_, a_, k_ = o.fn
                        inst = getattr(e, nm_)(*a_, **k_)
                        if o.dma:
                            inst.then_inc(sems[o.sem], 16)
                        elif o.signals:
                            inst.then_inc(sems[o.sem], 1)
                getattr(blk, engmap[ename])(body)


def make_consts():
    c = np.zeros((128, 6, 128), np.float32)
    i = np.arange(128)
    J, I = np.meshgrid(i, i, indexing="ij")
    same = (J // 64) == (I // 64)
    c[:, 0, :] = np.eye(128, dtype=np.float32)
    c[:, 1, :] = np.where(same & (I >= J), 0.0, -30000.0)
    c[:, 2, :] = np.where(same & (I > J), 1.0, 0.0)
    c[:, 3, :] = np.where(same & (J <= I), 1.0, 0.0)
    c[:, 4, :] = np.where((J >= 64) & (I < 64), 0.0, 1.0)
    c[:, 5, :] = 1.0
    return c.reshape(128, 6 * 128)


def build(cfg, debug=False):
    D, S_, F, H, DH, NT, KC, GIN = cfg.D, cfg.S, cfg.F, cfg.H, cfg.DH, cfg.NT, cfg.KC, cfg.GIN
    nc = bass.Bass("TRN2", target_bir_lowering=False)
    stack = contextlib.ExitStack()

    def din(name, shape, dt=F32):
        return nc.dram_tensor(name, list(shape), dt, kind="ExternalInput").ap()

    def dscr(name, shape, dt=F32):
        return nc.dram_tensor(name, list(shape), dt, kind="ExternalOutput" if debug else "Internal").ap()

    x_in = din("x", [S_, D])
    a_norm = din("a_norm", [1, D]); a_w_in = din("a_w_in", [D, GIN]); a_conv_t = din("a_conv_t", [3 * D, 4])
    a_A_log = din("a_A_log", [1, H]); a_dt_bias = din("a_dt_bias", [1, H])
    a_out_norm_t = din("a_out_norm_t", [1, D]); a_w_out = din("a_w_out", [D, D])
    kv_norm = din("kv_norm", [1, D]); w_kv = din("w_kv", [D, 2 * D]); k_norm = din("k_norm", [1, 128])
    b_norm = din("b_norm", [1, D]); b_w_q = din("b_w_q", [D, D]); b_q_norm = din("b_q_norm", [1, 128])
    b_lambda = din("b_lambda", [1, 4 * 128]); b_sub_norm_t = din("b_sub_norm_t", [1, D]); b_w_out = din("b_w_out", [D, D])
    ffn_norm = din("ffn_norm", [2, D]); ffn_w_gu = din("ffn_w_gate_up", [2, D, 2 * F]); ffn_w_down = din("ffn_w_down", [2, F, D])
    consts_d = din("consts", [128, 6 * 128])
    out_d = nc.dram_tensor("out", [S_, D], F32, kind="ExternalOutput").ap()

    q_tm = dscr("q_tm", [S_, D]); k_tm = dscr("k_tm", [S_, D]); v_tm = dscr("v_tm", [S_, D]); z_tm = dscr("z_tm", [S_, D])
    gb_tm = dscr("gb_tm", [S_, 2 * H]); gbT = dscr("gbT", [2 * H, S_])
    o_gdn = dscr("o_gdn", [S_, D]); x1 = dscr("x1", [S_, D]); x2 = dscr("x2", [S_, D]); x3 = dscr("x3", [S_, D])
    actT = dscr("actT", [F, S_], BF16)
    kT_d = dscr("kT_d", [2 * DH, 128, S_], BF16); qT_d = dscr("qT_d", [2 * DH, 128, S_], BF16)
    vb_d = dscr("vb_d", [S_, D], BF16); o_att = dscr("o_att", [S_, D])

    arena_t = stack.enter_context(nc.sbuf_tensor("arena", [128, ARENA_WORDS], F32))
    cst_t = stack.enter_context(nc.sbuf_tensor("cst", [128, 6 * 128], F32))
    cstb_t = stack.enter_context(nc.sbuf_tensor("cstb", [128, 128], BF16))
    banks = [stack.enter_context(nc.psum_tensor("bank%d" % i, [128, 512], F32)) for i in range(8)]

    Sc = Sched()
    cst = cst_t[:]
    IDENT = cst[:, 0:128]; MASKNEG = cst[:, 128:256]; STRICT = cst[:, 256:384]
    LCUM = cst[:, 384:512]; AMASK = cst[:, 512:640]; ONES = cst[:, 640:768]
    IDENTB = cstb_t[:]
    Sc.dma("sp", cst, consts_d, w=["cst"])
    Sc.dve(lambda e: e.tensor_copy(out=IDENTB, in_=IDENT), r=["cst"], w=["cstb"])
    Sc.barrier()

    class Arena:
        def __init__(s): s.off = 0; s.gen = 0
        def reset(s): s.off = 0; s.gen += 1
        def f32(s, n, name):
            a = arena_t[:, s.off:s.off + n]; s.off += n
            assert s.off <= ARENA_WORDS, (name, s.off)
            return a, (name, s.gen)
        def bf16(s, n, name):
            w = (n + 1) // 2
            a = arena_t[:, s.off:s.off + w].bitcast(BF16)[:, 0:n]; s.off += w
            assert s.off <= ARENA_WORDS, (name, s.off)
            return a, (name, s.gen)
    AR = Arena()

    def bank_f32(i): return banks[i][:]
    def bank_bf16(i): return banks[i][:].bitcast(BF16)

    rot = {"acc": 0, "aux": 0}
    def next_acc():
        i = rot["acc"] % 6; rot["acc"] += 1; return i
    def next_aux():
        i = 6 + rot["aux"] % 2; rot["aux"] += 1; return i

    def bcast_row(ap_row, n):
        return ap_row.partition_broadcast(128).rearrange("p o n -> p (o n)")

    def make_prov_tm(src, K, TG, prologue, extra_src=None):
        kc_n = K // 128
        hT, hT_tok = AR.bf16(kc_n * TG, "hT")
        hT3 = hT.rearrange("p (k t) -> p k t", t=TG)
        xin, xin_tok = AR.f32(K, "xin")
        xb, xb_tok = AR.bf16(K, "xb")
        zin = None
        if extra_src is not None:
            zin, zin_tok = AR.f32(K, "zin")
        ntt = TG // 128

        def prov(tg):
            for tt in range(ntt):
                r0 = tg * TG + tt * 128
                Sc.dma("sp", xin, src[r0:r0 + 128, :], w=[xin_tok])
                rd = [xin_tok]
                if zin is not None:
                    Sc.dma("sp", zin, extra_src[r0:r0 + 128, :], w=[zin_tok])
                    rd.append(zin_tok)
                prologue(xin, zin, xb, rd, xb_tok)
                for k0 in range(0, kc_n, 4):
                    nk = min(4, kc_n - k0)
                    bi = next_aux()
                    pb = bank_bf16(bi)
                    for kk in range(nk):
                        Sc.pe(lambda e, kk=kk, k0=k0, pb=pb: e.transpose(
                            out=pb[:, kk * 128:(kk + 1) * 128], in_=xb[:, (k0 + kk) * 128:(k0 + kk + 1) * 128], identity=IDENTB),
                            r=[xb_tok, "cstb"], w=[("ps", bi)])
                    src_ap = pb[:, 0:nk * 128].rearrange("p (k t) -> p k t", t=128)
                    dst_ap = hT3[:, k0:k0 + nk, tt * 128:(tt + 1) * 128]
                    if (k0 // 4) % 2 == 0:
                        Sc.dve(lambda e, d=dst_ap, s_=src_ap: e.tensor_copy(out=d, in_=s_), r=[("ps", bi)], w=[hT_tok])
                    else:
                        Sc.act(lambda e, d=dst_ap, s_=src_ap: e.activation(out=d, in_=s_, func=AF.Copy), r=[("ps", bi)], w=[hT_tok])
        return prov, hT3, hT_tok

    def make_prov_fm(src, K, TG):
        kc_n = K // 128
        hT, hT_tok = AR.bf16(kc_n * TG, "hT")
        hT3 = hT.rearrange("p (k t) -> p k t", t=TG)

        def prov(tg):
            step = 16
            for k0 in range(0, kc_n, step):
                nk = min(step, kc_n - k0)
                Sc.dma("sp", hT3[:, k0:k0 + nk, :],
                       src[k0 * 128:(k0 + nk) * 128, tg * TG:(tg + 1) * TG].rearrange("(k p) t -> p k t", p=128),
                       w=[(hT_tok, k0 // step)])
        return prov, hT3, (lambda ks_: (hT_tok, ks_))

    def rms_prologue_factory(gain_row_ap, K, group=None, post_scale=1.0):
        gain_b, gain_tok = AR.f32(K, "gain")
        sq, sq_tok = AR.f32(K, "sq")
        ng = 1 if group is None else K // group
        st, st_tok = AR.f32(4 * ng + 4, "stat")
        Sc.dma("sp", gain_b, bcast_row(gain_row_ap, K), w=[gain_tok])
        gsz = K if group is None else group

        def prologue(xin, zin, xb, rd, xb_tok):
            Sc.act(lambda e: e.activation(out=sq, in_=xin, func=AF.Square), r=rd[:1], w=[sq_tok])
            ss = st[:, 0:ng]; rs = st[:, ng:2 * ng]
            Sc.dve(lambda e: e.tensor_reduce(out=ss, in_=sq.rearrange("p (g c) -> p g c", c=gsz), axis=AX.X, op=ALU.add),
                   r=[sq_tok], w=[st_tok])
            ps2 = post_scale * post_scale
            Sc.dve(lambda e: e.tensor_scalar(out=rs, in0=ss, scalar1=1.0 / (gsz * ps2), scalar2=RMS_EPS / ps2, op0=ALU.mult, op1=ALU.add),
                   r=[st_tok], w=[st_tok])
            Sc.act(lambda e: e.activation(out=rs, in_=rs, func=AF.Sqrt), r=[st_tok], w=[st_tok])
            Sc.dve(lambda e: e.reciprocal(out=rs, in_=rs), r=[st_tok], w=[st_tok])
            if group is None:
                Sc.dve(lambda e: e.scalar_tensor_tensor(out=xb, in0=xin, scalar=rs[:, 0:1], in1=gain_b, op0=ALU.mult, op1=ALU.mult),
                       r=[rd[0], st_tok, gain_tok], w=[xb_tok])
            else:
                if zin is not None:
                    Sc.act(lambda e: e.activation(out=zin, in_=zin, func=AF.Silu), r=[rd[1]], w=[rd[1]])
                x3_ = xin.rearrange("p (g c) -> p g c", c=gsz)
                rsb = rs.unsqueeze(2).broadcast_to([128, ng, gsz])
                sq3 = sq.rearrange("p (g c) -> p g c", c=gsz)
                Sc.dve(lambda e: e.tensor_tensor(out=sq3, in0=x3_, in1=rsb, op=ALU.mult), r=[rd[0], st_tok, sq_tok], w=[sq_tok])
                if zin is not None:
                    Sc.dve(lambda e: e.tensor_tensor(out=sq, in0=sq, in1=gain_b, op=ALU.mult), r=[sq_tok, gain_tok], w=[sq_tok])
                    Sc.dve(lambda e: e.tensor_tensor(out=xb, in0=sq, in1=zin, op=ALU.mult), r=[sq_tok, rd[1]], w=[xb_tok])
                else:
                    Sc.dve(lambda e: e.tensor_tensor(out=xb, in0=sq, in1=gain_b, op=ALU.mult), r=[sq_tok, gain_tok], w=[xb_tok])
        return prologue

    def phase_A():
        AR.reset()
        TG = 512 if S_ >= 512 else S_
        ntg = S_ // TG
        prol = rms_prologue_factory(a_norm, D)
        prov, hT3, hT_tok = make_prov_tm(x_in, D, TG, prol)
        NCH = 3 * H
        cw_t, cw_tok = AR.f32(NCH * 4, "convw")
        cw3 = cw_t.rearrange("p (c k) -> p c k", k=4)
        Sc.dma("sp", cw3, a_conv_t.rearrange("(c p) k -> p c k", p=128), w=[cw_tok])
        halo, halo_tok = AR.f32(NCH * 3, "halo")
        halo3 = halo.rearrange("p (c k) -> p c k", k=3)
        Sc.dve(lambda e: e.memset(halo, 0.0), w=[halo_tok])
        NSLV[0] = 2
        pbufs = [AR.f32(3 + TG, "pbuf%d" % i) for i in range(2)]
        ybufs = [AR.f32(TG, "ybuf%d" % i) for i in range(8)]
        ycnt = [0]
        obufs = [AR.f32(TG, "obuf%d" % i) for i in range(2)]
        sqb, sqb_tok = AR.f32(TG, "sqb")
        stt, stt_tok = AR.f32(16, "stt")
        zbufs = [AR.f32(512, "zbuf%d" % i) for i in range(2)]
        negA, negA_tok = AR.f32(H, "negA")
        dtb, dtb_tok = AR.f32(H, "dtb")
        Sc.dma("sp", negA, bcast_row(a_A_log, H), w=[negA_tok])
        Sc.dma("sp", dtb, bcast_row(a_dt_bias, H), w=[dtb_tok])
        Sc.act(lambda e: e.activation(out=negA, in_=negA, func=AF.Exp), r=[negA_tok], w=[negA_tok])
        Sc.dve(lambda e: e.tensor_scalar(out=negA, in0=negA, scalar1=-1.0, scalar2=None, op0=ALU.mult), r=[negA_tok], w=[negA_tok])
        gbs = [AR.f32(2 * H, "gb%d" % i) for i in range(2)]
        gtmp = [AR.f32(2 * H, "gtmp%d" % i) for i in range(2)]
        gbTs = [AR.f32(128, "gbTs%d" % i) for i in range(2)]
        cnt = [0]
        ntt = TG // 128

        def epi(tg, bi_, segs, accs):
            c0 = segs[0][0]
            if c0 < 3 * D:
                stage2 = []
                for ai, b in enumerate(accs):
                    ch = c0 // 128 + ai
                    which = ch // H; head = ch % H
                    i2 = cnt[0] % 2; cnt[0] += 1
                    (pb, pb_tok), (ob, ob_tok) = pbufs[i2], obufs[i2]
                    yb, yb_tok = ybufs[ycnt[0] % 8]; ycnt[0] += 1
                    Sc.dve(lambda e, pb=pb, ch=ch: e.tensor_copy(out=pb[:, 0:3], in_=halo3[:, ch, :]), r=[halo_tok], w=[pb_tok])
                    Sc.act(lambda e, pb=pb, b=b: e.activation(out=pb[:, 3:3 + TG], in_=bank_f32(b)[:, 0:TG], func=AF.Copy),
                           r=[("ps", b)], w=[pb_tok])
                    Sc.dve(lambda e, pb=pb, ch=ch: e.tensor_copy(out=halo3[:, ch, :], in_=pb[:, TG:TG + 3]), r=[pb_tok], w=[halo_tok])
                    Sc.dve(lambda e, pb=pb, yb=yb, ch=ch: e.tensor_scalar(out=yb, in0=pb[:, 0:TG], scalar1=cw3[:, ch, 0:1], scalar2=None, op0=ALU.mult),
                           r=[pb_tok, cw_tok], w=[yb_tok])
                    for k in range(1, 4):
                        Sc.dve(lambda e, pb=pb, yb=yb, ch=ch, k=k: e.scalar_tensor_tensor(
                            out=yb, in0=pb[:, k:k + TG], scalar=cw3[:, ch, k:k + 1], in1=yb, op0=ALU.mult, op1=ALU.add),
                            r=[pb_tok, cw_tok, yb_tok], w=[yb_tok])
                    Sc.act(lambda e, yb=yb: e.activation(out=yb, in_=yb, func=AF.Silu), r=[yb_tok], w=[yb_tok])

                    def st2(yb=yb, yb_tok=yb_tok, ob=ob, ob_tok=ob_tok, which=which, head=head, tg=tg):
                        xb_ = next_aux()
                        for tt in range(ntt):
                            Sc.pe(lambda e, yb=yb, tt=tt, xb_=xb_: e.transpose(out=bank_f32(xb_)[:, tt * 128:(tt + 1) * 128],
                                  in_=yb[:, tt * 128:(tt + 1) * 128], identity=IDENT), r=[yb_tok, "cst"], w=[("ps", xb_)])
                        if which < 2:
                            Sc.act(lambda e, xb_=xb_: e.activation(out=sqb, in_=bank_f32(xb_)[:, 0:TG], func=AF.Square), r=[("ps", xb_)], w=[sqb_tok])
                            Sc.dve(lambda e: e.tensor_reduce(out=stt[:, 0:ntt], in_=sqb.rearrange("p (t c) -> p t c", c=128), axis=AX.X, op=ALU.add),
                                   r=[sqb_tok], w=[stt_tok])
                            Sc.dve(lambda e: e.tensor_scalar(out=stt[:, 4:4 + ntt], in0=stt[:, 0:ntt], scalar1=1e-6, scalar2=None, op0=ALU.add),
                                   r=[stt_tok], w=[stt_tok])
                            Sc.act(lambda e: e.activation(out=stt[:, 4:4 + ntt], in_=stt[:, 4:4 + ntt], func=AF.Sqrt), r=[stt_tok], w=[stt_tok])
                            Sc.dve(lambda e: e.reciprocal(out=stt[:, 4:4 + ntt], in_=stt[:, 4:4 + ntt]), r=[stt_tok], w=[stt_tok])
                            for tt in range(ntt):
                                Sc.dve(lambda e, ob=ob, tt=tt, xb_=xb_: e.tensor_scalar(out=ob[:, tt * 128:(tt + 1) * 128],
                                       in0=bank_f32(xb_)[:, tt * 128:(tt + 1) * 128], scalar1=stt[:, 4 + tt:5 + tt], scalar2=None, op0=ALU.mult),
                                       r=[("ps", xb_), stt_tok], w=[ob_tok])
                        else:
                            Sc.act(lambda e, ob=ob, xb_=xb_: e.activation(out=ob, in_=bank_f32(xb_)[:, 0:TG], func=AF.Copy), r=[("ps", xb_)], w=[ob_tok])
                        dst = (q_tm, k_tm, v_tm)[which]
                        Sc.dma("sp", dst[tg * TG:(tg + 1) * TG, head * 128:(head + 1) * 128].rearrange("(t p) c -> p t c", p=128),
                               ob.rearrange("p (t c) -> p t c", c=128), r=[ob_tok])
                    stage2.append(st2)
                return (lambda: [f() for f in stage2])
            elif c0 < 4 * D:
                ncols = sum(n for _, n in segs)
                for tt, b in enumerate(accs):
                    i2 = cnt[0] % 2; cnt[0] += 1
                    zb, zb_tok = zbufs[i2]
                    Sc.act(lambda e, zb=zb, b=b: e.activation(out=zb[:, 0:ncols], in_=bank_f32(b)[:, 0:ncols], func=AF.Copy), r=[("ps", b)], w=[zb_tok])
                    r0 = tg * TG + tt * 128
                    Sc.dma("sp", z_tm[r0:r0 + 128, c0 - 3 * D:c0 - 3 * D + ncols], zb[:, 0:ncols], r=[zb_tok])
            else:
                for tt, b in enumerate(accs):
                    i2 = cnt[0] % 2; cnt[0] += 1
                    (gb, gb_tok), (gt, gt_tok), (gT, gT_tok) = gbs[i2], gtmp[i2], gbTs[i2]
                    pb = bank_f32(b)
                    Sc.act(lambda e, gb=gb, pb=pb: e.activation(out=gb[:, H:2 * H], in_=pb[:, 0:H], func=AF.Sigmoid), r=[("ps", b)], w=[gb_tok])
                    Sc.dve(lambda e, gt=gt, pb=pb: e.tensor_tensor(out=gt[:, 0:H], in0=pb[:, H:2 * H], in1=dtb, op=ALU.add), r=[("ps", b), dtb_tok], w=[gt_tok])
                    Sc.act(lambda e, gt=gt: e.activation(out=gt[:, 0:H], in_=gt[:, 0:H], func=AF.Exp), r=[gt_tok], w=[gt_tok])
                    Sc.dve(lambda e, gt=gt: e.tensor_scalar(out=gt[:, 0:H], in0=gt[:, 0:H], scalar1=1.0, scalar2=None, op0=ALU.add), r=[gt_tok], w=[gt_tok])
                    Sc.act(lambda e, gt=gt: e.activation(out=gt[:, 0:H], in_=gt[:, 0:H], func=AF.Ln), r=[gt_tok], w=[gt_tok])
                    Sc.dve(lambda e, gt=gt: e.tensor_tensor(out=gt[:, H:2 * H], in0=gt[:, 0:H], in1=negA, op=ALU.mult), r=[gt_tok, negA_tok], w=[gt_tok])
                    xb_ = next_aux()
                    Sc.pe(lambda e, gt=gt, xb_=xb_: e.matmul(out=bank_f32(xb_)[:, 0:H], lhsT=LCUM, rhs=gt[:, H:2 * H], start=True, stop=True),
                          r=[gt_tok, "cst"], w=[("ps", xb_)])
                    Sc.dve(lambda e, gb=gb, xb_=xb_: e.tensor_copy(out=gb[:, 0:H], in_=bank_f32(xb_)[:, 0:H]), r=[("ps", xb_)], w=[gb_tok])
                    r0 = tg * TG + tt * 128
                    Sc.dma("sp", gb_tm[r0:r0 + 128, :], gb, r=[gb_tok])
                    xc_ = next_aux()
                    Sc.pe(lambda e, gb=gb, xc_=xc_: e.transpose(out=bank_f32(xc_)[0:2 * H, 0:128], in_=gb, identity=IDENT), r=[gb_tok, "cst"], w=[("ps", xc_)])
                    Sc.act(lambda e, gT=gT, xc_=xc_: e.activation(out=gT[0:2 * H, :], in_=bank_f32(xc_)[0:2 * H, 0:128], func=AF.Copy), r=[("ps", xc_)], w=[gT_tok])
                    Sc.dma("sp", gbT[:, r0:r0 + 128], gT[0:2 * H, :], r=[gT_tok])

        n_fm = 3 * D // 512; n_z = D // 512
        for tg in range(ntg):
            prov(tg)
            linear_one_tg("a_w_in", range(0, n_fm), hT3, hT_tok, TG, "fm", epi, tg)
            linear_one_tg("a_w_in", range(n_fm, n_fm + n_z + 1), hT3, hT_tok, TG, "tm", epi, tg)

    WREG = {}
    conv_q = []

    def reg_weight(name, w_ap, K, blocks):
        kc_n = K // 128
        W2 = nc.dram_tensor("W2_" + name, [len(blocks), 128, kc_n, 512], BF16, kind="Internal").ap()
        WREG[name] = dict(K=K, blocks=blocks, W2=W2, name=name)
        for b, segs in enumerate(blocks):
            off = 0
            for (c0, n) in segs:
                conv_q.append((W2[b, :, :, off:off + n], w_ap[:, c0:c0 + n].rearrange("(k p) c -> p k c", p=128), ("W2", name, b)))
                off += n

    def issue_conv(n):
        for _ in range(n):
            if not conv_q:
                return
            o_, i_, tok = conv_q.pop(0)
            Sc.dma("pool", o_, i_, w=[tok], bg=True, max_dma_last_dim=2048)
            conv_last[tok] = len(conv_q)

    conv_last = {}
    conv_need = {}

    def ensure_conv(tok):
        while any(t == tok for (_, _, t) in conv_q[:4]) or (tok not in conv_last):
            if not conv_q:
                break
            issue_conv(1)

    lin_state = {"pending": []}
    NSLV = [3]
    KS = 16

    def flush_pending():
        p = lin_state["pending"]
        lin_state["pending"] = []
        for f in p:
            f()

    def linear_one_tg(wname, blk_idx, hT3, hT_tok, TG, mode, epilogue, tg, pre_block=None):
        NSL = NSLV[0]
        hT_tokf = hT_tok if callable(hT_tok) else (lambda ks_: hT_tok)
        wr = WREG[wname]
        K = wr["K"]; W2 = wr["W2"]
        kc_n = K // 128
        nks = (kc_n + KS - 1) // KS
        key = AR.gen
        if lin_state.get("gen") != key:
            slots = []
            for i in range(NSL):
                a, tok = AR.bf16(KS * 512, "wslot%d" % i)
                slots.append((a.rearrange("p (k c) -> p k c", c=512), tok))
            lin_state["gen"] = key; lin_state["slots"] = slots; lin_state["cnt"] = 0
        slots = lin_state["slots"]
        ntt = TG // 128
        for bi_ in blk_idx:
            segs = wr["blocks"][bi_]
            ncols = sum(n for _, n in segs)
            nacc = (ncols + 127) // 128 if mode == "fm" else ntt
            accs = [next_acc() for _ in range(nacc)]
            if pre_block is not None:
                pre_block(tg, bi_)
            for ks in range(nks):
                k0 = ks * KS; nk = min(KS, kc_n - k0)
                sl, sl_tok = slots[lin_state["cnt"] % NSL]; lin_state["cnt"] += 1
                hT_tok_ = hT_tokf(ks)
                ensure_conv(("W2", wname, bi_))
                Sc.dma("pool", sl[:, 0:nk, 0:ncols], W2[bi_, :, k0:k0 + nk, 0:ncols], r=[("W2", wname, bi_)], w=[sl_tok])
                issue_conv(1)
                for kk in range(nk):
                    first = (ks == 0 and kk == 0); last = (ks == nks - 1 and kk == nk - 1)
                    for ai in range(nacc):
                        if mode == "fm":
                            cw = min(128, ncols - ai * 128)
                            Sc.pe(lambda e, ai=ai, kk=kk, k0=k0, sl=sl, cw=cw, first=first, last=last, b=accs[ai]: e.matmul(
                                out=bank_f32(b)[0:cw, 0:TG], lhsT=sl[:, kk, ai * 128:ai * 128 + cw], rhs=hT3[:, k0 + kk, :],
                                start=first, stop=last), r=[sl_tok, hT_tok_], w=[("ps", accs[ai])])
                        else:
                            Sc.pe(lambda e, ai=ai, kk=kk, k0=k0, sl=sl, ncols=ncols, first=first, last=last, b=accs[ai]: e.matmul(
                                out=bank_f32(b)[:, 0:ncols], lhsT=hT3[:, k0 + kk, ai * 128:(ai + 1) * 128], rhs=sl[:, kk, 0:ncols],
                                start=first, stop=last), r=[sl_tok, hT_tok_], w=[("ps", accs[ai])])
            flush_pending()
            d_ = epilogue(tg, bi_, segs, accs)
            if d_ is not None:
                lin_state["pending"].append(d_)

    def phase_B():
        AR.reset()
        scale = 128.0 ** -0.5
        gbcol, gbcol_tok = AR.f32(NT * 2 * H, "gbcol")
        gbcol3 = gbcol.rearrange("p (t c) -> p t c", c=2 * H)
        Sc.dma("sp", gbcol3, gb_tm.rearrange("(t p) c -> p t c", p=128), w=[gbcol_tok])
        bgcol, bgcol_tok = AR.f32(NT * H, "bgcol")
        bgcol3 = bgcol.rearrange("p (t c) -> p t c", c=H)
        Sc.act(lambda e: e.activation(out=bgcol3, in_=gbcol3[:, :, 0:H], func=AF.Exp), r=[gbcol_tok], w=[bgcol_tok])
        Sc.dve(lambda e: e.tensor_tensor(out=bgcol3, in0=bgcol3, in1=gbcol3[:, :, H:2 * H], op=ALU.mult), r=[bgcol_tok, gbcol_tok], w=[bgcol_tok])
        G = 8 if H >= 8 else (4 if H >= 4 else 1)
        streams = []
        for gi in range(G):
            st = {}
            def T(n, name, gi=gi):
                return AR.f32(n, "%s_%d" % (name, gi))
            def TB(n, name, gi=gi):
                return AR.bf16(n, "%s_%d" % (name, gi))
            st["dl"] = T(2, "dl")
            for nm in ("Gr0", "Gr1", "Br0", "Br1", "q0", "k0", "v0", "q1", "k1", "v1", "DT", "DTs", "X", "A", "R", "P", "Q", "P2", "Q2",
                       "u", "EG", "osb0", "osb1", "Sst"):
                st[nm] = T(128, nm)
            for nm in ("kT", "qT", "AcT", "vb", "kbg", "wT", "qdT", "kdA", "kdB", "vnew", "Sb", "Rb"):
                st[nm] = TB(128, nm)
            streams.append(st)
        F32R = mybir.dt.float32r
        def fr(ap):
            return ap.bitcast(F32R) if INV_F32R else ap
        bank_rot = [0]
        def nb():
            i = bank_rot[0] % 8; bank_rot[0] += 1; return i

        def head_gen(st, h):
            dl, dl_tok = st["dl"]
            Sst, Sst_tok = st["Sst"]
            Sc.dve(lambda e: e.memset(Sst, 0.0), w=[Sst_tok])
            Sb, Sb_tok = st["Sb"]
            Sc.dve(lambda e: e.memset(Sb, 0.0), w=[Sb_tok])
            vnew, vnew_tok = st["vnew"]
            Sc.dve(lambda e: e.memset(vnew, 0.0), w=[vnew_tok])
            kdA, kdA_tok = st["kdA"]; kdB, kdB_tok = st["kdB"]
            Sc.dve(lambda e: e.memset(kdA, 0.0), w=[kdA_tok])
            Sc.dve(lambda e: e.memset(kdB, 0.0), w=[kdB_tok])
            yield
            for t in range(NT):
                par = t % 2
                (qt, qt_tok), (kt, kt_tok), (vt, vt_tok) = st["q%d" % par], st["k%d" % par], st["v%d" % par]
                r0 = t * 128
                hs = slice(h * 128, (h + 1) * 128)
                Sc.dma("sp", qt, q_tm[r0:r0 + 128, hs], w=[qt_tok])
                Sc.dma("sp", kt, k_tm[r0:r0 + 128, hs], w=[kt_tok])
                Sc.dma("sp", vt, v_tm[r0:r0 + 128, hs], w=[vt_tok])
                gcol = gbcol3[:, t, h:h + 1]; bcol = gbcol3[:, t, H + h:H + h + 1]; bgc = bgcol3[:, t, h:h + 1]
                Gr, Grow_tok = st["Gr%d" % par]; Br, Brow_tok = st["Br%d" % par]
                Sc.dma("sp", Gr, gbT[h:h + 1, r0:r0 + 128].partition_broadcast(128).rearrange("p o n -> p (o n)"), w=[Grow_tok])
                Sc.dma("sp", Br, gbT[H + h:H + h + 1, r0:r0 + 128].partition_broadcast(128).rearrange("p o n -> p (o n)"), w=[Brow_tok])
                Sc.dve(lambda e, Gr=Gr, t=t: e.tensor_tensor(out=dl[0:64, 0:1], in0=Gr[0:64, 63:64], in1=gbcol3[0:64, t, h:h + 1], op=ALU.subtract),
                       r=[Grow_tok, gbcol_tok], w=[dl_tok])
                Sc.dve(lambda e, Gr=Gr, t=t: e.tensor_tensor(out=dl[64:128, 0:1], in0=Gr[64:128, 127:128], in1=gbcol3[64:128, t, h:h + 1], op=ALU.subtract),
                       r=[Grow_tok, gbcol_tok], w=[dl_tok])
                Sc.act(lambda e: e.activation(out=dl[:, 0:1], in_=dl[:, 0:1], func=AF.Exp), r=[dl_tok], w=[dl_tok])
                kT, kT_tok = st["kT"]; qT, qT_tok = st["qT"]
                b1 = nb()
                Sc.pe(lambda e, b1=b1: e.transpose(out=bank_f32(b1)[:, 0:128], in_=kt, identity=IDENT), r=[kt_tok, "cst"], w=[("ps", b1)])
                Sc.act(lambda e, b1=b1: e.activation(out=kT, in_=bank_f32(b1)[:, 0:128], func=AF.Copy), r=[("ps", b1)], w=[kT_tok])
                b2 = nb()
                Sc.pe(lambda e, b2=b2: e.transpose(out=bank_f32(b2)[:, 0:128], in_=qt, identity=IDENT), r=[qt_tok, "cst"], w=[("ps", b2)])
                Sc.dve(lambda e, b2=b2: e.tensor_copy(out=qT, in_=bank_f32(b2)[:, 0:128]), r=[("ps", b2)], w=[qT_tok])
                yield
                DT, DT_tok = st["DT"]; DTs, DTs_tok = st["DTs"]; EG, EG_tok = st["EG"]
                Sc.dve(lambda e: e.scalar_tensor_tensor(out=DT, in0=Gr, scalar=gcol, in1=MASKNEG, op0=ALU.subtract, op1=ALU.add),
                       r=[Grow_tok, gbcol_tok, "cst"], w=[DT_tok])
                Sc.act(lambda e: e.activation(out=DT, in_=DT, func=AF.Exp), r=[DT_tok], w=[DT_tok])
                Sc.act(lambda e: e.activation(out=EG, in_=Gr, func=AF.Exp), r=[Grow_tok], w=[EG_tok])
                Sc.dve(lambda e: e.tensor_tensor(out=DTs, in0=DT, in1=STRICT, op=ALU.mult), r=[DT_tok, "cst"], w=[DTs_tok])
                bkk = nb()
                Sc.pe(lambda e, bkk=bkk: e.matmul(out=bank_f32(bkk)[:, 0:128], lhsT=kT, rhs=kT, start=True, stop=True), r=[kT_tok], w=[("ps", bkk)])
                bqk = nb()
                Sc.pe(lambda e, bqk=bqk: e.matmul(out=bank_f32(bqk)[:, 0:128], lhsT=kT, rhs=qT, start=True, stop=True), r=[kT_tok, qT_tok], w=[("ps", bqk)])
                X, X_tok = st["X"]; A, A_tok = st["A"]; R, R_tok = st["R"]; AcT, AcT_tok = st["AcT"]
                Sc.dve(lambda e, bkk=bkk: e.tensor_tensor(out=X, in0=bank_f32(bkk)[:, 0:128], in1=Br, op=ALU.mult), r=[("ps", bkk), Brow_tok], w=[X_tok])
                Sc.dve(lambda e: e.tensor_tensor(out=X, in0=X, in1=DTs, op=ALU.mult), r=[X_tok, DTs_tok], w=[X_tok])
                Sc.dve(lambda e, bqk=bqk: e.scalar_tensor_tensor(out=AcT, in0=bank_f32(bqk)[:, 0:128], scalar=scale, in1=DT, op0=ALU.mult, op1=ALU.mult),
                       r=[("ps", bqk), DT_tok], w=[AcT_tok])
                yield
                ba = nb()
                Sc.pe(lambda e, ba=ba: e.transpose(out=bank_f32(ba)[:, 0:128], in_=X, identity=IDENT), r=[X_tok, "cst"], w=[("ps", ba)])
                Sc.act(lambda e, ba=ba: e.activation(out=A, in_=bank_f32(ba)[:, 0:128], func=AF.Copy), r=[("ps", ba)], w=[A_tok])
                Sc.dve(lambda e: e.tensor_tensor(out=R, in0=IDENT, in1=X, op=ALU.subtract), r=["cst", X_tok], w=[R_tok])
                Pc, Pc_tok = X, X_tok
                Qc, Qc_tok = A, A_tok
                pq = [(st["P"], st["Q"]), (st["P2"], st["Q2"])]
                for kstage in range(1, 6):
                    (Pn, Pn_tok), (Qn, Qn_tok) = pq[kstage % 2]
                    bq = nb()
                    Sc.pe(lambda e, bq=bq, Pc=Pc, Qc=Qc: e.matmul(out=bank_f32(bq)[:, 0:128], lhsT=fr(Pc), rhs=fr(Qc), start=True, stop=True),
                          r=[Pc_tok, Qc_tok], w=[("ps", bq)])
                    Sc.dve(lambda e, bq=bq, Qn=Qn: e.tensor_copy(out=Qn, in_=bank_f32(bq)[:, 0:128]), r=[("ps", bq)], w=[Qn_tok])
                    if kstage < 5:
                        bp = nb()
                        Sc.pe(lambda e, bp=bp, Pc=Pc, Qc=Qc: e.matmul(out=bank_f32(bp)[:, 0:128], lhsT=fr(Qc), rhs=fr(Pc), start=True, stop=True),
                              r=[Pc_tok, Qc_tok], w=[("ps", bp)])
                        Sc.act(lambda e, bp=bp, Pn=Pn: e.activation(out=Pn, in_=bank_f32(bp)[:, 0:128], func=AF.Copy), r=[("ps", bp)], w=[Pn_tok])
                    bm = nb()
                    Sc.pe(lambda e, bm=bm, Qn=Qn: e.matmul(out=bank_f32(bm)[:, 0:128], lhsT=fr(Qn), rhs=fr(R), start=True, stop=True),
                          r=[Qn_tok, R_tok], w=[("ps", bm)])
                    Sc.dve(lambda e, bm=bm: e.tensor_tensor(out=R, in0=R, in1=bank_f32(bm)[:, 0:128], op=ALU.add), r=[R_tok, ("ps", bm)], w=[R_tok])
                    Pc, Pc_tok, Qc, Qc_tok = Pn, Pn_tok, Qn, Qn_tok
                    yield
                vb, vb_tok = st["vb"]; kbg, kbg_tok = st["kbg"]; u, u_tok = st["u"]; wT, wT_tok = st["wT"]; qdT, qdT_tok = st["qdT"]
                Sc.dve(lambda e: e.tensor_scalar(out=vb, in0=vt, scalar1=bcol, scalar2=None, op0=ALU.mult), r=[vt_tok, gbcol_tok], w=[vb_tok])
                Sc.dve(lambda e: e.tensor_scalar(out=kbg, in0=kt, scalar1=bgc, scalar2=None, op0=ALU.mult), r=[kt_tok, bgcol_tok], w=[kbg_tok])
                Rb, Rb_tok = st["Rb"]
                Sc.act(lambda e: e.activation(out=Rb, in_=R, func=AF.Copy), r=[R_tok], w=[Rb_tok])
                bu = nb()
                Sc.pe(lambda e, bu=bu: e.matmul(out=bank_f32(bu)[:, 0:128], lhsT=Rb, rhs=vb, start=True, stop=True), r=[Rb_tok, vb_tok], w=[("ps", bu)])
                Sc.act(lambda e, bu=bu: e.activation(out=u, in_=bank_f32(bu)[:, 0:128], func=AF.Copy), r=[("ps", bu)], w=[u_tok])
                bw = nb()
                Sc.pe(lambda e, bw=bw: e.matmul(out=bank_f32(bw)[:, 0:128], lhsT=kbg, rhs=Rb, start=True, stop=True), r=[Rb_tok, kbg_tok], w=[("ps", bw)])
                Sc.dve(lambda e, bw=bw: e.tensor_copy(out=wT, in_=bank_f32(bw)[:, 0:128]), r=[("ps", bw)], w=[wT_tok])
                Sc.dve(lambda e: e.scalar_tensor_tensor(out=qdT, in0=qT, scalar=scale, in1=EG, op0=ALU.mult, op1=ALU.mult), r=[qT_tok, EG_tok], w=[qdT_tok])
                Sc.dve(lambda e: e.tensor_scalar(out=kdA[0:64, :], in0=kt[0:64, :], scalar1=dl[0:64, 0:1], scalar2=None, op0=ALU.mult),
                       r=[kt_tok, dl_tok], w=[kdA_tok])
                Sc.dve(lambda e: e.tensor_scalar(out=kdB[64:128, :], in0=kt[64:128, :], scalar1=dl[64:128, 0:1], scalar2=None, op0=ALU.mult),
                       r=[kt_tok, dl_tok], w=[kdB_tok])
                yield
                osb, osb_tok = st["osb%d" % par]
                for c in range(2):
                    rs_ = slice(64 * c, 64 * c + 64)
                    kd, kd_tok = (kdA, kdA_tok) if c == 0 else (kdB, kdB_tok)
                    bws = nb()
                    Sc.pe(lambda e, bws=bws: e.matmul(out=bank_f32(bws)[:, 0:128], lhsT=wT, rhs=Sb, start=True, stop=True), r=[wT_tok, Sb_tok], w=[("ps", bws)])
                    Sc.dve(lambda e, bws=bws, rs_=rs_: e.tensor_tensor(out=vnew[rs_, :], in0=u[rs_, :], in1=bank_f32(bws)[rs_, 0:128], op=ALU.subtract),
                           r=[u_tok, ("ps", bws)], w=[vnew_tok])
                    bo = nb()
                    Sc.pe(lambda e, bo=bo: e.matmul(out=bank_f32(bo)[:, 0:128], lhsT=qdT, rhs=Sb, start=True, stop=False), r=[qdT_tok, Sb_tok], w=[("ps", bo)])
                    Sc.pe(lambda e, bo=bo: e.matmul(out=bank_f32(bo)[:, 0:128], lhsT=AcT, rhs=vnew, start=False, stop=True), r=[AcT_tok, vnew_tok], w=[("ps", bo)])
                    Sc.act(lambda e, bo=bo, rs_=rs_: e.activation(out=osb[rs_, :], in_=bank_f32(bo)[rs_, 0:128], func=AF.Copy), r=[("ps", bo)], w=[osb_tok])
                    bs = nb()
                    Sc.pe(lambda e, bs=bs, kd=kd: e.matmul(out=bank_f32(bs)[:, 0:128], lhsT=kd, rhs=vnew, start=True, stop=True), r=[kd_tok, vnew_tok], w=[("ps", bs)])
                    sdcol = EG[:, 64 * c + 63:64 * c + 64]
                    Sc.dve(lambda e, bs=bs, sdcol=sdcol: e.scalar_tensor_tensor(out=Sb, in0=Sst, scalar=sdcol, in1=bank_f32(bs)[:, 0:128], op0=ALU.mult, op1=ALU.add),
                           r=[Sst_tok, EG_tok, ("ps", bs)], w=[Sb_tok])
                    Sc.dve(lambda e, bs=bs, sdcol=sdcol: e.scalar_tensor_tensor(out=Sst, in0=Sst, scalar=sdcol, in1=bank_f32(bs)[:, 0:128], op0=ALU.mult, op1=ALU.add),
                           r=[Sst_tok, EG_tok, ("ps", bs)], w=[Sst_tok])
                    yield
                Sc.dma("sp", o_gdn[r0:r0 + 128, hs], osb, r=[osb_tok])

        for h0 in range(0, H, G):
            gens = [head_gen(streams[gi], h0 + gi) for gi in range(min(G, H - h0))]
            alive = list(gens)
            while alive:
                nxt = []
                for g in alive:
                    try:
                        next(g); nxt.append(g)
                    except StopIteration:
                        pass
                alive = nxt

    def resid_linear(wname, prov, hT3, hT_tok, TG, resid_src, dst, ntg):
        NRB = 8
        rbufs = [AR.f32(512, "rbuf%d" % i) for i in range(NRB)]
        cnt = [0]
        cur = {}
        ntt_ = TG // 128

        def pre(tg, bi_):
            segs = WREG[wname]["blocks"][bi_]
            c0 = segs[0][0]; ncols = segs[0][1]
            lst = []
            for tt in range(ntt_):
                i2 = cnt[0] % NRB; cnt[0] += 1
                rb, rb_tok = rbufs[i2]
                r0 = tg * TG + tt * 128
                Sc.dma("sp", rb[:, 0:ncols], resid_src[r0:r0 + 128, c0:c0 + ncols], w=[rb_tok])
                lst.append((rb, rb_tok))
            cur[(tg, bi_)] = lst

        def epi(tg, bi_, segs, accs):
            c0 = segs[0][0]; ncols = segs[0][1]
            lst = cur.pop((tg, bi_))
            for tt, b in enumerate(accs):
                rb, rb_tok = lst[tt]
                r0 = tg * TG + tt * 128
                Sc.dve(lambda e, rb=rb, b=b: e.tensor_tensor(out=rb[:, 0:ncols], in0=rb[:, 0:ncols], in1=bank_f32(b)[:, 0:ncols], op=ALU.add),
                       r=[rb_tok, ("ps", b)], w=[rb_tok])
                Sc.dma("sp", dst[r0:r0 + 128, c0:c0 + ncols], rb[:, 0:ncols], r=[rb_tok])
        nb_ = len(WREG[wname]["blocks"])
        for tg in range(ntg):
            prov(tg)
            linear_one_tg(wname, range(nb_), hT3, hT_tok, TG, "tm", epi, tg, pre_block=pre)

    def phase_C():
        AR.reset()
        NSLV[0] = 2
        TG = 512 if S_ >= 512 else S_
        prol = rms_prologue_factory(a_out_norm_t, D, group=128)
        prov, hT3, hT_tok = make_prov_tm(o_gdn, D, TG, prol, extra_src=z_tm)
        resid_linear("a_w_out", prov, hT3, hT_tok, TG, x_in, x1, S_ // TG)

    def phase_FFN(layer, src, dst):
        AR.reset()
        NSLV[0] = 3
        TG = 512 if S_ >= 512 else S_
        ntg = S_ // TG
        prol = rms_prologue_factory(ffn_norm[layer:layer + 1, :], D)
        prov, hT3, hT_tok = make_prov_tm(src, D, TG, prol)
        sgb = [AR.f32(TG, "sg%d" % i) for i in range(2)]
        acb = [AR.bf16(TG, "ac%d" % i) for i in range(2)]
        cnt = [0]

        def epi(tg, bi_, segs, accs):
            f0 = segs[0][0]; nf = segs[0][1]
            nch = nf // 128
            for ci in range(nch):
                i2 = cnt[0] % 2; cnt[0] += 1
                (sg, sg_tok), (ac, ac_tok) = sgb[i2], acb[i2]
                bg_, bu_ = accs[ci], accs[nch + ci]
                Sc.act(lambda e, sg=sg, bg_=bg_: e.activation(out=sg, in_=bank_f32(bg_)[:, 0:TG], func=AF.Silu), r=[("ps", bg_)], w=[sg_tok])
                Sc.dve(lambda e, sg=sg, ac=ac, bu_=bu_: e.tensor_tensor(out=ac, in0=sg, in1=bank_f32(bu_)[:, 0:TG], op=ALU.mult),
                       r=[sg_tok, ("ps", bu_)], w=[ac_tok])
                fr = f0 + ci * 128
                Sc.dma("sp", actT[fr:fr + 128, tg * TG:(tg + 1) * TG], ac, r=[ac_tok])
        nb_ = len(WREG["gu%d" % layer]["blocks"])
        for tg in range(ntg):
            prov(tg)
            linear_one_tg("gu%d" % layer, range(nb_), hT3, hT_tok, TG, "fm", epi, tg)
        Sc.barrier()
        AR.reset()
        TG2 = 512 if S_ >= 512 else S_
        prov2, hT3b, hT_tokb = make_prov_fm(actT, F, TG2)
        resid_linear("dn%d" % layer, prov2, hT3b, hT_tokb, TG2, src, dst, S_ // TG2)

    def phase_E():
        TG = 512 if S_ >= 512 else S_
        ntg = S_ // TG

        def run(norm_row, wname, ncols_total, gain_row, dstT, do_v):
            AR.reset()
            prol = rms_prologue_factory(norm_row, D)
            prov, hT3, hT_tok = make_prov_tm(x2, D, TG, prol)
            gn, gn_tok = AR.f32(128, "gn")
            Sc.dma("sp", gn, bcast_row(gain_row, 128), w=[gn_tok])
            sqb, sqb_tok = AR.f32(512, "sqb")
            stt, stt_tok = AR.f32(16, "stt")
            NSLV[0] = 3
            knb = [AR.bf16(512, "kn%d" % i) for i in range(8)]
            kcnt = [0]
            kTb = [AR.bf16(512, "kTb%d" % i) for i in range(2)]
            vbb = [AR.bf16(512, "vbb%d" % i) for i in range(2)]
            cnt = [0]

            def epi(tg, bi_, segs, accs):
                c0 = segs[0][0]
                stage2 = []
                for tt, b in enumerate(accs):
                    i2 = cnt[0] % 2; cnt[0] += 1
                    r0 = tg * TG + tt * 128
                    pb = bank_f32(b)
                    if c0 < D:
                        kTs, kTs_tok = kTb[i2]
                        kn, kn_tok = knb[kcnt[0] % 8]; kcnt[0] += 1
                        Sc.act(lambda e, pb=pb: e.activation(out=sqb, in_=pb[:, 0:512], func=AF.Square), r=[("ps", b)], w=[sqb_tok])
                        Sc.dve(lambda e: e.tensor_reduce(out=stt[:, 0:4], in_=sqb.rearrange("p (g c) -> p g c", c=128), axis=AX.X, op=ALU.add), r=[sqb_tok], w=[stt_tok])
                        Sc.dve(lambda e: e.tensor_scalar(out=stt[:, 4:8], in0=stt[:, 0:4], scalar1=1.0 / 128, scalar2=RMS_EPS, op0=ALU.mult, op1=ALU.add), r=[stt_tok], w=[stt_tok])
                        Sc.act(lambda e: e.activation(out=stt[:, 4:8], in_=stt[:, 4:8], func=AF.Sqrt), r=[stt_tok], w=[stt_tok])
                        Sc.dve(lambda e: e.reciprocal(out=stt[:, 4:8], in_=stt[:, 4:8]), r=[stt_tok], w=[stt_tok])
                        for g in range(4):
                            Sc.dve(lambda e, kn=kn, pb=pb, g=g: e.scalar_tensor_tensor(out=kn[:, g * 128:(g + 1) * 128], in0=pb[:, g * 128:(g + 1) * 128],
                                   scalar=stt[:, 4 + g:5 + g], in1=gn, op0=ALU.mult, op1=ALU.mult), r=[("ps", b), stt_tok, gn_tok], w=[kn_tok])
                        def st2(kn=kn, kn_tok=kn_tok, kTs=kTs, kTs_tok=kTs_tok, c0=c0, r0=r0):
                            xb_ = next_aux()
                            for g in range(4):
                                Sc.pe(lambda e, kn=kn, g=g, xb_=xb_: e.transpose(out=bank_bf16(xb_)[:, g * 128:(g + 1) * 128], in_=kn[:, g * 128:(g + 1) * 128], identity=IDENTB),
                                      r=[kn_tok, "cstb"], w=[("ps", xb_)])
                            Sc.act(lambda e, kTs=kTs, xb_=xb_: e.activation(out=kTs, in_=bank_bf16(xb_)[:, 0:512], func=AF.Copy), r=[("ps", xb_)], w=[kTs_tok])
                            g0 = c0 // 128
                            Sc.dma("sp", dstT[g0:g0 + 4, :, r0:r0 + 128].rearrange("g p t -> p g t"), kTs.rearrange("p (g t) -> p g t", t=128), r=[kTs_tok])
                        stage2.append(st2)
                    else:
                        vbt, vbt_tok = vbb[i2]
                        Sc.act(lambda e, vbt=vbt, pb=pb: e.activation(out=vbt, in_=pb[:, 0:512], func=AF.Copy), r=[("ps", b)], w=[vbt_tok])
                        Sc.dma("sp", vb_d[r0:r0 + 128, c0 - D:c0 - D + 512], vbt, r=[vbt_tok])
                return (lambda: [f() for f in stage2]) if stage2 else None
            nb_ = len(WREG[wname]["blocks"])
            for tg in range(ntg):
                prov(tg)
                linear_one_tg(wname, range(nb_), hT3, hT_tok, TG, "tm", epi, tg)
            Sc.barrier()
        run(kv_norm, "w_kv", 2 * D, k_norm, kT_d, True)
        run(b_norm, "b_w_q", D, b_q_norm, qT_d, False)

    def phase_F(lam_init):
        AR.reset()
        scale = 128.0 ** -0.5
        QG = 512 if S_ >= 512 else S_
        nqg = S_ // QG
        ntq = QG // 128
        lp, lp_tok = AR.f32(512, "lp")
        Sc.dma("sp", lp, bcast_row(b_lambda, 512), w=[lp_tok])
        lt, lt_tok = AR.f32(272, "lt")
        Sc.dve(lambda e: e.tensor_tensor(out=lt[:, 0:128], in0=lp[:, 0:128], in1=lp[:, 128:256], op=ALU.mult), r=[lp_tok], w=[lt_tok])
        Sc.dve(lambda e: e.tensor_tensor(out=lt[:, 128:256], in0=lp[:, 256:384], in1=lp[:, 384:512], op=ALU.mult), r=[lp_tok], w=[lt_tok])
        Sc.dve(lambda e: e.tensor_reduce(out=lt[:, 256:258], in_=lt[:, 0:256].rearrange("p (g c) -> p g c", c=128), axis=AX.X, op=ALU.add), r=[lt_tok], w=[lt_tok])
        Sc.act(lambda e: e.activation(out=lt[:, 258:260], in_=lt[:, 256:258], func=AF.Exp), r=[lt_tok], w=[lt_tok])
        Sc.dve(lambda e: e.tensor_tensor(out=lt[:, 260:261], in0=lt[:, 259:260], in1=lt[:, 258:259], op=ALU.subtract), r=[lt_tok], w=[lt_tok])
        Sc.dve(lambda e: e.tensor_scalar(out=lt[:, 261:262], in0=lt[:, 260:261], scalar1=-lam_init, scalar2=None, op0=ALU.add), r=[lt_tok], w=[lt_tok])
        neglam = lt[:, 261:262]
        kTs = [AR.bf16(S_, "kTs%d" % m) for m in range(2)]
        qTs = [AR.bf16(S_, "qTs%d" % m) for m in range(2)]
        vx, vx_tok = AR.bf16(NT * 258, "vx")
        vx3 = vx.rearrange("p (t c) -> p t c", c=258)
        Sc.dve(lambda e: e.memset(vx, 1.0), w=[vx_tok])
        Eb = [AR.bf16(QG, "E%d" % i) for i in range(3)]
        Em = [AR.bf16(128, "Em%d" % i) for i in range(2)]
        AMB, AMB_tok = AR.bf16(128, "AMB")
        Sc.dve(lambda e: e.tensor_copy(out=AMB, in_=AMASK), r=["cst"], w=[AMB_tok])
        Osb = [[AR.f32(257, "O%d_%d" % (m, tt)) for tt in range(ntq)] for m in range(2)]
        outb = [AR.f32(256, "ob%d" % i) for i in range(2)]
        rcp, rcp_tok = AR.f32(8, "rcp")
        ecnt = [0]; ocnt = [0]
        for h in range(DH):
            for m in range(2):
                Sc.dma("sp", kTs[m][0], kT_d[2 * h + m], w=[kTs[m][1]])
                Sc.dma("sp", qTs[m][0], qT_d[2 * h + m], w=[qTs[m][1]])
            Sc.dma("sp", vx3[:, :, 0:256], vb_d[:, h * 256:(h + 1) * 256].rearrange("(t p) c -> p t c", p=128), w=[vx_tok])
            for qg in range(nqg):
                for m in range(2):
                    kTm, kT_tok = kTs[m]; qTm, qT_tok = qTs[m]
                    obanks = [0, 1, 2, 3][:ntq]
                    nkt = (qg + 1) * ntq
                    def issue_st(kt_):
                        sb_ = 4 + (ecnt[0] % 3)
                        E, E_tok = Eb[ecnt[0] % 3]; ecnt[0] += 1
                        Sc.pe(lambda e, sb_=sb_, kt_=kt_, kTm=kTm, qTm=qTm, qg=qg: e.matmul(out=bank_f32(sb_)[:, 0:QG], lhsT=kTm[:, kt_ * 128:(kt_ + 1) * 128],
                              rhs=qTm[:, qg * QG:(qg + 1) * QG], start=True, stop=True), r=[kT_tok, qT_tok], w=[("ps", sb_)])
                        Sc.act(lambda e, sb_=sb_, E=E: e.activation(out=E, in_=bank_f32(sb_)[:, 0:QG], func=AF.Exp, scale=scale), r=[("ps", sb_)], w=[E_tok])
                        return E, E_tok
                    pend = [issue_st(0)]
                    if nkt > 1:
                        pend.append(issue_st(1))
                    for kt_ in range(nkt):
                        E, E_tok = pend.pop(0)
                        if kt_ + 2 < nkt:
                            pend.append(issue_st(kt_ + 2))
                        for tt in range(ntq):
                            qt_ = qg * ntq + tt
                            if qt_ < kt_:
                                continue
                            lhs = E[:, tt * 128:(tt + 1) * 128]; lhs_tok = E_tok
                            if qt_ == kt_:
                                Emm, Emm_tok = Em[tt % 2]
                                Sc.dve(lambda e, Emm=Emm, lhs=lhs: e.tensor_tensor(out=Emm, in0=lhs, in1=AMB, op=ALU.mult), r=[E_tok, AMB_tok], w=[Emm_tok])
                                lhs, lhs_tok = Emm, Emm_tok
                            Sc.pe(lambda e, tt=tt, lhs=lhs, kt_=kt_, qt_=qt_: e.matmul(out=bank_f32(obanks[tt])[:, 0:257], lhsT=lhs, rhs=vx3[:, kt_, 0:257],
                                  start=(kt_ == 0), stop=(kt_ == qt_)), r=[lhs_tok, vx_tok], w=[("ps", obanks[tt])])
                    for tt in range(ntq):
                        O, O_tok = Osb[m][tt]
                        if tt % 2 == 0:
                            Sc.act(lambda e, O=O, tt=tt: e.activation(out=O, in_=bank_f32(obanks[tt])[:, 0:257], func=AF.Copy), r=[("ps", obanks[tt])], w=[O_tok])
                        else:
                            Sc.dve(lambda e, O=O, tt=tt: e.tensor_copy(out=O, in_=bank_f32(obanks[tt])[:, 0:257]), r=[("ps", obanks[tt])], w=[O_tok])
                for tt in range(ntq):
                    (O0, O0_tok), (O1, O1_tok) = Osb[0][tt], Osb[1][tt]
                    ob, ob_tok = outb[ocnt[0] % 2]; ocnt[0] += 1
                    Sc.dve(lambda e, O0=O0: e.reciprocal(out=rcp[:, 0:1], in_=O0[:, 256:257]), r=[O0_tok], w=[rcp_tok])
                    Sc.dve(lambda e, O1=O1: e.reciprocal(out=rcp[:, 1:2], in_=O1[:, 256:257]), r=[O1_tok], w=[rcp_tok])
                    Sc.dve(lambda e: e.tensor_tensor(out=rcp[:, 2:3], in0=rcp[:, 1:2], in1=neglam, op=ALU.mult), r=[rcp_tok, lt_tok], w=[rcp_tok])
                    Sc.dve(lambda e, ob=ob, O0=O0: e.tensor_scalar(out=ob, in0=O0[:, 0:256], scalar1=rcp[:, 0:1], scalar2=None, op0=ALU.mult), r=[O0_tok, rcp_tok], w=[ob_tok])
                    Sc.dve(lambda e, ob=ob, O1=O1: e.scalar_tensor_tensor(out=ob, in0=O1[:, 0:256], scalar=rcp[:, 2:3], in1=ob, op0=ALU.mult, op1=ALU.add),
                           r=[O1_tok, rcp_tok, ob_tok], w=[ob_tok])
                    r0 = qg * QG + tt * 128
                    Sc.dma("sp", o_att[r0:r0 + 128, h * 256:(h + 1) * 256], ob, r=[ob_tok])

    def phase_G(lam_init):
        AR.reset()
        NSLV[0] = 3
        TG = 512 if S_ >= 512 else S_
        prol = rms_prologue_factory(b_sub_norm_t, D, group=256, post_scale=(1.0 - lam_init))
        prov, hT3, hT_tok = make_prov_tm(o_att, D, TG, prol)
        resid_linear("b_w_out", prov, hT3, hT_tok, TG, x2, x3, S_ // TG)

    _orig_barrier = Sc.barrier
    def _barrier():
        flush_pending()
        _orig_barrier()
    Sc.barrier = _barrier
    lam_init = 0.8 - 0.6 * math.exp(-0.3 * 1)
    blkD = [[(c, 512)] for c in range(0, D, 512)]
    reg_weight("a_w_in", a_w_in, D, [[(c, 512)] for c in range(0, 4 * D, 512)] + [[(4 * D, 2 * H)]])
    reg_weight("a_w_out", a_w_out, D, blkD)
    reg_weight("gu0", ffn_w_gu[0], D, [[(f, 256), (F + f, 256)] for f in range(0, F, 256)])
    reg_weight("dn0", ffn_w_down[0], F, blkD)
    reg_weight("w_kv", w_kv, D, [[(c, 512)] for c in range(0, 2 * D, 512)])
    reg_weight("b_w_q", b_w_q, D, blkD)
    reg_weight("b_w_out", b_w_out, D, blkD)
    reg_weight("gu1", ffn_w_gu[1], D, [[(f, 256), (F + f, 256)] for f in range(0, F, 256)])
    reg_weight("dn1", ffn_w_down[1], F, blkD)
    issue_conv(len(WREG["a_w_in"]["blocks"]))
    phase_A(); Sc.barrier()
    phase_B(); Sc.barrier()
    phase_C(); Sc.barrier()
    phase_FFN(0, x1, x2); Sc.barrier()
    phase_E()
    phase_F(lam_init); Sc.barrier()
    phase_G(lam_init); Sc.barrier()
    phase_FFN(1, x3, out_d)
    Sc.finish()
    Sc.emit(nc, stack)
    stack.close()
    return nc


def prep_inputs(inputs, cfg, b):
    f = lambda a: np.ascontiguousarray(np.asarray(a, dtype=np.float32))
    H = cfg.H
    m = {
        "x": f(inputs["x"][b]),
        "a_norm": f(inputs["a_norm"]).reshape(1, -1),
        "a_w_in": f(inputs["a_w_in"][0]),
        "a_conv_t": f(np.asarray(inputs["a_conv"][0]).T),
        "a_A_log": f(inputs["a_A_log"]).reshape(1, -1),
        "a_dt_bias": f(inputs["a_dt_bias"]).reshape(1, -1),
        "a_out_norm_t": f(np.tile(np.asarray(inputs["a_out_norm"][0]), H)).reshape(1, -1),
        "a_w_out": f(inputs["a_w_out"][0]),
        "kv_norm": f(inputs["kv_norm"]).reshape(1, -1),
        "w_kv": f(inputs["w_kv"]),
        "k_norm": f(inputs["k_norm"]).reshape(1, -1),
        "b_norm": f(inputs["b_norm"]).reshape(1, -1),
        "b_w_q": f(inputs["b_w_q"][0]),
        "b_q_norm": f(inputs["b_q_norm"]).reshape(1, -1),
        "b_lambda": f(inputs["b_lambda"][0]).reshape(1, -1),
        "b_sub_norm_t": f(np.tile(np.asarray(inputs["b_sub_norm"][0]), cfg.DH)).reshape(1, -1),
        "b_w_out": f(inputs["b_w_out"][0]),
        "ffn_norm": f(inputs["ffn_norm"]),
        "ffn_w_gate_up": f(inputs["ffn_w_gate_up"]),
        "ffn_w_down": f(inputs["ffn_w_down"]),
        "consts": make_consts(),
    }
    return m


def run(inputs, cfg, debug=False, trace=False):
    nc = build(cfg, debug=debug)
    B = np.asarray(inputs["x"]).shape[0]
    in_maps = [prep_inputs(inputs, cfg, c) for c in range(B)]
    res = run_bass_kernel_spmd(nc, in_maps, core_ids=list(range(B)), **({"trace": True} if trace else {}))
    return res


def kernel(**inputs):
    cfg = Cfg(4096, 4096, 11008)
    res = run(inputs, cfg)
    B = np.asarray(inputs["x"]).shape[0]
    return np.stack([np.asarray(res.results[b]["out"], dtype=np.float32) for b in range(B)], axis=0)
```

```python
import math
import contextlib
import numpy as np
import concourse.bass as bass
import concourse.mybir as mybir
from concourse.bass_utils import run_bass_kernel_spmd

F32 = mybir.dt.float32
BF16 = mybir.dt.bfloat16
AF = mybir.ActivationFunctionType
ALU = mybir.AluOpType
AX = mybir.AxisListType

RMS_EPS = 1e-6
ARENA_WORDS = 41500
INV_F32R = False


class Cfg:
    def __init__(s, D, S, F):
        s.D = D; s.S = S; s.F = F
        s.H = D // 128; s.GIN = 4 * D + 2 * s.H; s.DH = D // 256
        s.NT = S // 128; s.KC = D // 128


class _Op:
    __slots__ = ("eng", "fn", "deps", "dma", "signals", "sem", "val", "prev")

    def __init__(s, eng, fn, deps, dma):
        s.eng = eng; s.fn = fn; s.deps = deps; s.dma = dma
        s.signals = False; s.sem = None; s.val = 0; s.prev = 0


class _Rec:
    def __init__(s):
        s.calls = []

    def __getattr__(s, name):
        def f(*a, **k):
            s.calls.append((name, a, k))
            return s
        return f


class Sched:
    CE = ("pe", "act", "dve", "pool")
    EPOCH = 12000
    NSLOT = 12
    USES = 1500

    def __init__(s):
        s.ops = []
        s.last_w = {}
        s.readers = {}
        s.barrier_deps = {}
        s.last_on = {}
        s.dmas_since = []

    def op(s, eng, fn, r=(), w=(), dma=False, bg=False):
        if fn is not None:
            rec = _Rec()
            fn(rec)
            assert len(rec.calls) == 1
            fn = rec.calls[0]
        i = len(s.ops)
        deps = {}
        for b in r:
            j = s.last_w.get(b)
            if j is not None:
                deps[j] = True
        for b in w:
            j = s.last_w.get(b)
            if j is not None and j not in deps:
                deps[j] = False
            for x in s.readers.get(b, ()):
                if x not in deps:
                    deps[x] = False
        bd = s.barrier_deps.pop(eng, None)
        if bd:
            for j in bd:
                deps[j] = True
        s.ops.append(_Op(eng, fn, deps, dma))
        for b in r:
            s.readers.setdefault(b, []).append(i)
        for b in w:
            s.last_w[b] = i
            s.readers[b] = []
        if dma:
            if not bg:
                s.dmas_since.append(i)
        else:
            s.last_on[eng] = i
        return i

    def pe(s, fn, r=(), w=()): return s.op("pe", fn, r, w)
    def act(s, fn, r=(), w=()): return s.op("act", fn, r, w)
    def dve(s, fn, r=(), w=()): return s.op("dve", fn, r, w)
    def pool(s, fn, r=(), w=()): return s.op("pool", fn, r, w)

    def dma(s, q, out, in_, r=(), w=(), bg=False, **kw):
        return s.op(q, lambda e: e.dma_start(out=out, in_=in_, **kw), r, w, dma=True, bg=bg)

    def barrier(s):
        deps = list(s.last_on.values()) + list(s.dmas_since)
        s.dmas_since = []
        s.last_w = {k: v for k, v in s.last_w.items() if isinstance(k, tuple) and k[0] == "W2"}
        s.readers = {}
        for e in ("pe", "act", "dve", "pool", "sp"):
            s.barrier_deps[e] = list(deps)

    def finish(s):
        s.barrier()
        s.op("sp", None)

    def emit(s, nc, stack):
        ops = s.ops
        for o in ops:
            keep = {}
            for j, raw in o.deps.items():
                p = ops[j]
                if p.dma or o.dma:
                    keep[j] = raw
                elif p.eng == o.eng:
                    if o.eng != "pe" and raw:
                        keep[j] = raw
                else:
                    keep[j] = raw
            o.deps = keep
            for j in keep:
                ops[j].signals = True
        cnt = {e: 0 for e in s.CE}
        dcnt = {"sp": 0, "pool": 0}
        for o in ops:
            if o.dma:
                n = dcnt[o.eng]; dcnt[o.eng] += 1
                slot = n % s.NSLOT; use = n // s.NSLOT
                o.sem = ("d", o.eng, (use // s.USES) * s.NSLOT + slot)
                o.prev = 16 * (use % s.USES)
                o.val = o.prev + 16
            elif o.signals:
                c = cnt[o.eng]; cnt[o.eng] += 1
                o.sem = ("c", o.eng, c // s.EPOCH)
                o.val = c % s.EPOCH + 1
        names = sorted({o.sem for o in ops if o.sem is not None}, key=str)
        sems = {}
        for k, nm in enumerate(names):
            sems[nm] = stack.enter_context(nc.semaphore("s%d" % k))
        assert len(sems) < 140, len(sems)
        per_eng = {e: [] for e in ("pe", "act", "dve", "pool", "sp")}
        for i, o in enumerate(ops):
            per_eng[o.eng].append(o)
        engmap = {"pe": "tensor", "act": "scalar", "dve": "vector", "pool": "gpsimd", "sp": "sync"}
        with nc.Block() as b0:
            def clr(g):
                for nm in names:
                    g.sem_clear(sems[nm])
            b0.gpsimd(clr)
        with nc.Block() as blk:
            for ename, lst in per_eng.items():
                if not lst:
                    continue

                def body(e, lst=lst):
                    known = {}
                    for o in lst:
                        for j in o.deps:
                            p = ops[j]
                            if known.get(p.sem, 0) < p.val:
                                e.wait_ge(sems[p.sem], p.val)
                                known[p.sem] = p.val
                        if o.dma and o.prev > 0 and known.get(o.sem, 0) < o.prev:
                            e.wait_ge(sems[o.sem], o.prev)
                            known[o.sem] = o.prev
                        if o.fn is None:
                            continue
                        nm_, a_, k_ = o.fn
                        inst = getattr(e, nm_)(*a_, **k_)
                        if o.dma:
                            inst.then_inc(sems[o.sem], 16)
                        elif o.signals:
                            inst.then_inc(sems[o.sem], 1)
                getattr(blk, engmap[ename])(body)


def make_consts():
    c = np.zeros((128, 6, 128), np.float32)
    i = np.arange(128)
    J, I = np.meshgrid(i, i, indexing="ij")
    same = (J // 64) == (I // 64)
    c[:, 0, :] = np.eye(128, dtype=np.float32)
    c[:, 1, :] = np.where(same & (I >= J), 0.0, -30000.0)
    c[:, 2, :] = np.where(same & (I > J), 1.0, 0.0)
    c[:, 3, :] = np.where(same & (J <= I), 1.0, 0.0)
    c[:, 4, :] = np.where((J >= 64) & (I < 64), 0.0, 1.0)
    c[:, 5, :] = 1.0
    return c.reshape(128, 6 * 128)


def build(cfg, debug=False):
    D, S_, F, H, DH, NT, KC, GIN = cfg.D, cfg.S, cfg.F, cfg.H, cfg.DH, cfg.NT, cfg.KC, cfg.GIN
    nc = bass.Bass("TRN2", target_bir_lowering=False)
    stack = contextlib.ExitStack()

    def din(name, shape, dt=F32):
        return nc.dram_tensor(name, list(shape), dt, kind="ExternalInput").ap()

    def dscr(name, shape, dt=F32):
        return nc.dram_tensor(name, list(shape), dt, kind="ExternalOutput" if debug else "Internal").ap()

    x_in = din("x", [S_, D])
    a_norm = din("a_norm", [1, D]); a_w_in = din("a_w_in", [D, GIN]); a_conv_t = din("a_conv_t", [3 * D, 4])
    a_A_log = din("a_A_log", [1, H]); a_dt_bias = din("a_dt_bias", [1, H])
    a_out_norm_t = din("a_out_norm_t", [1, D]); a_w_out = din("a_w_out", [D, D])
    kv_norm = din("kv_norm", [1, D]); w_kv = din("w_kv", [D, 2 * D]); k_norm = din("k_norm", [1, 128])
    b_norm = din("b_norm", [1, D]); b_w_q = din("b_w_q", [D, D]); b_q_norm = din("b_q_norm", [1, 128])
    b_lambda = din("b_lambda", [1, 4 * 128]); b_sub_norm_t = din("b_sub_norm_t", [1, D]); b_w_out = din("b_w_out", [D, D])
    ffn_norm = din("ffn_norm", [2, D]); ffn_w_gu = din("ffn_w_gate_up", [2, D, 2 * F]); ffn_w_down = din("ffn_w_down", [2, F, D])
    consts_d = din("consts", [128, 6 * 128])
    out_d = nc.dram_tensor("out", [S_, D], F32, kind="ExternalOutput").ap()

    q_tm = dscr("q_tm", [S_, D], BF16); k_tm = dscr("k_tm", [S_, D], BF16); v_tm = dscr("v_tm", [S_, D], BF16); z_tm = dscr("z_tm", [S_, D])
    gb_tm = dscr("gb_tm", [S_, 2 * H]); gbT = dscr("gbT", [2 * H, S_])
    o_gdn = dscr("o_gdn", [S_, D]); x1 = dscr("x1", [S_, D]); x2 = dscr("x2", [S_, D]); x3 = dscr("x3", [S_, D])
    actT = dscr("actT", [F, S_], BF16)
    kT_d = dscr("kT_d", [2 * DH, 128, S_], BF16); qT_d = dscr("qT_d", [2 * DH, 128, S_], BF16)
    vb_d = dscr("vb_d", [S_, D], BF16); o_att = dscr("o_att", [S_, D])

    arena_t = stack.enter_context(nc.sbuf_tensor("arena", [128, ARENA_WORDS], F32))
    cst_t = stack.enter_context(nc.sbuf_tensor("cst", [128, 6 * 128], F32))
    cstb_t = stack.enter_context(nc.sbuf_tensor("cstb", [128, 128], BF16))
    banks = [stack.enter_context(nc.psum_tensor("bank%d" % i, [128, 512], F32)) for i in range(8)]

    Sc = Sched()
    cst = cst_t[:]
    IDENT = cst[:, 0:128]; MASKNEG = cst[:, 128:256]; STRICT = cst[:, 256:384]
    LCUM = cst[:, 384:512]; AMASK = cst[:, 512:640]; ONES = cst[:, 640:768]
    IDENTB = cstb_t[:]
    Sc.dma("sp", cst, consts_d, w=["cst"])
    Sc.dve(lambda e: e.tensor_copy(out=IDENTB, in_=IDENT), r=["cst"], w=["cstb"])
    Sc.barrier()

    class Arena:
        def __init__(s): s.off = 0; s.gen = 0
        def reset(s): s.off = 0; s.gen += 1
        def f32(s, n, name):
            a = arena_t[:, s.off:s.off + n]; s.off += n
            assert s.off <= ARENA_WORDS, (name, s.off)
            return a, (name, s.gen)
        def bf16(s, n, name):
            w = (n + 1) // 2
            a = arena_t[:, s.off:s.off + w].bitcast(BF16)[:, 0:n]; s.off += w
            assert s.off <= ARENA_WORDS, (name, s.off)
            return a, (name, s.gen)
    AR = Arena()

    def bank_f32(i): return banks[i][:]
    def bank_bf16(i): return banks[i][:].bitcast(BF16)

    rot = {"acc": 0, "aux": 0}
    def next_acc():
        i = rot["acc"] % 6; rot["acc"] += 1; return i
    def next_aux():
        i = 6 + rot["aux"] % 2; rot["aux"] += 1; return i

    def bcast_row(ap_row, n):
        return ap_row.partition_broadcast(128).rearrange("p o n -> p (o n)")

    def make_prov_tm(src, K, TG, prologue, extra_src=None):
        kc_n = K // 128
        hT, hT_tok = AR.bf16(kc_n * TG, "hT")
        hT3 = hT.rearrange("p (k t) -> p k t", t=TG)
        xin, xin_tok = AR.f32(K, "xin")
        xb, xb_tok = AR.bf16(K, "xb")
        zin = None
        if extra_src is not None:
            zin, zin_tok = AR.f32(K, "zin")
        ntt = TG // 128

        def prov(tg):
            for tt in range(ntt):
                r0 = tg * TG + tt * 128
                Sc.dma("sp", xin, src[r0:r0 + 128, :], w=[xin_tok])
                rd = [xin_tok]
                if zin is not None:
                    Sc.dma("sp", zin, extra_src[r0:r0 + 128, :], w=[zin_tok])
                    rd.append(zin_tok)
                prologue(xin, zin, xb, rd, xb_tok)
                for k0 in range(0, kc_n, 4):
                    nk = min(4, kc_n - k0)
                    bi = next_aux()
                    pb = bank_bf16(bi)
                    for kk in range(nk):
                        Sc.pe(lambda e, kk=kk, k0=k0, pb=pb: e.transpose(
                            out=pb[:, kk * 128:(kk + 1) * 128], in_=xb[:, (k0 + kk) * 128:(k0 + kk + 1) * 128], identity=IDENTB),
                            r=[xb_tok, "cstb"], w=[("ps", bi)])
                    src_ap = pb[:, 0:nk * 128].rearrange("p (k t) -> p k t", t=128)
                    dst_ap = hT3[:, k0:k0 + nk, tt * 128:(tt + 1) * 128]
                    if (k0 // 4) % 2 == 0:
                        Sc.dve(lambda e, d=dst_ap, s_=src_ap: e.tensor_copy(out=d, in_=s_), r=[("ps", bi)], w=[hT_tok])
                    else:
                        Sc.act(lambda e, d=dst_ap, s_=src_ap: e.activation(out=d, in_=s_, func=AF.Copy), r=[("ps", bi)], w=[hT_tok])
        return prov, hT3, hT_tok

    def make_prov_fm(src, K, TG):
        kc_n = K // 128
        hT, hT_tok = AR.bf16(kc_n * TG, "hT")
        hT3 = hT.rearrange("p (k t) -> p k t", t=TG)

        def prov(tg):
            step = 16
            for k0 in range(0, kc_n, step):
                nk = min(step, kc_n - k0)
                Sc.dma("sp", hT3[:, k0:k0 + nk, :],
                       src[k0 * 128:(k0 + nk) * 128, tg * TG:(tg + 1) * TG].rearrange("(k p) t -> p k t", p=128),
                       w=[(hT_tok, k0 // step)])
        return prov, hT3, (lambda ks_: (hT_tok, ks_))

    def rms_prologue_factory(gain_row_ap, K, group=None, post_scale=1.0):
        gain_b, gain_tok = AR.f32(K, "gain")
        sq, sq_tok = AR.f32(K, "sq")
        ng = 1 if group is None else K // group
        st, st_tok = AR.f32(4 * ng + 4, "stat")
        Sc.dma("sp", gain_b, bcast_row(gain_row_ap, K), w=[gain_tok])
        gsz = K if group is None else group

        def prologue(xin, zin, xb, rd, xb_tok):
            Sc.act(lambda e: e.activation(out=sq, in_=xin, func=AF.Square), r=rd[:1], w=[sq_tok])
            ss = st[:, 0:ng]; rs = st[:, ng:2 * ng]
            Sc.dve(lambda e: e.tensor_reduce(out=ss, in_=sq.rearrange("p (g c) -> p g c", c=gsz), axis=AX.X, op=ALU.add),
                   r=[sq_tok], w=[st_tok])
            ps2 = post_scale * post_scale
            Sc.dve(lambda e: e.tensor_scalar(out=rs, in0=ss, scalar1=1.0 / (gsz * ps2), scalar2=RMS_EPS / ps2, op0=ALU.mult, op1=ALU.add),
                   r=[st_tok], w=[st_tok])
            Sc.act(lambda e: e.activation(out=rs, in_=rs, func=AF.Sqrt), r=[st_tok], w=[st_tok])
            Sc.dve(lambda e: e.reciprocal(out=rs, in_=rs), r=[st_tok], w=[st_tok])
            if group is None:
                Sc.dve(lambda e: e.scalar_tensor_tensor(out=xb, in0=xin, scalar=rs[:, 0:1], in1=gain_b, op0=ALU.mult, op1=ALU.mult),
                       r=[rd[0], st_tok, gain_tok], w=[xb_tok])
            else:
                if zin is not None:
                    Sc.act(lambda e: e.activation(out=zin, in_=zin, func=AF.Silu), r=[rd[1]], w=[rd[1]])
                x3_ = xin.rearrange("p (g c) -> p g c", c=gsz)
                rsb = rs.unsqueeze(2).broadcast_to([128, ng, gsz])
                sq3 = sq.rearrange("p (g c) -> p g c", c=gsz)
                Sc.dve(lambda e: e.tensor_tensor(out=sq3, in0=x3_, in1=rsb, op=ALU.mult), r=[rd[0], st_tok, sq_tok], w=[sq_tok])
                if zin is not None:
                    Sc.dve(lambda e: e.tensor_tensor(out=sq, in0=sq, in1=gain_b, op=ALU.mult), r=[sq_tok, gain_tok], w=[sq_tok])
                    Sc.dve(lambda e: e.tensor_tensor(out=xb, in0=sq, in1=zin, op=ALU.mult), r=[sq_tok, rd[1]], w=[xb_tok])
                else:
                    Sc.dve(lambda e: e.tensor_tensor(out=xb, in0=sq, in1=gain_b, op=ALU.mult), r=[sq_tok, gain_tok], w=[xb_tok])
        return prologue

    def phase_A():
        AR.reset()
        TG = 512 if S_ >= 512 else S_
        ntg = S_ // TG
        prol = rms_prologue_factory(a_norm, D)
        prov, hT3, hT_tok = make_prov_tm(x_in, D, TG, prol)
        NCH = 3 * H
        cw_t, cw_tok = AR.f32(NCH * 4, "convw")
        cw3 = cw_t.rearrange("p (c k) -> p c k", k=4)
        Sc.dma("sp", cw3, a_conv_t.rearrange("(c p) k -> p c k", p=128), w=[cw_tok])
        halo, halo_tok = AR.f32(NCH * 3, "halo")
        halo3 = halo.rearrange("p (c k) -> p c k", k=3)
        Sc.dve(lambda e: e.memset(halo, 0.0), w=[halo_tok])
        NSLV[0] = 2
        pbufs = [AR.f32(3 + TG, "pbuf%d" % i) for i in range(2)]
        ybufs = [AR.bf16(TG, "ybuf%d" % i) for i in range(8)]
        yfs = [AR.f32(TG, "yf%d" % i) for i in range(2)]
        ycnt = [0]
        obufs = [AR.bf16(TG, "obuf%d" % i) for i in range(2)]
        sqb, sqb_tok = AR.f32(TG, "sqb")
        stt, stt_tok = AR.f32(16, "stt")
        zbufs = [AR.f32(512, "zbuf%d" % i) for i in range(2)]
        negA, negA_tok = AR.f32(H, "negA")
        dtb, dtb_tok = AR.f32(H, "dtb")
        Sc.dma("sp", negA, bcast_row(a_A_log, H), w=[negA_tok])
        Sc.dma("sp", dtb, bcast_row(a_dt_bias, H), w=[dtb_tok])
        Sc.act(lambda e: e.activation(out=negA, in_=negA, func=AF.Exp), r=[negA_tok], w=[negA_tok])
        Sc.dve(lambda e: e.tensor_scalar(out=negA, in0=negA, scalar1=-1.0, scalar2=None, op0=ALU.mult), r=[negA_tok], w=[negA_tok])
        gbs = [AR.f32(2 * H, "gb%d" % i) for i in range(2)]
        gtmp = [AR.f32(2 * H, "gtmp%d" % i) for i in range(2)]
        gbTs = [AR.f32(128, "gbTs%d" % i) for i in range(2)]
        cnt = [0]
        ntt = TG // 128

        def epi(tg, bi_, segs, accs):
            c0 = segs[0][0]
            if c0 < 3 * D:
                stage2 = []
                for ai, b in enumerate(accs):
                    ch = c0 // 128 + ai
                    which = ch // H; head = ch % H
                    i2 = cnt[0] % 2; cnt[0] += 1
                    (pb, pb_tok), (ob, ob_tok) = pbufs[i2], obufs[i2]
                    yb, yb_tok = ybufs[ycnt[0] % 8]; ycnt[0] += 1
                    Sc.dve(lambda e, pb=pb, ch=ch: e.tensor_copy(out=pb[:, 0:3], in_=halo3[:, ch, :]), r=[halo_tok], w=[pb_tok])
                    Sc.act(lambda e, pb=pb, b=b: e.activation(out=pb[:, 3:3 + TG], in_=bank_f32(b)[:, 0:TG], func=AF.Copy),
                           r=[("ps", b)], w=[pb_tok])
                    Sc.dve(lambda e, pb=pb, ch=ch: e.tensor_copy(out=halo3[:, ch, :], in_=pb[:, TG:TG + 3]), r=[pb_tok], w=[halo_tok])
                    yf, yf_tok = yfs[i2]
                    Sc.dve(lambda e, pb=pb, yf=yf, ch=ch: e.tensor_scalar(out=yf, in0=pb[:, 0:TG], scalar1=cw3[:, ch, 0:1], scalar2=None, op0=ALU.mult),
                           r=[pb_tok, cw_tok], w=[yf_tok])
                    for k in range(1, 4):
                        Sc.dve(lambda e, pb=pb, yf=yf, ch=ch, k=k: e.scalar_tensor_tensor(
                            out=yf, in0=pb[:, k:k + TG], scalar=cw3[:, ch, k:k + 1], in1=yf, op0=ALU.mult, op1=ALU.add),
                            r=[pb_tok, cw_tok, yf_tok], w=[yf_tok])
                    Sc.act(lambda e, yb=yb, yf=yf: e.activation(out=yb, in_=yf, func=AF.Silu), r=[yf_tok], w=[yb_tok])

                    def st2(yb=yb, yb_tok=yb_tok, ob=ob, ob_tok=ob_tok, which=which, head=head, tg=tg):
                        xb_ = next_aux()
                        for tt in range(ntt):
                            Sc.pe(lambda e, yb=yb, tt=tt, xb_=xb_: e.transpose(out=bank_bf16(xb_)[:, tt * 128:(tt + 1) * 128],
                                  in_=yb[:, tt * 128:(tt + 1) * 128], identity=IDENTB), r=[yb_tok, "cstb"], w=[("ps", xb_)])
                        if which < 2:
                            Sc.act(lambda e, xb_=xb_: e.activation(out=sqb, in_=bank_bf16(xb_)[:, 0:TG], func=AF.Square), r=[("ps", xb_)], w=[sqb_tok])
                            Sc.dve(lambda e: e.tensor_reduce(out=stt[:, 0:ntt], in_=sqb.rearrange("p (t c) -> p t c", c=128), axis=AX.X, op=ALU.add),
                                   r=[sqb_tok], w=[stt_tok])
                            Sc.dve(lambda e: e.tensor_scalar(out=stt[:, 4:4 + ntt], in0=stt[:, 0:ntt], scalar1=1e-6, scalar2=None, op0=ALU.add),
                                   r=[stt_tok], w=[stt_tok])
                            Sc.act(lambda e: e.activation(out=stt[:, 4:4 + ntt], in_=stt[:, 4:4 + ntt], func=AF.Sqrt), r=[stt_tok], w=[stt_tok])
                            Sc.dve(lambda e: e.reciprocal(out=stt[:, 4:4 + ntt], in_=stt[:, 4:4 + ntt]), r=[stt_tok], w=[stt_tok])
                            for tt in range(ntt):
                                Sc.dve(lambda e, ob=ob, tt=tt, xb_=xb_: e.tensor_scalar(out=ob[:, tt * 128:(tt + 1) * 128],
                                       in0=bank_bf16(xb_)[:, tt * 128:(tt + 1) * 128], scalar1=stt[:, 4 + tt:5 + tt], scalar2=None, op0=ALU.mult),
                                       r=[("ps", xb_), stt_tok], w=[ob_tok])
                        else:
                            Sc.act(lambda e, ob=ob, xb_=xb_: e.activation(out=ob, in_=bank_bf16(xb_)[:, 0:TG], func=AF.Copy), r=[("ps", xb_)], w=[ob_tok])
                        dst = (q_tm, k_tm, v_tm)[which]
                        Sc.dma("sp", dst[tg * TG:(tg + 1) * TG, head * 128:(head + 1) * 128].rearrange("(t p) c -> p t c", p=128),
                               ob.rearrange("p (t c) -> p t c", c=128), r=[ob_tok])
                    stage2.append(st2)
                return (lambda: [f() for f in stage2])
            elif c0 < 4 * D:
                ncols = sum(n for _, n in segs)
                for tt, b in enumerate(accs):
                    i2 = cnt[0] % 2; cnt[0] += 1
                    zb, zb_tok = zbufs[i2]
                    Sc.act(lambda e, zb=zb, b=b: e.activation(out=zb[:, 0:ncols], in_=bank_f32(b)[:, 0:ncols], func=AF.Copy), r=[("ps", b)], w=[zb_tok])
                    r0 = tg * TG + tt * 128
                    Sc.dma("sp", z_tm[r0:r0 + 128, c0 - 3 * D:c0 - 3 * D + ncols], zb[:, 0:ncols], r=[zb_tok])
            else:
                for tt, b in enumerate(accs):
                    i2 = cnt[0] % 2; cnt[0] += 1
                    (gb, gb_tok), (gt, gt_tok), (gT, gT_tok) = gbs[i2], gtmp[i2], gbTs[i2]
                    pb = bank_f32(b)
                    Sc.act(lambda e, gb=gb, pb=pb: e.activation(out=gb[:, H:2 * H], in_=pb[:, 0:H], func=AF.Sigmoid), r=[("ps", b)], w=[gb_tok])
                    Sc.dve(lambda e, gt=gt, pb=pb: e.tensor_tensor(out=gt[:, 0:H], in0=pb[:, H:2 * H], in1=dtb, op=ALU.add), r=[("ps", b), dtb_tok], w=[gt_tok])
                    Sc.act(lambda e, gt=gt: e.activation(out=gt[:, 0:H], in_=gt[:, 0:H], func=AF.Exp), r=[gt_tok], w=[gt_tok])
                    Sc.dve(lambda e, gt=gt: e.tensor_scalar(out=gt[:, 0:H], in0=gt[:, 0:H], scalar1=1.0, scalar2=None, op0=ALU.add), r=[gt_tok], w=[gt_tok])
                    Sc.act(lambda e, gt=gt: e.activation(out=gt[:, 0:H], in_=gt[:, 0:H], func=AF.Ln), r=[gt_tok], w=[gt_tok])
                    Sc.dve(lambda e, gt=gt: e.tensor_tensor(out=gt[:, H:2 * H], in0=gt[:, 0:H], in1=negA, op=ALU.mult), r=[gt_tok, negA_tok], w=[gt_tok])
                    xb_ = next_aux()
                    Sc.pe(lambda e, gt=gt, xb_=xb_: e.matmul(out=bank_f32(xb_)[:, 0:H], lhsT=LCUM, rhs=gt[:, H:2 * H], start=True, stop=True),
                          r=[gt_tok, "cst"], w=[("ps", xb_)])
                    Sc.dve(lambda e, gb=gb, xb_=xb_: e.tensor_copy(out=gb[:, 0:H], in_=bank_f32(xb_)[:, 0:H]), r=[("ps", xb_)], w=[gb_tok])
                    r0 = tg * TG + tt * 128
                    Sc.dma("sp", gb_tm[r0:r0 + 128, :], gb, r=[gb_tok])
                    xc_ = next_aux()
                    Sc.pe(lambda e, gb=gb, xc_=xc_: e.transpose(out=bank_f32(xc_)[0:2 * H, 0:128], in_=gb, identity=IDENT), r=[gb_tok, "cst"], w=[("ps", xc_)])
                    Sc.act(lambda e, gT=gT, xc_=xc_: e.activation(out=gT[0:2 * H, :], in_=bank_f32(xc_)[0:2 * H, 0:128], func=AF.Copy), r=[("ps", xc_)], w=[gT_tok])
                    Sc.dma("sp", gbT[:, r0:r0 + 128], gT[0:2 * H, :], r=[gT_tok])

        n_fm = 3 * D // 512; n_z = D // 512
        for tg in range(ntg):
            prov(tg)
            linear_one_tg("a_w_in", range(0, n_fm), hT3, hT_tok, TG, "fm", epi, tg)
            linear_one_tg("a_w_in", range(n_fm, n_fm + n_z + 1), hT3, hT_tok, TG, "tm", epi, tg)

    WREG = {}
    conv_q = []

    def reg_weight(name, w_ap, K, blocks):
        kc_n = K // 128
        W2 = nc.dram_tensor("W2_" + name, [len(blocks), 128, kc_n, 512], BF16, kind="Internal").ap()
        WREG[name] = dict(K=K, blocks=blocks, W2=W2, name=name)
        for b, segs in enumerate(blocks):
            off = 0
            for (c0, n) in segs:
                conv_q.append((W2[b, :, :, off:off + n], w_ap[:, c0:c0 + n].rearrange("(k p) c -> p k c", p=128), ("W2", name, b)))
                off += n

    def issue_conv(n):
        for _ in range(n):
            if not conv_q:
                return
            o_, i_, tok = conv_q.pop(0)
            Sc.dma("pool", o_, i_, w=[tok], bg=True, max_dma_last_dim=2048)
            conv_last[tok] = len(conv_q)

    conv_last = {}
    conv_need = {}

    def ensure_conv(tok):
        while any(t == tok for (_, _, t) in conv_q[:4]) or (tok not in conv_last):
            if not conv_q:
                break
            issue_conv(1)

    lin_state = {"pending": []}
    NSLV = [3]
    KS = 16

    def flush_pending():
        p = lin_state["pending"]
        lin_state["pending"] = []
        for f in p:
            f()

    def linear_one_tg(wname, blk_idx, hT3, hT_tok, TG, mode, epilogue, tg, pre_block=None):
        NSL = NSLV[0]
        hT_tokf = hT_tok if callable(hT_tok) else (lambda ks_: hT_tok)
        wr = WREG[wname]
        K = wr["K"]; W2 = wr["W2"]
        kc_n = K // 128
        nks = (kc_n + KS - 1) // KS
        key = AR.gen
        if lin_state.get("gen") != key:
            slots = []
            for i in range(NSL):
                a, tok = AR.bf16(KS * 512, "wslot%d" % i)
                slots.append((a.rearrange("p (k c) -> p k c", c=512), tok))
            lin_state["gen"] = key; lin_state["slots"] = slots; lin_state["cnt"] = 0
        slots = lin_state["slots"]
        ntt = TG // 128
        for bi_ in blk_idx:
            segs = wr["blocks"][bi_]
            ncols = sum(n for _, n in segs)
            nacc = (ncols + 127) // 128 if mode == "fm" else ntt
            accs = [next_acc() for _ in range(nacc)]
            if pre_block is not None:
                pre_block(tg, bi_)
            for ks in range(nks):
                k0 = ks * KS; nk = min(KS, kc_n - k0)
                sl, sl_tok = slots[lin_state["cnt"] % NSL]; lin_state["cnt"] += 1
                hT_tok_ = hT_tokf(ks)
                ensure_conv(("W2", wname, bi_))
                Sc.dma("pool", sl[:, 0:nk, 0:ncols], W2[bi_, :, k0:k0 + nk, 0:ncols], r=[("W2", wname, bi_)], w=[sl_tok])
                issue_conv(1)
                if ks == 0 and nacc >= 4:
                    h2 = nacc // 2
                    order = [(kk, ai) for kk in range(nk) for ai in range(h2)] + [(kk, ai) for kk in range(nk) for ai in range(h2, nacc)]
                else:
                    order = [(kk, ai) for kk in range(nk) for ai in range(nacc)]
                for (kk, ai) in order:
                    first = (ks == 0 and kk == 0); last = (ks == nks - 1 and kk == nk - 1)
                    if True:
                        if mode == "fm":
                            cw = min(128, ncols - ai * 128)
                            Sc.pe(lambda e, ai=ai, kk=kk, k0=k0, sl=sl, cw=cw, first=first, last=last, b=accs[ai]: e.matmul(
                                out=bank_f32(b)[0:cw, 0:TG], lhsT=sl[:, kk, ai * 128:ai * 128 + cw], rhs=hT3[:, k0 + kk, :],
                                start=first, stop=last), r=[sl_tok, hT_tok_], w=[("ps", accs[ai])])
                        else:
                            Sc.pe(lambda e, ai=ai, kk=kk, k0=k0, sl=sl, ncols=ncols, first=first, last=last, b=accs[ai]: e.matmul(
                                out=bank_f32(b)[:, 0:ncols], lhsT=hT3[:, k0 + kk, ai * 128:(ai + 1) * 128], rhs=sl[:, kk, 0:ncols],
                                start=first, stop=last), r=[sl_tok, hT_tok_], w=[("ps", accs[ai])])
            flush_pending()
            d_ = epilogue(tg, bi_, segs, accs)
            if d_ is not None:
                lin_state["pending"].append(d_)

    def phase_B():
        AR.reset()
        scale = 128.0 ** -0.5
        gbcol, gbcol_tok = AR.f32(NT * 2 * H, "gbcol")
        gbcol3 = gbcol.rearrange("p (t c) -> p t c", c=2 * H)
        Sc.dma("sp", gbcol3, gb_tm.rearrange("(t p) c -> p t c", p=128), w=[gbcol_tok])
        bgcol, bgcol_tok = AR.f32(NT * H, "bgcol")
        bgcol3 = bgcol.rearrange("p (t c) -> p t c", c=H)
        Sc.act(lambda e: e.activation(out=bgcol3, in_=gbcol3[:, :, 0:H], func=AF.Exp), r=[gbcol_tok], w=[bgcol_tok])
        Sc.dve(lambda e: e.tensor_tensor(out=bgcol3, in0=bgcol3, in1=gbcol3[:, :, H:2 * H], op=ALU.mult), r=[bgcol_tok, gbcol_tok], w=[bgcol_tok])
        G = 8 if H >= 8 else (4 if H >= 4 else 1)
        streams = []
        for gi in range(G):
            st = {}
            def T(n, name, gi=gi):
                return AR.f32(n, "%s_%d" % (name, gi))
            def TB(n, name, gi=gi):
                return AR.bf16(n, "%s_%d" % (name, gi))
            st["dl"] = T(2, "dl")
            for nm in ("q0", "k0", "v0", "q1", "k1", "v1"):
                st[nm] = TB(128, nm)
            for nm in ("Gr0", "Gr1", "Br0", "Br1", "DT", "DTs", "X", "A", "R", "P", "Q", "P2", "Q2",
                       "u", "EG", "osb0", "osb1", "Sst"):
                st[nm] = T(128, nm)
            for nm in ("kT", "qT", "AcT", "vb", "kbg", "wT", "qdT", "kdA", "kdB", "vnew", "Sb", "Rb"):
                st[nm] = TB(128, nm)
            streams.append(st)
        F32R = mybir.dt.float32r
        def fr(ap):
            return ap.bitcast(F32R) if INV_F32R else ap
        bank_rot = [0]
        def nb():
            i = bank_rot[0] % 8; bank_rot[0] += 1; return i

        def head_gen(st, h):
            dl, dl_tok = st["dl"]
            Sst, Sst_tok = st["Sst"]
            Sc.dve(lambda e: e.memset(Sst, 0.0), w=[Sst_tok])
            Sb, Sb_tok = st["Sb"]
            Sc.dve(lambda e: e.memset(Sb, 0.0), w=[Sb_tok])
            vnew, vnew_tok = st["vnew"]
            Sc.dve(lambda e: e.memset(vnew, 0.0), w=[vnew_tok])
            kdA, kdA_tok = st["kdA"]; kdB, kdB_tok = st["kdB"]
            Sc.dve(lambda e: e.memset(kdA, 0.0), w=[kdA_tok])
            Sc.dve(lambda e: e.memset(kdB, 0.0), w=[kdB_tok])
            yield
            for t in range(NT):
                par = t % 2
                (qt, qt_tok), (kt, kt_tok), (vt, vt_tok) = st["q%d" % par], st["k%d" % par], st["v%d" % par]
                r0 = t * 128
                hs = slice(h * 128, (h + 1) * 128)
                Sc.dma("sp", qt, q_tm[r0:r0 + 128, hs], w=[qt_tok])
                Sc.dma("sp", kt, k_tm[r0:r0 + 128, hs], w=[kt_tok])
                Sc.dma("sp", vt, v_tm[r0:r0 + 128, hs], w=[vt_tok])
                gcol = gbcol3[:, t, h:h + 1]; bcol = gbcol3[:, t, H + h:H + h + 1]; bgc = bgcol3[:, t, h:h + 1]
                Gr, Grow_tok = st["Gr%d" % par]; Br, Brow_tok = st["Br%d" % par]
                Sc.dma("sp", Gr, gbT[h:h + 1, r0:r0 + 128].partition_broadcast(128).rearrange("p o n -> p (o n)"), w=[Grow_tok])
                Sc.dma("sp", Br, gbT[H + h:H + h + 1, r0:r0 + 128].partition_broadcast(128).rearrange("p o n -> p (o n)"), w=[Brow_tok])
                Sc.dve(lambda e, Gr=Gr, t=t: e.tensor_tensor(out=dl[0:64, 0:1], in0=Gr[0:64, 63:64], in1=gbcol3[0:64, t, h:h + 1], op=ALU.subtract),
                       r=[Grow_tok, gbcol_tok], w=[dl_tok])
                Sc.dve(lambda e, Gr=Gr, t=t: e.tensor_tensor(out=dl[64:128, 0:1], in0=Gr[64:128, 127:128], in1=gbcol3[64:128, t, h:h + 1], op=ALU.subtract),
                       r=[Grow_tok, gbcol_tok], w=[dl_tok])
                Sc.act(lambda e: e.activation(out=dl[:, 0:1], in_=dl[:, 0:1], func=AF.Exp), r=[dl_tok], w=[dl_tok])
                kT, kT_tok = st["kT"]; qT, qT_tok = st["qT"]
                b1 = nb()
                Sc.pe(lambda e, b1=b1: e.transpose(out=bank_bf16(b1)[:, 0:128], in_=kt, identity=IDENTB), r=[kt_tok, "cstb"], w=[("ps", b1)])
                Sc.act(lambda e, b1=b1: e.activation(out=kT, in_=bank_bf16(b1)[:, 0:128], func=AF.Copy), r=[("ps", b1)], w=[kT_tok])
                b2 = nb()
                Sc.pe(lambda e, b2=b2: e.transpose(out=bank_bf16(b2)[:, 0:128], in_=qt, identity=IDENTB), r=[qt_tok, "cstb"], w=[("ps", b2)])
                Sc.dve(lambda e, b2=b2: e.tensor_copy(out=qT, in_=bank_bf16(b2)[:, 0:128]), r=[("ps", b2)], w=[qT_tok])
                yield
                DT, DT_tok = st["DT"]; DTs, DTs_tok = st["DTs"]; EG, EG_tok = st["EG"]
                Sc.dve(lambda e: e.scalar_tensor_tensor(out=DT, in0=Gr, scalar=gcol, in1=MASKNEG, op0=ALU.subtract, op1=ALU.add),
                       r=[Grow_tok, gbcol_tok, "cst"], w=[DT_tok])
                Sc.act(lambda e: e.activation(out=DT, in_=DT, func=AF.Exp), r=[DT_tok], w=[DT_tok])
                Sc.act(lambda e: e.activation(out=EG, in_=Gr, func=AF.Exp), r=[Grow_tok], w=[EG_tok])
                Sc.dve(lambda e: e.tensor_tensor(out=DTs, in0=DT, in1=STRICT, op=ALU.mult), r=[DT_tok, "cst"], w=[DTs_tok])
                bkk = nb()
                Sc.pe(lambda e, bkk=bkk: e.matmul(out=bank_f32(bkk)[:, 0:128], lhsT=kT, rhs=kT, start=True, stop=True), r=[kT_tok], w=[("ps", bkk)])
                bqk = nb()
                Sc.pe(lambda e, bqk=bqk: e.matmul(out=bank_f32(bqk)[:, 0:128], lhsT=kT, rhs=qT, start=True, stop=True), r=[kT_tok, qT_tok], w=[("ps", bqk)])
                X, X_tok = st["X"]; A, A_tok = st["A"]; R, R_tok = st["R"]; AcT, AcT_tok = st["AcT"]
                Sc.dve(lambda e, bkk=bkk: e.tensor_tensor(out=fr(X), in0=bank_f32(bkk)[:, 0:128], in1=Br, op=ALU.mult), r=[("ps", bkk), Brow_tok], w=[X_tok])
                Sc.dve(lambda e: e.tensor_tensor(out=fr(X), in0=X, in1=DTs, op=ALU.mult), r=[X_tok, DTs_tok], w=[X_tok])
                Sc.dve(lambda e, bqk=bqk: e.scalar_tensor_tensor(out=AcT, in0=bank_f32(bqk)[:, 0:128], scalar=scale, in1=DT, op0=ALU.mult, op1=ALU.mult),
                       r=[("ps", bqk), DT_tok], w=[AcT_tok])
                yield
                ba = nb()
                Sc.pe(lambda e, ba=ba: e.transpose(out=bank_f32(ba)[:, 0:128], in_=X, identity=IDENT), r=[X_tok, "cst"], w=[("ps", ba)])
                Sc.act(lambda e, ba=ba: e.activation(out=fr(A), in_=bank_f32(ba)[:, 0:128], func=AF.Copy), r=[("ps", ba)], w=[A_tok])
                Sc.dve(lambda e: e.tensor_tensor(out=fr(R), in0=IDENT, in1=X, op=ALU.subtract), r=["cst", X_tok], w=[R_tok])
                Pc, Pc_tok = X, X_tok
                Qc, Qc_tok = A, A_tok
                pq = [(st["P"], st["Q"]), (st["P2"], st["Q2"])]
                for kstage in range(1, 6):
                    (Pn, Pn_tok), (Qn, Qn_tok) = pq[kstage % 2]
                    bq = nb()
                    Sc.pe(lambda e, bq=bq, Pc=Pc, Qc=Qc: e.matmul(out=bank_f32(bq)[:, 0:128], lhsT=fr(Pc), rhs=fr(Qc), start=True, stop=True),
                          r=[Pc_tok, Qc_tok], w=[("ps", bq)])
                    Sc.dve(lambda e, bq=bq, Qn=Qn: e.tensor_copy(out=fr(Qn), in_=bank_f32(bq)[:, 0:128]), r=[("ps", bq)], w=[Qn_tok])
                    if kstage < 5:
                        bp = nb()
                        Sc.pe(lambda e, bp=bp, Pc=Pc, Qc=Qc: e.matmul(out=bank_f32(bp)[:, 0:128], lhsT=fr(Qc), rhs=fr(Pc), start=True, stop=True),
                              r=[Pc_tok, Qc_tok], w=[("ps", bp)])
                        Sc.act(lambda e, bp=bp, Pn=Pn: e.activation(out=fr(Pn), in_=bank_f32(bp)[:, 0:128], func=AF.Copy), r=[("ps", bp)], w=[Pn_tok])
                    bm = nb()
                    Sc.pe(lambda e, bm=bm, Qn=Qn: e.matmul(out=bank_f32(bm)[:, 0:128], lhsT=fr(Qn), rhs=fr(R), start=True, stop=True),
                          r=[Qn_tok, R_tok], w=[("ps", bm)])
                    Sc.dve(lambda e, bm=bm: e.tensor_tensor(out=fr(R), in0=R, in1=bank_f32(bm)[:, 0:128], op=ALU.add), r=[R_tok, ("ps", bm)], w=[R_tok])
                    Pc, Pc_tok, Qc, Qc_tok = Pn, Pn_tok, Qn, Qn_tok
                    yield
                vb, vb_tok = st["vb"]; kbg, kbg_tok = st["kbg"]; u, u_tok = st["u"]; wT, wT_tok = st["wT"]; qdT, qdT_tok = st["qdT"]
                Sc.dve(lambda e: e.tensor_scalar(out=vb, in0=vt, scalar1=bcol, scalar2=None, op0=ALU.mult), r=[vt_tok, gbcol_tok], w=[vb_tok])
                Sc.dve(lambda e: e.tensor_scalar(out=kbg, in0=kt, scalar1=bgc, scalar2=None, op0=ALU.mult), r=[kt_tok, bgcol_tok], w=[kbg_tok])
                Rb, Rb_tok = st["Rb"]
                Sc.act(lambda e: e.activation(out=Rb, in_=R, func=AF.Copy), r=[R_tok], w=[Rb_tok])
                bu = nb()
                Sc.pe(lambda e, bu=bu: e.matmul(out=bank_f32(bu)[:, 0:128], lhsT=Rb, rhs=vb, start=True, stop=True), r=[Rb_tok, vb_tok], w=[("ps", bu)])
                Sc.act(lambda e, bu=bu: e.activation(out=u, in_=bank_f32(bu)[:, 0:128], func=AF.Copy), r=[("ps", bu)], w=[u_tok])
                bw = nb()
                Sc.pe(lambda e, bw=bw: e.matmul(out=bank_f32(bw)[:, 0:128], lhsT=kbg, rhs=Rb, start=True, stop=True), r=[Rb_tok, kbg_tok], w=[("ps", bw)])
                Sc.dve(lambda e, bw=bw: e.tensor_copy(out=wT, in_=bank_f32(bw)[:, 0:128]), r=[("ps", bw)], w=[wT_tok])
                Sc.dve(lambda e: e.scalar_tensor_tensor(out=qdT, in0=qT, scalar=scale, in1=EG, op0=ALU.mult, op1=ALU.mult), r=[qT_tok, EG_tok], w=[qdT_tok])
                Sc.dve(lambda e: e.tensor_scalar(out=kdA[0:64, :], in0=kt[0:64, :], scalar1=dl[0:64, 0:1], scalar2=None, op0=ALU.mult),
                       r=[kt_tok, dl_tok], w=[kdA_tok])
                Sc.dve(lambda e: e.tensor_scalar(out=kdB[64:128, :], in0=kt[64:128, :], scalar1=dl[64:128, 0:1], scalar2=None, op0=ALU.mult),
                       r=[kt_tok, dl_tok], w=[kdB_tok])
                yield
                osb, osb_tok = st["osb%d" % par]
                for c in range(2):
                    rs_ = slice(64 * c, 64 * c + 64)
                    kd, kd_tok = (kdA, kdA_tok) if c == 0 else (kdB, kdB_tok)
                    bws = nb()
                    Sc.pe(lambda e, bws=bws: e.matmul(out=bank_f32(bws)[:, 0:128], lhsT=wT, rhs=Sb, start=True, stop=True), r=[wT_tok, Sb_tok], w=[("ps", bws)])
                    Sc.dve(lambda e, bws=bws, rs_=rs_: e.tensor_tensor(out=vnew[rs_, :], in0=u[rs_, :], in1=bank_f32(bws)[rs_, 0:128], op=ALU.subtract),
                           r=[u_tok, ("ps", bws)], w=[vnew_tok])
                    bo = nb()
                    Sc.pe(lambda e, bo=bo: e.matmul(out=bank_f32(bo)[:, 0:128], lhsT=qdT, rhs=Sb, start=True, stop=False), r=[qdT_tok, Sb_tok], w=[("ps", bo)])
                    Sc.pe(lambda e, bo=bo: e.matmul(out=bank_f32(bo)[:, 0:128], lhsT=AcT, rhs=vnew, start=False, stop=True), r=[AcT_tok, vnew_tok], w=[("ps", bo)])
                    Sc.act(lambda e, bo=bo, rs_=rs_: e.activation(out=osb[rs_, :], in_=bank_f32(bo)[rs_, 0:128], func=AF.Copy), r=[("ps", bo)], w=[osb_tok])
                    bs = nb()
                    Sc.pe(lambda e, bs=bs, kd=kd: e.matmul(out=bank_f32(bs)[:, 0:128], lhsT=kd, rhs=vnew, start=True, stop=True), r=[kd_tok, vnew_tok], w=[("ps", bs)])
                    sdcol = EG[:, 64 * c + 63:64 * c + 64]
                    Sc.dve(lambda e, bs=bs, sdcol=sdcol: e.scalar_tensor_tensor(out=Sb, in0=Sst, scalar=sdcol, in1=bank_f32(bs)[:, 0:128], op0=ALU.mult, op1=ALU.add),
                           r=[Sst_tok, EG_tok, ("ps", bs)], w=[Sb_tok])
                    Sc.dve(lambda e, bs=bs, sdcol=sdcol: e.scalar_tensor_tensor(out=Sst, in0=Sst, scalar=sdcol, in1=bank_f32(bs)[:, 0:128], op0=ALU.mult, op1=ALU.add),
                           r=[Sst_tok, EG_tok, ("ps", bs)], w=[Sst_tok])
                    yield
                Sc.dma("sp", o_gdn[r0:r0 + 128, hs], osb, r=[osb_tok])

        for h0 in range(0, H, G):
            gens = [head_gen(streams[gi], h0 + gi) for gi in range(min(G, H - h0))]
            alive = list(gens)
            while alive:
                nxt = []
                for g in alive:
                    try:
                        next(g); nxt.append(g)
                    except StopIteration:
                        pass
                alive = nxt

    def resid_linear(wname, prov, hT3, hT_tok, TG, resid_src, dst, ntg):
        NRB = 8
        rbufs = [AR.f32(512, "rbuf%d" % i) for i in range(NRB)]
        cnt = [0]
        cur = {}
        ntt_ = TG // 128

        def pre(tg, bi_):
            segs = WREG[wname]["blocks"][bi_]
            c0 = segs[0][0]; ncols = segs[0][1]
            lst = []
            for tt in range(ntt_):
                i2 = cnt[0] % NRB; cnt[0] += 1
                rb, rb_tok = rbufs[i2]
                r0 = tg * TG + tt * 128
                Sc.dma("sp", rb[:, 0:ncols], resid_src[r0:r0 + 128, c0:c0 + ncols], w=[rb_tok])
                lst.append((rb, rb_tok))
            cur[(tg, bi_)] = lst

        def epi(tg, bi_, segs, accs):
            c0 = segs[0][0]; ncols = segs[0][1]
            lst = cur.pop((tg, bi_))
            for tt, b in enumerate(accs):
                rb, rb_tok = lst[tt]
                r0 = tg * TG + tt * 128
                Sc.dve(lambda e, rb=rb, b=b: e.tensor_tensor(out=rb[:, 0:ncols], in0=rb[:, 0:ncols], in1=bank_f32(b)[:, 0:ncols], op=ALU.add),
                       r=[rb_tok, ("ps", b)], w=[rb_tok])
                Sc.dma("sp", dst[r0:r0 + 128, c0:c0 + ncols], rb[:, 0:ncols], r=[rb_tok])
        nb_ = len(WREG[wname]["blocks"])
        for tg in range(ntg):
            prov(tg)
            linear_one_tg(wname, range(nb_), hT3, hT_tok, TG, "tm", epi, tg, pre_block=pre)

    def phase_C():
        AR.reset()
        NSLV[0] = 2
        TG = 512 if S_ >= 512 else S_
        prol = rms_prologue_factory(a_out_norm_t, D, group=128)
        prov, hT3, hT_tok = make_prov_tm(o_gdn, D, TG, prol, extra_src=z_tm)
        resid_linear("a_w_out", prov, hT3, hT_tok, TG, x_in, x1, S_ // TG)

    def phase_FFN(layer, src, dst):
        AR.reset()
        NSLV[0] = 3
        TG = 512 if S_ >= 512 else S_
        ntg = S_ // TG
        prol = rms_prologue_factory(ffn_norm[layer:layer + 1, :], D)
        prov, hT3, hT_tok = make_prov_tm(src, D, TG, prol)
        sgb = [AR.f32(TG, "sg%d" % i) for i in range(2)]
        acb = [AR.bf16(TG, "ac%d" % i) for i in range(2)]
        cnt = [0]

        def epi(tg, bi_, segs, accs):
            f0 = segs[0][0]; nf = segs[0][1]
            nch = nf // 128
            for ci in range(nch):
                i2 = cnt[0] % 2; cnt[0] += 1
                (sg, sg_tok), (ac, ac_tok) = sgb[i2], acb[i2]
                bg_, bu_ = accs[ci], accs[nch + ci]
                Sc.act(lambda e, sg=sg, bg_=bg_: e.activation(out=sg, in_=bank_f32(bg_)[:, 0:TG], func=AF.Silu), r=[("ps", bg_)], w=[sg_tok])
                Sc.dve(lambda e, sg=sg, ac=ac, bu_=bu_: e.tensor_tensor(out=ac, in0=sg, in1=bank_f32(bu_)[:, 0:TG], op=ALU.mult),
                       r=[sg_tok, ("ps", bu_)], w=[ac_tok])
                fr = f0 + ci * 128
                Sc.dma("sp", actT[fr:fr + 128, tg * TG:(tg + 1) * TG], ac, r=[ac_tok])
        nb_ = len(WREG["gu%d" % layer]["blocks"])
        for tg in range(ntg):
            prov(tg)
            linear_one_tg("gu%d" % layer, range(nb_), hT3, hT_tok, TG, "fm", epi, tg)
        Sc.barrier()
        AR.reset()
        TG2 = 512 if S_ >= 512 else S_
        prov2, hT3b, hT_tokb = make_prov_fm(actT, F, TG2)
        resid_linear("dn%d" % layer, prov2, hT3b, hT_tokb, TG2, src, dst, S_ // TG2)

    def phase_E():
        TG = 512 if S_ >= 512 else S_
        ntg = S_ // TG

        def run(norm_row, wname, ncols_total, gain_row, dstT, do_v):
            AR.reset()
            prol = rms_prologue_factory(norm_row, D)
            prov, hT3, hT_tok = make_prov_tm(x2, D, TG, prol)
            gn, gn_tok = AR.f32(128, "gn")
            Sc.dma("sp", gn, bcast_row(gain_row, 128), w=[gn_tok])
            sqb, sqb_tok = AR.f32(512, "sqb")
            stt, stt_tok = AR.f32(16, "stt")
            NSLV[0] = 3
            knb = [AR.bf16(512, "kn%d" % i) for i in range(8)]
            kcnt = [0]
            kTb = [AR.bf16(512, "kTb%d" % i) for i in range(2)]
            vbb = [AR.bf16(512, "vbb%d" % i) for i in range(2)]
            cnt = [0]

            def epi(tg, bi_, segs, accs):
                c0 = segs[0][0]
                stage2 = []
                for tt, b in enumerate(accs):
                    i2 = cnt[0] % 2; cnt[0] += 1
                    r0 = tg * TG + tt * 128
                    pb = bank_f32(b)
                    if c0 < D:
                        kTs, kTs_tok = kTb[i2]
                        kn, kn_tok = knb[kcnt[0] % 8]; kcnt[0] += 1
                        Sc.act(lambda e, pb=pb: e.activation(out=sqb, in_=pb[:, 0:512], func=AF.Square), r=[("ps", b)], w=[sqb_tok])
                        Sc.dve(lambda e: e.tensor_reduce(out=stt[:, 0:4], in_=sqb.rearrange("p (g c) -> p g c", c=128), axis=AX.X, op=ALU.add), r=[sqb_tok], w=[stt_tok])
                        Sc.dve(lambda e: e.tensor_scalar(out=stt[:, 4:8], in0=stt[:, 0:4], scalar1=1.0 / 128, scalar2=RMS_EPS, op0=ALU.mult, op1=ALU.add), r=[stt_tok], w=[stt_tok])
                        Sc.act(lambda e: e.activation(out=stt[:, 4:8], in_=stt[:, 4:8], func=AF.Sqrt), r=[stt_tok], w=[stt_tok])
                        Sc.dve(lambda e: e.reciprocal(out=stt[:, 4:8], in_=stt[:, 4:8]), r=[stt_tok], w=[stt_tok])
                        for g in range(4):
                            Sc.dve(lambda e, kn=kn, pb=pb, g=g: e.scalar_tensor_tensor(out=kn[:, g * 128:(g + 1) * 128], in0=pb[:, g * 128:(g + 1) * 128],
                                   scalar=stt[:, 4 + g:5 + g], in1=gn, op0=ALU.mult, op1=ALU.mult), r=[("ps", b), stt_tok, gn_tok], w=[kn_tok])
                        def st2(kn=kn, kn_tok=kn_tok, kTs=kTs, kTs_tok=kTs_tok, c0=c0, r0=r0):
                            xb_ = next_aux()
                            for g in range(4):
                                Sc.pe(lambda e, kn=kn, g=g, xb_=xb_: e.transpose(out=bank_bf16(xb_)[:, g * 128:(g + 1) * 128], in_=kn[:, g * 128:(g + 1) * 128], identity=IDENTB),
                                      r=[kn_tok, "cstb"], w=[("ps", xb_)])
                            Sc.act(lambda e, kTs=kTs, xb_=xb_: e.activation(out=kTs, in_=bank_bf16(xb_)[:, 0:512], func=AF.Copy), r=[("ps", xb_)], w=[kTs_tok])
                            g0 = c0 // 128
                            Sc.dma("sp", dstT[g0:g0 + 4, :, r0:r0 + 128].rearrange("g p t -> p g t"), kTs.rearrange("p (g t) -> p g t", t=128), r=[kTs_tok])
                        stage2.append(st2)
                    else:
                        vbt, vbt_tok = vbb[i2]
                        Sc.act(lambda e, vbt=vbt, pb=pb: e.activation(out=vbt, in_=pb[:, 0:512], func=AF.Copy), r=[("ps", b)], w=[vbt_tok])
                        Sc.dma("sp", vb_d[r0:r0 + 128, c0 - D:c0 - D + 512], vbt, r=[vbt_tok])
                return (lambda: [f() for f in stage2]) if stage2 else None
            nb_ = len(WREG[wname]["blocks"])
            for tg in range(ntg):
                prov(tg)
                linear_one_tg(wname, range(nb_), hT3, hT_tok, TG, "tm", epi, tg)
            Sc.barrier()
        run(kv_norm, "w_kv", 2 * D, k_norm, kT_d, True)
        run(b_norm, "b_w_q", D, b_q_norm, qT_d, False)

    def phase_F(lam_init):
        AR.reset()
        scale = 128.0 ** -0.5
        QG = 512 if S_ >= 512 else S_
        nqg = S_ // QG
        ntq = QG // 128
        lp, lp_tok = AR.f32(512, "lp")
        Sc.dma("sp", lp, bcast_row(b_lambda, 512), w=[lp_tok])
        lt, lt_tok = AR.f32(272, "lt")
        Sc.dve(lambda e: e.tensor_tensor(out=lt[:, 0:128], in0=lp[:, 0:128], in1=lp[:, 128:256], op=ALU.mult), r=[lp_tok], w=[lt_tok])
        Sc.dve(lambda e: e.tensor_tensor(out=lt[:, 128:256], in0=lp[:, 256:384], in1=lp[:, 384:512], op=ALU.mult), r=[lp_tok], w=[lt_tok])
        Sc.dve(lambda e: e.tensor_reduce(out=lt[:, 256:258], in_=lt[:, 0:256].rearrange("p (g c) -> p g c", c=128), axis=AX.X, op=ALU.add), r=[lt_tok], w=[lt_tok])
        Sc.act(lambda e: e.activation(out=lt[:, 258:260], in_=lt[:, 256:258], func=AF.Exp), r=[lt_tok], w=[lt_tok])
        Sc.dve(lambda e: e.tensor_tensor(out=lt[:, 260:261], in0=lt[:, 259:260], in1=lt[:, 258:259], op=ALU.subtract), r=[lt_tok], w=[lt_tok])
        Sc.dve(lambda e: e.tensor_scalar(out=lt[:, 261:262], in0=lt[:, 260:261], scalar1=-lam_init, scalar2=None, op0=ALU.add), r=[lt_tok], w=[lt_tok])
        neglam = lt[:, 261:262]
        kTs = [AR.bf16(S_, "kTs%d" % m) for m in range(2)]
        qTs = [AR.bf16(S_, "qTs%d" % m) for m in range(2)]
        vx, vx_tok = AR.bf16(NT * 258, "vx")
        vx3 = vx.rearrange("p (t c) -> p t c", c=258)
        Sc.dve(lambda e: e.memset(vx, 1.0), w=[vx_tok])
        Eb = [AR.bf16(QG, "E%d" % i) for i in range(3)]
        Em = [AR.bf16(128, "Em%d" % i) for i in range(2)]
        AMB, AMB_tok = AR.bf16(128, "AMB")
        Sc.dve(lambda e: e.tensor_copy(out=AMB, in_=AMASK), r=["cst"], w=[AMB_tok])
        Osb = [[AR.f32(257, "O%d_%d" % (m, tt)) for tt in range(ntq)] for m in range(2)]
        outb = [AR.f32(256, "ob%d" % i) for i in range(2)]
        rcp, rcp_tok = AR.f32(8, "rcp")
        ecnt = [0]; ocnt = [0]
        for h in range(DH):
            for m in range(2):
                Sc.dma("sp", kTs[m][0], kT_d[2 * h + m], w=[kTs[m][1]])
                Sc.dma("sp", qTs[m][0], qT_d[2 * h + m], w=[qTs[m][1]])
            Sc.dma("sp", vx3[:, :, 0:256], vb_d[:, h * 256:(h + 1) * 256].rearrange("(t p) c -> p t c", p=128), w=[vx_tok])
            for qg in range(nqg):
                for m in range(2):
                    kTm, kT_tok = kTs[m]; qTm, qT_tok = qTs[m]
                    obanks = [0, 1, 2, 3][:ntq]
                    nkt = (qg + 1) * ntq
                    def issue_st(kt_):
                        sb_ = 4 + (ecnt[0] % 3)
                        E, E_tok = Eb[ecnt[0] % 3]; ecnt[0] += 1
                        Sc.pe(lambda e, sb_=sb_, kt_=kt_, kTm=kTm, qTm=qTm, qg=qg: e.matmul(out=bank_f32(sb_)[:, 0:QG], lhsT=kTm[:, kt_ * 128:(kt_ + 1) * 128],
                              rhs=qTm[:, qg * QG:(qg + 1) * QG], start=True, stop=True), r=[kT_tok, qT_tok], w=[("ps", sb_)])
                        Sc.act(lambda e, sb_=sb_, E=E: e.activation(out=E, in_=bank_f32(sb_)[:, 0:QG], func=AF.Exp, scale=scale), r=[("ps", sb_)], w=[E_tok])
                        return E, E_tok
                    pend = [issue_st(0)]
                    if nkt > 1:
                        pend.append(issue_st(1))
                    for kt_ in range(nkt):
                        E, E_tok = pend.pop(0)
                        if kt_ + 2 < nkt:
                            pend.append(issue_st(kt_ + 2))
                        for tt in range(ntq):
                            qt_ = qg * ntq + tt
                            if qt_ < kt_:
                                continue
                            lhs = E[:, tt * 128:(tt + 1) * 128]; lhs_tok = E_tok
                            if qt_ == kt_:
                                Emm, Emm_tok = Em[tt % 2]
                                Sc.dve(lambda e, Emm=Emm, lhs=lhs: e.tensor_tensor(out=Emm, in0=lhs, in1=AMB, op=ALU.mult), r=[E_tok, AMB_tok], w=[Emm_tok])
                                lhs, lhs_tok = Emm, Emm_tok
                            Sc.pe(lambda e, tt=tt, lhs=lhs, kt_=kt_, qt_=qt_: e.matmul(out=bank_f32(obanks[tt])[:, 0:257], lhsT=lhs, rhs=vx3[:, kt_, 0:257],
                                  start=(kt_ == 0), stop=(kt_ == qt_)), r=[lhs_tok, vx_tok], w=[("ps", obanks[tt])])
                    for tt in range(ntq):
                        O, O_tok = Osb[m][tt]
                        if tt % 2 == 0:
                            Sc.act(lambda e, O=O, tt=tt: e.activation(out=O, in_=bank_f32(obanks[tt])[:, 0:257], func=AF.Copy), r=[("ps", obanks[tt])], w=[O_tok])
                        else:
                            Sc.dve(lambda e, O=O, tt=tt: e.tensor_copy(out=O, in_=bank_f32(obanks[tt])[:, 0:257]), r=[("ps", obanks[tt])], w=[O_tok])
                for tt in range(ntq):
                    (O0, O0_tok), (O1, O1_tok) = Osb[0][tt], Osb[1][tt]
                    ob, ob_tok = outb[ocnt[0] % 2]; ocnt[0] += 1
                    Sc.dve(lambda e, O0=O0: e.reciprocal(out=rcp[:, 0:1], in_=O0[:, 256:257]), r=[O0_tok], w=[rcp_tok])
                    Sc.dve(lambda e, O1=O1: e.reciprocal(out=rcp[:, 1:2], in_=O1[:, 256:257]), r=[O1_tok], w=[rcp_tok])
                    Sc.dve(lambda e: e.tensor_tensor(out=rcp[:, 2:3], in0=rcp[:, 1:2], in1=neglam, op=ALU.mult), r=[rcp_tok, lt_tok], w=[rcp_tok])
                    Sc.dve(lambda e, ob=ob, O0=O0: e.tensor_scalar(out=ob, in0=O0[:, 0:256], scalar1=rcp[:, 0:1], scalar2=None, op0=ALU.mult), r=[O0_tok, rcp_tok], w=[ob_tok])
                    Sc.dve(lambda e, ob=ob, O1=O1: e.scalar_tensor_tensor(out=ob, in0=O1[:, 0:256], scalar=rcp[:, 2:3], in1=ob, op0=ALU.mult, op1=ALU.add),
                           r=[O1_tok, rcp_tok, ob_tok], w=[ob_tok])
                    r0 = qg * QG + tt * 128
                    Sc.dma("sp", o_att[r0:r0 + 128, h * 256:(h + 1) * 256], ob, r=[ob_tok])

    def phase_G(lam_init):
        AR.reset()
        NSLV[0] = 3
        TG = 512 if S_ >= 512 else S_
        prol = rms_prologue_factory(b_sub_norm_t, D, group=256, post_scale=(1.0 - lam_init))
        prov, hT3, hT_tok = make_prov_tm(o_att, D, TG, prol)
        resid_linear("b_w_out", prov, hT3, hT_tok, TG, x2, x3, S_ // TG)

    _orig_barrier = Sc.barrier
    def _barrier():
        flush_pending()
        _orig_barrier()
    Sc.barrier = _barrier
    lam_init = 0.8 - 0.6 * math.exp(-0.3 * 1)
    blkD = [[(c, 512)] for c in range(0, D, 512)]
    reg_weight("a_w_in", a_w_in, D, [[(c, 512)] for c in range(0, 4 * D, 512)] + [[(4 * D, 2 * H)]])
    reg_weight("a_w_out", a_w_out, D, blkD)
    reg_weight("gu0", ffn_w_gu[0], D, [[(f, 256), (F + f, 256)] for f in range(0, F, 256)])
    reg_weight("dn0", ffn_w_down[0], F, blkD)
    reg_weight("w_kv", w_kv, D, [[(c, 512)] for c in range(0, 2 * D, 512)])
    reg_weight("b_w_q", b_w_q, D, blkD)
    reg_weight("b_w_out", b_w_out, D, blkD)
    reg_weight("gu1", ffn_w_gu[1], D, [[(f, 256), (F + f, 256)] for f in range(0, F, 256)])
    reg_weight("dn1", ffn_w_down[1], F, blkD)
    issue_conv(len(WREG["a_w_in"]["blocks"]))
    phase_A(); Sc.barrier()
    phase_B(); Sc.barrier()
    phase_C(); Sc.barrier()
    phase_FFN(0, x1, x2); Sc.barrier()
    phase_E()
    phase_F(lam_init); Sc.barrier()
    phase_G(lam_init); Sc.barrier()
    phase_FFN(1, x3, out_d)
    Sc.finish()
    Sc.emit(nc, stack)
    stack.close()
    return nc


def prep_inputs(inputs, cfg, b):
    f = lambda a: np.ascontiguousarray(np.asarray(a, dtype=np.float32))
    H = cfg.H
    m = {
        "x": f(inputs["x"][b]),
        "a_norm": f(inputs["a_norm"]).reshape(1, -1),
        "a_w_in": f(inputs["a_w_in"][0]),
        "a_conv_t": f(np.asarray(inputs["a_conv"][0]).T),
        "a_A_log": f(inputs["a_A_log"]).reshape(1, -1),
        "a_dt_bias": f(inputs["a_dt_bias"]).reshape(1, -1),
        "a_out_norm_t": f(np.tile(np.asarray(inputs["a_out_norm"][0]), H)).reshape(1, -1),
        "a_w_out": f(inputs["a_w_out"][0]),
        "kv_norm": f(inputs["kv_norm"]).reshape(1, -1),
        "w_kv": f(inputs["w_kv"]),
        "k_norm": f(inputs["k_norm"]).reshape(1, -1),
        "b_norm": f(inputs["b_norm"]).reshape(1, -1),
        "b_w_q": f(inputs["b_w_q"][0]),
        "b_q_norm": f(inputs["b_q_norm"]).reshape(1, -1),
        "b_lambda": f(inputs["b_lambda"][0]).reshape(1, -1),
        "b_sub_norm_t": f(np.tile(np.asarray(inputs["b_sub_norm"][0]), cfg.DH)).reshape(1, -1),
        "b_w_out": f(inputs["b_w_out"][0]),
        "ffn_norm": f(inputs["ffn_norm"]),
        "ffn_w_gate_up": f(inputs["ffn_w_gate_up"]),
        "ffn_w_down": f(inputs["ffn_w_down"]),
        "consts": make_consts(),
    }
    return m


def run(inputs, cfg, debug=False, trace=False):
    nc = build(cfg, debug=debug)
    B = np.asarray(inputs["x"]).shape[0]
    in_maps = [prep_inputs(inputs, cfg, c) for c in range(B)]
    res = run_bass_kernel_spmd(nc, in_maps, core_ids=list(range(B)), **({"trace": True} if trace else {}))
    return res


def kernel(**inputs):
    cfg = Cfg(4096, 4096, 11008)
    res = run(inputs, cfg)
    B = np.asarray(inputs["x"]).shape[0]
    return np.stack([np.asarray(res.results[b]["out"], dtype=np.float32) for b in range(B)], axis=0)
```

```python
import math
import contextlib
import numpy as np
import concourse.bass as bass
import concourse.mybir as mybir
from concourse.bass_utils import run_bass_kernel_spmd

F32 = mybir.dt.float32
BF16 = mybir.dt.bfloat16
AF = mybir.ActivationFunctionType
ALU = mybir.AluOpType
AX = mybir.AxisListType

RMS_EPS = 1e-6
ARENA_WORDS = 41500
INV_F32R = False


class Cfg:
    def __init__(s, D, S, F):
        s.D = D; s.S = S; s.F = F
        s.H = D // 128; s.GIN = 4 * D + 2 * s.H; s.DH = D // 256
        s.NT = S // 128; s.KC = D // 128


class _Op:
    __slots__ = ("eng", "fn", "deps", "dma", "signals", "sem", "val", "prev")

    def __init__(s, eng, fn, deps, dma):
        s.eng = eng; s.fn = fn; s.deps = deps; s.dma = dma
        s.signals = False; s.sem = None; s.val = 0; s.prev = 0


class _Rec:
    def __init__(s):
        s.calls = []

    def __getattr__(s, name):
        def f(*a, **k):
            s.calls.append((name, a, k))
            return s
        return f


class Sched:
    CE = ("pe", "act", "dve", "pool")
    EPOCH = 12000
    NSLOT = 12
    USES = 1500

    def __init__(s):
        s.ops = []
        s.last_w = {}
        s.readers = {}
        s.barrier_deps = {}
        s.last_on = {}
        s.dmas_since = []

    def op(s, eng, fn, r=(), w=(), dma=False, bg=False):
        if fn is not None and not isinstance(fn, tuple):
            rec = _Rec()
            fn(rec)
            assert len(rec.calls) == 1
            fn = rec.calls[0]
        i = len(s.ops)
        deps = {}
        for b in r:
            j = s.last_w.get(b)
            if j is not None:
                deps[j] = True
        for b in w:
            j = s.last_w.get(b)
            if j is not None and j not in deps:
                deps[j] = False
            for x in s.readers.get(b, ()):
                if x not in deps:
                    deps[x] = False
        bd = s.barrier_deps.pop(eng, None)
        if bd:
            for j in bd:
                deps[j] = True
        s.ops.append(_Op(eng, fn, deps, dma))
        for b in r:
            s.readers.setdefault(b, []).append(i)
        for b in w:
            s.last_w[b] = i
            s.readers[b] = []
        if dma:
            if not bg:
                s.dmas_since.append(i)
        else:
            s.last_on[eng] = i
        return i

    def pe(s, fn, r=(), w=()): return s.op("pe", fn, r, w)
    def act(s, fn, r=(), w=()): return s.op("act", fn, r, w)
    def dve(s, fn, r=(), w=()): return s.op("dve", fn, r, w)
    def pool(s, fn, r=(), w=()): return s.op("pool", fn, r, w)

    def dma(s, q, out, in_, r=(), w=(), bg=False, **kw):
        return s.op(q, lambda e: e.dma_start(out=out, in_=in_, **kw), r, w, dma=True, bg=bg)

    def barrier(s):
        deps = list(s.last_on.values()) + list(s.dmas_since)
        s.dmas_since = []
        s.last_w = {k: v for k, v in s.last_w.items() if isinstance(k, tuple) and k[0] == "W2"}
        s.readers = {}
        for e in ("pe", "act", "dve", "pool", "sp"):
            s.barrier_deps[e] = list(deps)

    def finish(s):
        s.barrier()
        s.op("sp", None)

    def emit(s, nc, stack):
        ops = s.ops
        pos = {}
        ecount = {}
        for i, o in enumerate(ops):
            if not o.dma:
                pos[i] = ecount.get(o.eng, 0)
                ecount[o.eng] = pos[i] + 1
        for i, o in enumerate(ops):
            keep = {}
            for j, raw in o.deps.items():
                p = ops[j]
                if p.dma or o.dma:
                    keep[j] = raw
                elif p.eng == o.eng:
                    if o.eng != "pe" and raw:
                        keep[j] = raw
                else:
                    keep[j] = raw
            o.deps = keep
            for j in keep:
                ops[j].signals = True
        cnt = {e: 0 for e in s.CE}
        dcnt = {"sp": 0, "pool": 0}
        for o in ops:
            if o.dma:
                n = dcnt[o.eng]; dcnt[o.eng] += 1
                slot = n % s.NSLOT; use = n // s.NSLOT
                o.sem = ("d", o.eng, (use // s.USES) * s.NSLOT + slot)
                o.prev = 16 * (use % s.USES)
                o.val = o.prev + 16
            elif o.signals:
                c = cnt[o.eng]; cnt[o.eng] += 1
                o.sem = ("c", o.eng, c // s.EPOCH)
                o.val = c % s.EPOCH + 1
        names = sorted({o.sem for o in ops if o.sem is not None}, key=str)
        sems = {}
        for k, nm in enumerate(names):
            sems[nm] = stack.enter_context(nc.semaphore("s%d" % k))
        assert len(sems) < 140, len(sems)
        per_eng = {e: [] for e in ("pe", "act", "dve", "pool", "sp")}
        for i, o in enumerate(ops):
            per_eng[o.eng].append(o)
        engmap = {"pe": "tensor", "act": "scalar", "dve": "vector", "pool": "gpsimd", "sp": "sync"}
        with nc.Block() as b0:
            def clr(g):
                for nm in names:
                    g.sem_clear(sems[nm])
            b0.gpsimd(clr)
        with nc.Block() as blk:
            for ename, lst in per_eng.items():
                if not lst:
                    continue

                def body(e, lst=lst):
                    known = {}
                    for o in lst:
                        for j in o.deps:
                            p = ops[j]
                            if known.get(p.sem, 0) < p.val:
                                e.wait_ge(sems[p.sem], p.val)
                                known[p.sem] = p.val
                        if o.dma and o.prev > 0 and known.get(o.sem, 0) < o.prev:
                            e.wait_ge(sems[o.sem], o.prev)
                            known[o.sem] = o.prev
                        if o.fn is None:
                            continue
                        nm_, a_, k_ = o.fn
                        inst = getattr(e, nm_)(*a_, **k_)
                        if o.dma:
                            inst.then_inc(sems[o.sem], 16)
                        elif o.signals:
                            inst.then_inc(sems[o.sem], 1)
                getattr(blk, engmap[ename])(body)


def make_consts():
    c = np.zeros((128, 6, 128), np.float32)
    i = np.arange(128)
    J, I = np.meshgrid(i, i, indexing="ij")
    same = (J // 64) == (I // 64)
    c[:, 0, :] = np.eye(128, dtype=np.float32)
    c[:, 1, :] = np.where(same & (I >= J), 0.0, -30000.0)
    c[:, 2, :] = np.where(same & (I > J), 1.0, 0.0)
    c[:, 3, :] = np.where(same & (J <= I), 1.0, 0.0)
    c[:, 4, :] = np.where((J >= 64) & (I < 64), 0.0, 1.0)
    c[:, 5, :] = 1.0
    return c.reshape(128, 6 * 128)


def build(cfg, debug=False):
    D, S_, F, H, DH, NT, KC, GIN = cfg.D, cfg.S, cfg.F, cfg.H, cfg.DH, cfg.NT, cfg.KC, cfg.GIN
    nc = bass.Bass("TRN2", target_bir_lowering=False)
    stack = contextlib.ExitStack()

    def din(name, shape, dt=F32):
        return nc.dram_tensor(name, list(shape), dt, kind="ExternalInput").ap()

    def dscr(name, shape, dt=F32):
        return nc.dram_tensor(name, list(shape), dt, kind="ExternalOutput" if debug else "Internal").ap()

    x_in = din("x", [S_, D])
    a_norm = din("a_norm", [1, D]); a_w_in = din("a_w_in", [D, GIN]); a_conv_t = din("a_conv_t", [3 * D, 4])
    a_A_log = din("a_A_log", [1, H]); a_dt_bias = din("a_dt_bias", [1, H])
    a_out_norm_t = din("a_out_norm_t", [1, D]); a_w_out = din("a_w_out", [D, D])
    kv_norm = din("kv_norm", [1, D]); w_kv = din("w_kv", [D, 2 * D]); k_norm = din("k_norm", [1, 128])
    b_norm = din("b_norm", [1, D]); b_w_q = din("b_w_q", [D, D]); b_q_norm = din("b_q_norm", [1, 128])
    b_lambda = din("b_lambda", [1, 4 * 128]); b_sub_norm_t = din("b_sub_norm_t", [1, D]); b_w_out = din("b_w_out", [D, D])
    ffn_norm = din("ffn_norm", [2, D]); ffn_w_gu = din("ffn_w_gate_up", [2, D, 2 * F]); ffn_w_down = din("ffn_w_down", [2, F, D])
    consts_d = din("consts", [128, 6 * 128])
    out_d = nc.dram_tensor("out", [S_, D], F32, kind="ExternalOutput").ap()

    q_tm = dscr("q_tm", [S_, D], BF16); k_tm = dscr("k_tm", [S_, D], BF16); v_tm = dscr("v_tm", [S_, D], BF16); z_tm = dscr("z_tm", [S_, D])
    gb_tm = dscr("gb_tm", [S_, 2 * H]); gbT = dscr("gbT", [2 * H, S_])
    o_gdn = dscr("o_gdn", [S_, D]); x1 = dscr("x1", [S_, D]); x2 = dscr("x2", [S_, D]); x3 = dscr("x3", [S_, D])
    actT = dscr("actT", [F, S_], BF16)
    kT_d = dscr("kT_d", [2 * DH, 128, S_], BF16); qT_d = dscr("qT_d", [2 * DH, 128, S_], BF16)
    vb_d = dscr("vb_d", [S_, D], BF16); o_att = dscr("o_att", [S_, D])

    arena_t = stack.enter_context(nc.sbuf_tensor("arena", [128, ARENA_WORDS], F32))
    cst_t = stack.enter_context(nc.sbuf_tensor("cst", [128, 6 * 128], F32))
    cstb_t = stack.enter_context(nc.sbuf_tensor("cstb", [128, 128], BF16))
    banks = [stack.enter_context(nc.psum_tensor("bank%d" % i, [128, 512], F32)) for i in range(8)]

    Sc = Sched()
    cst = cst_t[:]
    IDENT = cst[:, 0:128]; MASKNEG = cst[:, 128:256]; STRICT = cst[:, 256:384]
    LCUM = cst[:, 384:512]; AMASK = cst[:, 512:640]; ONES = cst[:, 640:768]
    IDENTB = cstb_t[:]
    Sc.dma("sp", cst, consts_d, w=["cst"])
    Sc.dve(lambda e: e.tensor_copy(out=IDENTB, in_=IDENT), r=["cst"], w=["cstb"])
    Sc.barrier()

    class Arena:
        def __init__(s): s.off = 0; s.gen = 0
        def reset(s): s.off = 0; s.gen += 1
        def f32(s, n, name):
            a = arena_t[:, s.off:s.off + n]; s.off += n
            assert s.off <= ARENA_WORDS, (name, s.off)
            return a, (name, s.gen)
        def bf16(s, n, name):
            w = (n + 1) // 2
            a = arena_t[:, s.off:s.off + w].bitcast(BF16)[:, 0:n]; s.off += w
            assert s.off <= ARENA_WORDS, (name, s.off)
            return a, (name, s.gen)
    AR = Arena()

    def bank_f32(i): return banks[i][:]
    def bank_bf16(i): return banks[i][:].bitcast(BF16)

    rot = {"acc": 0, "aux": 0}
    def next_acc():
        i = rot["acc"] % 6; rot["acc"] += 1; return i
    def next_aux():
        i = 6 + rot["aux"] % 2; rot["aux"] += 1; return i

    def bcast_row(ap_row, n):
        return ap_row.partition_broadcast(128).rearrange("p o n -> p (o n)")

    def make_prov_tm(src, K, TG, prologue, extra_src=None):
        kc_n = K // 128
        hT, hT_tok = AR.bf16(kc_n * TG, "hT")
        hT3 = hT.rearrange("p (k t) -> p k t", t=TG)
        xin, xin_tok = AR.f32(K, "xin")
        xb, xb_tok = AR.bf16(K, "xb")
        zin = None
        if extra_src is not None:
            zin, zin_tok = AR.f32(K, "zin")
        ntt = TG // 128

        def prov(tg):
            for tt in range(ntt):
                r0 = tg * TG + tt * 128
                Sc.dma("sp", xin, src[r0:r0 + 128, :], w=[xin_tok])
                rd = [xin_tok]
                if zin is not None:
                    Sc.dma("sp", zin, extra_src[r0:r0 + 128, :], w=[zin_tok])
                    rd.append(zin_tok)
                prologue(xin, zin, xb, rd, xb_tok)
                for k0 in range(0, kc_n, 4):
                    nk = min(4, kc_n - k0)
                    bi = next_aux()
                    pb = bank_bf16(bi)
                    for kk in range(nk):
                        Sc.pe(lambda e, kk=kk, k0=k0, pb=pb: e.transpose(
                            out=pb[:, kk * 128:(kk + 1) * 128], in_=xb[:, (k0 + kk) * 128:(k0 + kk + 1) * 128], identity=IDENTB),
                            r=[xb_tok, "cstb"], w=[("ps", bi)])
                    src_ap = pb[:, 0:nk * 128].rearrange("p (k t) -> p k t", t=128)
                    dst_ap = hT3[:, k0:k0 + nk, tt * 128:(tt + 1) * 128]
                    if (k0 // 4) % 2 == 0:
                        Sc.dve(lambda e, d=dst_ap, s_=src_ap: e.tensor_copy(out=d, in_=s_), r=[("ps", bi)], w=[hT_tok])
                    else:
                        Sc.act(lambda e, d=dst_ap, s_=src_ap: e.activation(out=d, in_=s_, func=AF.Copy), r=[("ps", bi)], w=[hT_tok])
        return prov, hT3, hT_tok

    def make_prov_fm(src, K, TG):
        kc_n = K // 128
        hT, hT_tok = AR.bf16(kc_n * TG, "hT")
        hT3 = hT.rearrange("p (k t) -> p k t", t=TG)

        def prov(tg):
            step = 16
            for k0 in range(0, kc_n, step):
                nk = min(step, kc_n - k0)
                Sc.dma("sp", hT3[:, k0:k0 + nk, :],
                       src[k0 * 128:(k0 + nk) * 128, tg * TG:(tg + 1) * TG].rearrange("(k p) t -> p k t", p=128),
                       w=[(hT_tok, k0 // step)])
        return prov, hT3, (lambda ks_: (hT_tok, ks_))

    def rms_prologue_factory(gain_row_ap, K, group=None, post_scale=1.0):
        gain_b, gain_tok = AR.f32(K, "gain")
        sq, sq_tok = AR.f32(K, "sq")
        ng = 1 if group is None else K // group
        st, st_tok = AR.f32(4 * ng + 4, "stat")
        Sc.dma("sp", gain_b, bcast_row(gain_row_ap, K), w=[gain_tok])
        gsz = K if group is None else group

        def prologue(xin, zin, xb, rd, xb_tok):
            Sc.act(lambda e: e.activation(out=sq, in_=xin, func=AF.Square), r=rd[:1], w=[sq_tok])
            ss = st[:, 0:ng]; rs = st[:, ng:2 * ng]
            Sc.dve(lambda e: e.tensor_reduce(out=ss, in_=sq.rearrange("p (g c) -> p g c", c=gsz), axis=AX.X, op=ALU.add),
                   r=[sq_tok], w=[st_tok])
            ps2 = post_scale * post_scale
            Sc.dve(lambda e: e.tensor_scalar(out=rs, in0=ss, scalar1=1.0 / (gsz * ps2), scalar2=RMS_EPS / ps2, op0=ALU.mult, op1=ALU.add),
                   r=[st_tok], w=[st_tok])
            Sc.act(lambda e: e.activation(out=rs, in_=rs, func=AF.Sqrt), r=[st_tok], w=[st_tok])
            Sc.dve(lambda e: e.reciprocal(out=rs, in_=rs), r=[st_tok], w=[st_tok])
            if group is None:
                Sc.dve(lambda e: e.scalar_tensor_tensor(out=xb, in0=xin, scalar=rs[:, 0:1], in1=gain_b, op0=ALU.mult, op1=ALU.mult),
                       r=[rd[0], st_tok, gain_tok], w=[xb_tok])
            else:
                if zin is not None:
                    Sc.act(lambda e: e.activation(out=zin, in_=zin, func=AF.Silu), r=[rd[1]], w=[rd[1]])
                x3_ = xin.rearrange("p (g c) -> p g c", c=gsz)
                rsb = rs.unsqueeze(2).broadcast_to([128, ng, gsz])
                sq3 = sq.rearrange("p (g c) -> p g c", c=gsz)
                Sc.dve(lambda e: e.tensor_tensor(out=sq3, in0=x3_, in1=rsb, op=ALU.mult), r=[rd[0], st_tok, sq_tok], w=[sq_tok])
                if zin is not None:
                    Sc.dve(lambda e: e.tensor_tensor(out=sq, in0=sq, in1=gain_b, op=ALU.mult), r=[sq_tok, gain_tok], w=[sq_tok])
                    Sc.dve(lambda e: e.tensor_tensor(out=xb, in0=sq, in1=zin, op=ALU.mult), r=[sq_tok, rd[1]], w=[xb_tok])
                else:
                    Sc.dve(lambda e: e.tensor_tensor(out=xb, in0=sq, in1=gain_b, op=ALU.mult), r=[sq_tok, gain_tok], w=[xb_tok])
        return prologue

    def phase_A():
        AR.reset()
        TG = 512 if S_ >= 512 else S_
        ntg = S_ // TG
        prol = rms_prologue_factory(a_norm, D)
        prov, hT3, hT_tok = make_prov_tm(x_in, D, TG, prol)
        NCH = 3 * H
        cw_t, cw_tok = AR.f32(NCH * 4, "convw")
        cw3 = cw_t.rearrange("p (c k) -> p c k", k=4)
        Sc.dma("sp", cw3, a_conv_t.rearrange("(c p) k -> p c k", p=128), w=[cw_tok])
        halo, halo_tok = AR.f32(NCH * 3, "halo")
        halo3 = halo.rearrange("p (c k) -> p c k", k=3)
        Sc.dve(lambda e: e.memset(halo, 0.0), w=[halo_tok])
        NSLV[0] = 2
        pbufs = [AR.f32(3 + TG, "pbuf%d" % i) for i in range(2)]
        ybufs = [AR.bf16(TG, "ybuf%d" % i) for i in range(8)]
        yfs = [AR.f32(TG, "yf%d" % i) for i in range(2)]
        ycnt = [0]
        obufs = [AR.bf16(TG, "obuf%d" % i) for i in range(2)]
        sqb, sqb_tok = AR.f32(TG, "sqb")
        stt, stt_tok = AR.f32(16, "stt")
        zbufs = [AR.f32(512, "zbuf%d" % i) for i in range(2)]
        negA, negA_tok = AR.f32(H, "negA")
        dtb, dtb_tok = AR.f32(H, "dtb")
        Sc.dma("sp", negA, bcast_row(a_A_log, H), w=[negA_tok])
        Sc.dma("sp", dtb, bcast_row(a_dt_bias, H), w=[dtb_tok])
        Sc.act(lambda e: e.activation(out=negA, in_=negA, func=AF.Exp), r=[negA_tok], w=[negA_tok])
        Sc.dve(lambda e: e.tensor_scalar(out=negA, in0=negA, scalar1=-1.0, scalar2=None, op0=ALU.mult), r=[negA_tok], w=[negA_tok])
        gbs = [AR.f32(2 * H, "gb%d" % i) for i in range(2)]
        gtmp = [AR.f32(2 * H, "gtmp%d" % i) for i in range(2)]
        gbTs = [AR.f32(128, "gbTs%d" % i) for i in range(2)]
        cnt = [0]
        ntt = TG // 128

        def epi(tg, bi_, segs, accs):
            c0 = segs[0][0]
            if c0 < 3 * D:
                stage2 = []
                for ai, b in enumerate(accs):
                    ch = c0 // 128 + ai
                    which = ch // H; head = ch % H
                    i2 = cnt[0] % 2; cnt[0] += 1
                    (pb, pb_tok), (ob, ob_tok) = pbufs[i2], obufs[i2]
                    yb, yb_tok = ybufs[ycnt[0] % 8]; ycnt[0] += 1
                    Sc.dve(lambda e, pb=pb, ch=ch: e.tensor_copy(out=pb[:, 0:3], in_=halo3[:, ch, :]), r=[halo_tok], w=[pb_tok])
                    Sc.act(lambda e, pb=pb, b=b: e.activation(out=pb[:, 3:3 + TG], in_=bank_f32(b)[:, 0:TG], func=AF.Copy),
                           r=[("ps", b)], w=[pb_tok])
                    Sc.dve(lambda e, pb=pb, ch=ch: e.tensor_copy(out=halo3[:, ch, :], in_=pb[:, TG:TG + 3]), r=[pb_tok], w=[halo_tok])
                    yf, yf_tok = yfs[i2]
                    Sc.dve(lambda e, pb=pb, yf=yf, ch=ch: e.tensor_scalar(out=yf, in0=pb[:, 0:TG], scalar1=cw3[:, ch, 0:1], scalar2=None, op0=ALU.mult),
                           r=[pb_tok, cw_tok], w=[yf_tok])
                    for k in range(1, 4):
                        Sc.dve(lambda e, pb=pb, yf=yf, ch=ch, k=k: e.scalar_tensor_tensor(
                            out=yf, in0=pb[:, k:k + TG], scalar=cw3[:, ch, k:k + 1], in1=yf, op0=ALU.mult, op1=ALU.add),
                            r=[pb_tok, cw_tok, yf_tok], w=[yf_tok])
                    Sc.act(lambda e, yb=yb, yf=yf: e.activation(out=yb, in_=yf, func=AF.Silu), r=[yf_tok], w=[yb_tok])

                    def st2(yb=yb, yb_tok=yb_tok, ob=ob, ob_tok=ob_tok, which=which, head=head, tg=tg):
                        xb_ = next_aux()
                        for tt in range(ntt):
                            Sc.pe(lambda e, yb=yb, tt=tt, xb_=xb_: e.transpose(out=bank_bf16(xb_)[:, tt * 128:(tt + 1) * 128],
                                  in_=yb[:, tt * 128:(tt + 1) * 128], identity=IDENTB), r=[yb_tok, "cstb"], w=[("ps", xb_)])
                        if which < 2:
                            Sc.act(lambda e, xb_=xb_: e.activation(out=sqb, in_=bank_bf16(xb_)[:, 0:TG], func=AF.Square), r=[("ps", xb_)], w=[sqb_tok])
                            Sc.dve(lambda e: e.tensor_reduce(out=stt[:, 0:ntt], in_=sqb.rearrange("p (t c) -> p t c", c=128), axis=AX.X, op=ALU.add),
                                   r=[sqb_tok], w=[stt_tok])
                            Sc.dve(lambda e: e.tensor_scalar(out=stt[:, 4:4 + ntt], in0=stt[:, 0:ntt], scalar1=1e-6, scalar2=None, op0=ALU.add),
                                   r=[stt_tok], w=[stt_tok])
                            Sc.act(lambda e: e.activation(out=stt[:, 4:4 + ntt], in_=stt[:, 4:4 + ntt], func=AF.Sqrt), r=[stt_tok], w=[stt_tok])
                            Sc.dve(lambda e: e.reciprocal(out=stt[:, 4:4 + ntt], in_=stt[:, 4:4 + ntt]), r=[stt_tok], w=[stt_tok])
                            for tt in range(ntt):
                                Sc.dve(lambda e, ob=ob, tt=tt, xb_=xb_: e.tensor_scalar(out=ob[:, tt * 128:(tt + 1) * 128],
                                       in0=bank_bf16(xb_)[:, tt * 128:(tt + 1) * 128], scalar1=stt[:, 4 + tt:5 + tt], scalar2=None, op0=ALU.mult),
                                       r=[("ps", xb_), stt_tok], w=[ob_tok])
                        else:
                            Sc.act(lambda e, ob=ob, xb_=xb_: e.activation(out=ob, in_=bank_bf16(xb_)[:, 0:TG], func=AF.Copy), r=[("ps", xb_)], w=[ob_tok])
                        dst = (q_tm, k_tm, v_tm)[which]
                        Sc.dma("sp", dst[tg * TG:(tg + 1) * TG, head * 128:(head + 1) * 128].rearrange("(t p) c -> p t c", p=128),
                               ob.rearrange("p (t c) -> p t c", c=128), r=[ob_tok])
                    stage2.append(st2)
                return (lambda: [f() for f in stage2])
            elif c0 < 4 * D:
                ncols = sum(n for _, n in segs)
                for tt, b in enumerate(accs):
                    i2 = cnt[0] % 2; cnt[0] += 1
                    zb, zb_tok = zbufs[i2]
                    Sc.act(lambda e, zb=zb, b=b: e.activation(out=zb[:, 0:ncols], in_=bank_f32(b)[:, 0:ncols], func=AF.Copy), r=[("ps", b)], w=[zb_tok])
                    r0 = tg * TG + tt * 128
                    Sc.dma("sp", z_tm[r0:r0 + 128, c0 - 3 * D:c0 - 3 * D + ncols], zb[:, 0:ncols], r=[zb_tok])
            else:
                for tt, b in enumerate(accs):
                    i2 = cnt[0] % 2; cnt[0] += 1
                    (gb, gb_tok), (gt, gt_tok), (gT, gT_tok) = gbs[i2], gtmp[i2], gbTs[i2]
                    pb = bank_f32(b)
                    Sc.act(lambda e, gb=gb, pb=pb: e.activation(out=gb[:, H:2 * H], in_=pb[:, 0:H], func=AF.Sigmoid), r=[("ps", b)], w=[gb_tok])
                    Sc.dve(lambda e, gt=gt, pb=pb: e.tensor_tensor(out=gt[:, 0:H], in0=pb[:, H:2 * H], in1=dtb, op=ALU.add), r=[("ps", b), dtb_tok], w=[gt_tok])
                    Sc.act(lambda e, gt=gt: e.activation(out=gt[:, 0:H], in_=gt[:, 0:H], func=AF.Exp), r=[gt_tok], w=[gt_tok])
                    Sc.dve(lambda e, gt=gt: e.tensor_scalar(out=gt[:, 0:H], in0=gt[:, 0:H], scalar1=1.0, scalar2=None, op0=ALU.add), r=[gt_tok], w=[gt_tok])
                    Sc.act(lambda e, gt=gt: e.activation(out=gt[:, 0:H], in_=gt[:, 0:H], func=AF.Ln), r=[gt_tok], w=[gt_tok])
                    Sc.dve(lambda e, gt=gt: e.tensor_tensor(out=gt[:, H:2 * H], in0=gt[:, 0:H], in1=negA, op=ALU.mult), r=[gt_tok, negA_tok], w=[gt_tok])
                    xb_ = next_aux()
                    Sc.pe(lambda e, gt=gt, xb_=xb_: e.matmul(out=bank_f32(xb_)[:, 0:H], lhsT=LCUM, rhs=gt[:, H:2 * H], start=True, stop=True),
                          r=[gt_tok, "cst"], w=[("ps", xb_)])
                    Sc.dve(lambda e, gb=gb, xb_=xb_: e.tensor_copy(out=gb[:, 0:H], in_=bank_f32(xb_)[:, 0:H]), r=[("ps", xb_)], w=[gb_tok])
                    r0 = tg * TG + tt * 128
                    Sc.dma("sp", gb_tm[r0:r0 + 128, :], gb, r=[gb_tok])
                    xc_ = next_aux()
                    Sc.pe(lambda e, gb=gb, xc_=xc_: e.transpose(out=bank_f32(xc_)[0:2 * H, 0:128], in_=gb, identity=IDENT), r=[gb_tok, "cst"], w=[("ps", xc_)])
                    Sc.act(lambda e, gT=gT, xc_=xc_: e.activation(out=gT[0:2 * H, :], in_=bank_f32(xc_)[0:2 * H, 0:128], func=AF.Copy), r=[("ps", xc_)], w=[gT_tok])
                    Sc.dma("sp", gbT[:, r0:r0 + 128], gT[0:2 * H, :], r=[gT_tok])

        n_fm = 3 * D // 512; n_z = D // 512
        for tg in range(ntg):
            prov(tg)
            linear_one_tg("a_w_in", range(0, n_fm), hT3, hT_tok, TG, "fm", epi, tg)
            linear_one_tg("a_w_in", range(n_fm, n_fm + n_z + 1), hT3, hT_tok, TG, "tm", epi, tg)

    WREG = {}
    conv_q = []

    def reg_weight(name, w_ap, K, blocks):
        kc_n = K // 128
        W2 = nc.dram_tensor("W2_" + name, [len(blocks), 128, kc_n, 512], BF16, kind="Internal").ap()
        WREG[name] = dict(K=K, blocks=blocks, W2=W2, name=name)
        for b, segs in enumerate(blocks):
            off = 0
            for (c0, n) in segs:
                conv_q.append((W2[b, :, :, off:off + n], w_ap[:, c0:c0 + n].rearrange("(k p) c -> p k c", p=128), ("W2", name, b)))
                off += n

    def issue_conv(n):
        for _ in range(n):
            if not conv_q:
                return
            o_, i_, tok = conv_q.pop(0)
            Sc.dma("pool", o_, i_, w=[tok], bg=True, max_dma_last_dim=2048)
            conv_last[tok] = len(conv_q)

    conv_last = {}
    conv_need = {}

    def ensure_conv(tok):
        while any(t == tok for (_, _, t) in conv_q[:4]) or (tok not in conv_last):
            if not conv_q:
                break
            issue_conv(1)

    lin_state = {"pending": []}
    NSLV = [3]
    KS = 16

    def flush_pending():
        p = lin_state["pending"]
        lin_state["pending"] = []
        for f in p:
            f()

    def linear_one_tg(wname, blk_idx, hT3, hT_tok, TG, mode, epilogue, tg, pre_block=None):
        NSL = NSLV[0]
        hT_tokf = hT_tok if callable(hT_tok) else (lambda ks_: hT_tok)
        wr = WREG[wname]
        K = wr["K"]; W2 = wr["W2"]
        kc_n = K // 128
        nks = (kc_n + KS - 1) // KS
        key = AR.gen
        if lin_state.get("gen") != key:
            slots = []
            for i in range(NSL):
                a, tok = AR.bf16(KS * 512, "wslot%d" % i)
                slots.append((a.rearrange("p (k c) -> p k c", c=512), tok))
            lin_state["gen"] = key; lin_state["slots"] = slots; lin_state["cnt"] = 0
        slots = lin_state["slots"]
        ntt = TG // 128
        for bi_ in blk_idx:
            segs = wr["blocks"][bi_]
            ncols = sum(n for _, n in segs)
            nacc = (ncols + 127) // 128 if mode == "fm" else ntt
            accs = [next_acc() for _ in range(nacc)]
            if pre_block is not None:
                pre_block(tg, bi_)
            for ks in range(nks):
                k0 = ks * KS; nk = min(KS, kc_n - k0)
                sl, sl_tok = slots[lin_state["cnt"] % NSL]; lin_state["cnt"] += 1
                hT_tok_ = hT_tokf(ks)
                ensure_conv(("W2", wname, bi_))
                Sc.dma("pool", sl[:, 0:nk, 0:ncols], W2[bi_, :, k0:k0 + nk, 0:ncols], r=[("W2", wname, bi_)], w=[sl_tok])
                issue_conv(1)
                if ks == 0 and nacc >= 4:
                    h2 = nacc // 2
                    order = [(kk, ai) for kk in range(nk) for ai in range(h2)] + [(kk, ai) for kk in range(nk) for ai in range(h2, nacc)]
                else:
                    order = [(kk, ai) for kk in range(nk) for ai in range(nacc)]
                for (kk, ai) in order:
                    first = (ks == 0 and kk == 0); last = (ks == nks - 1 and kk == nk - 1)
                    if True:
                        if mode == "fm":
                            cw = min(128, ncols - ai * 128)
                            Sc.pe(lambda e, ai=ai, kk=kk, k0=k0, sl=sl, cw=cw, first=first, last=last, b=accs[ai]: e.matmul(
                                out=bank_f32(b)[0:cw, 0:TG], lhsT=sl[:, kk, ai * 128:ai * 128 + cw], rhs=hT3[:, k0 + kk, :],
                                start=first, stop=last), r=[sl_tok, hT_tok_], w=[("ps", accs[ai])])
                        else:
                            Sc.pe(lambda e, ai=ai, kk=kk, k0=k0, sl=sl, ncols=ncols, first=first, last=last, b=accs[ai]: e.matmul(
                                out=bank_f32(b)[:, 0:ncols], lhsT=hT3[:, k0 + kk, ai * 128:(ai + 1) * 128], rhs=sl[:, kk, 0:ncols],
                                start=first, stop=last), r=[sl_tok, hT_tok_], w=[("ps", accs[ai])])
            flush_pending()
            d_ = epilogue(tg, bi_, segs, accs)
            if d_ is not None:
                lin_state["pending"].append(d_)

    def phase_B():
        AR.reset()
        scale = 128.0 ** -0.5
        gbcol, gbcol_tok = AR.f32(NT * 2 * H, "gbcol")
        gbcol3 = gbcol.rearrange("p (t c) -> p t c", c=2 * H)
        Sc.dma("sp", gbcol3, gb_tm.rearrange("(t p) c -> p t c", p=128), w=[gbcol_tok])
        bgcol, bgcol_tok = AR.f32(NT * H, "bgcol")
        bgcol3 = bgcol.rearrange("p (t c) -> p t c", c=H)
        Sc.act(lambda e: e.activation(out=bgcol3, in_=gbcol3[:, :, 0:H], func=AF.Exp), r=[gbcol_tok], w=[bgcol_tok])
        Sc.dve(lambda e: e.tensor_tensor(out=bgcol3, in0=bgcol3, in1=gbcol3[:, :, H:2 * H], op=ALU.mult), r=[bgcol_tok, gbcol_tok], w=[bgcol_tok])
        G = 8 if H >= 8 else (4 if H >= 4 else 1)
        streams = []
        for gi in range(G):
            st = {}
            def T(n, name, gi=gi):
                return AR.f32(n, "%s_%d" % (name, gi))
            def TB(n, name, gi=gi):
                return AR.bf16(n, "%s_%d" % (name, gi))
            st["dl"] = T(2, "dl")
            for nm in ("q0", "k0", "v0", "q1", "k1", "v1"):
                st[nm] = TB(128, nm)
            for nm in ("Gr0", "Gr1", "Br0", "Br1", "DT", "DTs", "X", "A", "R", "P", "Q", "P2", "Q2",
                       "u", "EG", "osb0", "osb1", "Sst"):
                st[nm] = T(128, nm)
            for nm in ("kT", "qT", "AcT", "vb", "kbg", "wT", "qdT", "kdA", "kdB", "vnew", "Sb", "Rb"):
                st[nm] = TB(128, nm)
            streams.append(st)
        F32R = mybir.dt.float32r
        def fr(ap):
            return ap.bitcast(F32R) if INV_F32R else ap
        class SRec:
            def __init__(s, gi):
                s.lst = []; s.gi = gi; s.rot = 0
            def _add(s, eng, fn, r, w, dma=False):
                rec = _Rec(); fn(rec)
                s.lst.append((eng, rec.calls[0], list(r), list(w), dma))
            def pe(s, fn, r=(), w=()): s._add("pe", fn, r, w)
            def act(s, fn, r=(), w=()): s._add("act", fn, r, w)
            def dve(s, fn, r=(), w=()): s._add("dve", fn, r, w)
            def dma(s, q, out, in_, r=(), w=(), **kw):
                s._add(q, lambda e: e.dma_start(out=out, in_=in_, **kw), r, w, dma=True)
            def nb(s):
                return (s.gi, 0)

        def pf(bk):
            return banks[bk[0]][:, bk[1] * 128:(bk[1] + 1) * 128]
        def pb16(bk):
            return banks[bk[0]][:].bitcast(BF16)[:, bk[1] * 256:bk[1] * 256 + 128]

        def head_gen(st, h, Sc, nb):
            dl, dl_tok = st["dl"]
            Sst, Sst_tok = st["Sst"]
            Sc.dve(lambda e: e.memset(Sst, 0.0), w=[Sst_tok])
            Sb, Sb_tok = st["Sb"]
            Sc.dve(lambda e: e.memset(Sb, 0.0), w=[Sb_tok])
            vnew, vnew_tok = st["vnew"]
            Sc.dve(lambda e: e.memset(vnew, 0.0), w=[vnew_tok])
            kdA, kdA_tok = st["kdA"]; kdB, kdB_tok = st["kdB"]
            Sc.dve(lambda e: e.memset(kdA, 0.0), w=[kdA_tok])
            Sc.dve(lambda e: e.memset(kdB, 0.0), w=[kdB_tok])
            yield
            for t in range(NT):
                par = t % 2
                (qt, qt_tok), (kt, kt_tok), (vt, vt_tok) = st["q%d" % par], st["k%d" % par], st["v%d" % par]
                r0 = t * 128
                hs = slice(h * 128, (h + 1) * 128)
                Sc.dma("sp", qt, q_tm[r0:r0 + 128, hs], w=[qt_tok])
                Sc.dma("sp", kt, k_tm[r0:r0 + 128, hs], w=[kt_tok])
                Sc.dma("sp", vt, v_tm[r0:r0 + 128, hs], w=[vt_tok])
                gcol = gbcol3[:, t, h:h + 1]; bcol = gbcol3[:, t, H + h:H + h + 1]; bgc = bgcol3[:, t, h:h + 1]
                Gr, Grow_tok = st["Gr%d" % par]; Br, Brow_tok = st["Br%d" % par]
                Sc.dma("sp", Gr, gbT[h:h + 1, r0:r0 + 128].partition_broadcast(128).rearrange("p o n -> p (o n)"), w=[Grow_tok])
                Sc.dma("sp", Br, gbT[H + h:H + h + 1, r0:r0 + 128].partition_broadcast(128).rearrange("p o n -> p (o n)"), w=[Brow_tok])
                Sc.dve(lambda e, Gr=Gr, t=t: e.tensor_tensor(out=dl[0:64, 0:1], in0=Gr[0:64, 63:64], in1=gbcol3[0:64, t, h:h + 1], op=ALU.subtract),
                       r=[Grow_tok, gbcol_tok], w=[dl_tok])
                Sc.dve(lambda e, Gr=Gr, t=t: e.tensor_tensor(out=dl[64:128, 0:1], in0=Gr[64:128, 127:128], in1=gbcol3[64:128, t, h:h + 1], op=ALU.subtract),
                       r=[Grow_tok, gbcol_tok], w=[dl_tok])
                Sc.act(lambda e: e.activation(out=dl[:, 0:1], in_=dl[:, 0:1], func=AF.Exp), r=[dl_tok], w=[dl_tok])
                kT, kT_tok = st["kT"]; qT, qT_tok = st["qT"]
                b1 = nb()
                Sc.pe(lambda e, b1=b1: e.transpose(out=pb16(b1), in_=kt, identity=IDENTB), r=[kt_tok, "cstb"], w=[("psg", b1)])
                Sc.act(lambda e, b1=b1: e.activation(out=kT, in_=pb16(b1), func=AF.Copy), r=[("psg", b1)], w=[kT_tok])
                b2 = nb()
                Sc.pe(lambda e, b2=b2: e.transpose(out=pb16(b2), in_=qt, identity=IDENTB), r=[qt_tok, "cstb"], w=[("psg", b2)])
                Sc.dve(lambda e, b2=b2: e.tensor_copy(out=qT, in_=pb16(b2)), r=[("psg", b2)], w=[qT_tok])
                yield
                DT, DT_tok = st["DT"]; DTs, DTs_tok = st["DTs"]; EG, EG_tok = st["EG"]
                Sc.dve(lambda e: e.scalar_tensor_tensor(out=DT, in0=Gr, scalar=gcol, in1=MASKNEG, op0=ALU.subtract, op1=ALU.add),
                       r=[Grow_tok, gbcol_tok, "cst"], w=[DT_tok])
                Sc.act(lambda e: e.activation(out=DT, in_=DT, func=AF.Exp), r=[DT_tok], w=[DT_tok])
                Sc.act(lambda e: e.activation(out=EG, in_=Gr, func=AF.Exp), r=[Grow_tok], w=[EG_tok])
                Sc.dve(lambda e: e.tensor_tensor(out=DTs, in0=DT, in1=STRICT, op=ALU.mult), r=[DT_tok, "cst"], w=[DTs_tok])
                X, X_tok = st["X"]; A, A_tok = st["A"]; R, R_tok = st["R"]; AcT, AcT_tok = st["AcT"]
                bkk = nb()
                Sc.pe(lambda e, bkk=bkk: e.matmul(out=pf(bkk), lhsT=kT, rhs=kT, start=True, stop=True), r=[kT_tok], w=[("psg", bkk)])
                Sc.dve(lambda e, bkk=bkk: e.tensor_tensor(out=fr(X), in0=pf(bkk), in1=Br, op=ALU.mult), r=[("psg", bkk), Brow_tok], w=[X_tok])
                bqk = nb()
                Sc.pe(lambda e, bqk=bqk: e.matmul(out=pf(bqk), lhsT=kT, rhs=qT, start=True, stop=True), r=[kT_tok, qT_tok], w=[("psg", bqk)])
                Sc.dve(lambda e: e.tensor_tensor(out=fr(X), in0=X, in1=DTs, op=ALU.mult), r=[X_tok, DTs_tok], w=[X_tok])
                Sc.dve(lambda e, bqk=bqk: e.scalar_tensor_tensor(out=AcT, in0=pf(bqk), scalar=scale, in1=DT, op0=ALU.mult, op1=ALU.mult),
                       r=[("psg", bqk), DT_tok], w=[AcT_tok])
                yield
                ba = nb()
                Sc.pe(lambda e, ba=ba: e.transpose(out=pf(ba), in_=X, identity=IDENT), r=[X_tok, "cst"], w=[("psg", ba)])
                Sc.act(lambda e, ba=ba: e.activation(out=fr(A), in_=pf(ba), func=AF.Copy), r=[("psg", ba)], w=[A_tok])
                Sc.dve(lambda e: e.tensor_tensor(out=fr(R), in0=IDENT, in1=X, op=ALU.subtract), r=["cst", X_tok], w=[R_tok])
                Pc, Pc_tok = X, X_tok
                Qc, Qc_tok = A, A_tok
                pq = [(st["P"], st["Q"]), (st["P2"], st["Q2"])]
                for kstage in range(1, 6):
                    (Pn, Pn_tok), (Qn, Qn_tok) = pq[kstage % 2]
                    bq = nb()
                    Sc.pe(lambda e, bq=bq, Pc=Pc, Qc=Qc: e.matmul(out=pf(bq), lhsT=fr(Pc), rhs=fr(Qc), start=True, stop=True),
                          r=[Pc_tok, Qc_tok], w=[("psg", bq)])
                    Sc.dve(lambda e, bq=bq, Qn=Qn: e.tensor_copy(out=fr(Qn), in_=pf(bq)), r=[("psg", bq)], w=[Qn_tok])
                    if kstage < 5:
                        bp = nb()
                        Sc.pe(lambda e, bp=bp, Pc=Pc, Qc=Qc: e.matmul(out=pf(bp), lhsT=fr(Qc), rhs=fr(Pc), start=True, stop=True),
                              r=[Pc_tok, Qc_tok], w=[("psg", bp)])
                        Sc.act(lambda e, bp=bp, Pn=Pn: e.activation(out=fr(Pn), in_=pf(bp), func=AF.Copy), r=[("psg", bp)], w=[Pn_tok])
                    bm = nb()
                    Sc.pe(lambda e, bm=bm, Qn=Qn: e.matmul(out=pf(bm), lhsT=fr(Qn), rhs=fr(R), start=True, stop=True),
                          r=[Qn_tok, R_tok], w=[("psg", bm)])
                    Sc.dve(lambda e, bm=bm: e.tensor_tensor(out=fr(R), in0=R, in1=pf(bm), op=ALU.add), r=[R_tok, ("psg", bm)], w=[R_tok])
                    Pc, Pc_tok, Qc, Qc_tok = Pn, Pn_tok, Qn, Qn_tok
                    yield
                vb, vb_tok = st["vb"]; kbg, kbg_tok = st["kbg"]; u, u_tok = st["u"]; wT, wT_tok = st["wT"]; qdT, qdT_tok = st["qdT"]
                Sc.dve(lambda e: e.tensor_scalar(out=vb, in0=vt, scalar1=bcol, scalar2=None, op0=ALU.mult), r=[vt_tok, gbcol_tok], w=[vb_tok])
                Sc.dve(lambda e: e.tensor_scalar(out=kbg, in0=kt, scalar1=bgc, scalar2=None, op0=ALU.mult), r=[kt_tok, bgcol_tok], w=[kbg_tok])
                Rb, Rb_tok = st["Rb"]
                Sc.act(lambda e: e.activation(out=Rb, in_=R, func=AF.Copy), r=[R_tok], w=[Rb_tok])
                bu = nb()
                Sc.pe(lambda e, bu=bu: e.matmul(out=pf(bu), lhsT=Rb, rhs=vb, start=True, stop=True), r=[Rb_tok, vb_tok], w=[("psg", bu)])
                Sc.act(lambda e, bu=bu: e.activation(out=u, in_=pf(bu), func=AF.Copy), r=[("psg", bu)], w=[u_tok])
                bw = nb()
                Sc.pe(lambda e, bw=bw: e.matmul(out=pf(bw), lhsT=kbg, rhs=Rb, start=True, stop=True), r=[Rb_tok, kbg_tok], w=[("psg", bw)])
                Sc.dve(lambda e, bw=bw: e.tensor_copy(out=wT, in_=pf(bw)), r=[("psg", bw)], w=[wT_tok])
                Sc.dve(lambda e: e.scalar_tensor_tensor(out=qdT, in0=qT, scalar=scale, in1=EG, op0=ALU.mult, op1=ALU.mult), r=[qT_tok, EG_tok], w=[qdT_tok])
                Sc.dve(lambda e: e.tensor_scalar(out=kdA[0:64, :], in0=kt[0:64, :], scalar1=dl[0:64, 0:1], scalar2=None, op0=ALU.mult),
                       r=[kt_tok, dl_tok], w=[kdA_tok])
                Sc.dve(lambda e: e.tensor_scalar(out=kdB[64:128, :], in0=kt[64:128, :], scalar1=dl[64:128, 0:1], scalar2=None, op0=ALU.mult),
                       r=[kt_tok, dl_tok], w=[kdB_tok])
                yield
                osb, osb_tok = st["osb%d" % par]
                for c in range(2):
                    rs_ = slice(64 * c, 64 * c + 64)
                    kd, kd_tok = (kdA, kdA_tok) if c == 0 else (kdB, kdB_tok)
                    bws = nb()
                    Sc.pe(lambda e, bws=bws: e.matmul(out=pf(bws), lhsT=wT, rhs=Sb, start=True, stop=True), r=[wT_tok, Sb_tok], w=[("psg", bws)])
                    Sc.dve(lambda e, bws=bws, rs_=rs_: e.tensor_tensor(out=vnew[rs_, :], in0=u[rs_, :], in1=pf(bws)[rs_, :], op=ALU.subtract),
                           r=[u_tok, ("psg", bws)], w=[vnew_tok])
                    bo = nb()
                    Sc.pe(lambda e, bo=bo: e.matmul(out=pf(bo), lhsT=qdT, rhs=Sb, start=True, stop=False), r=[qdT_tok, Sb_tok], w=[("psg", bo)])
                    Sc.pe(lambda e, bo=bo: e.matmul(out=pf(bo), lhsT=AcT, rhs=vnew, start=False, stop=True), r=[AcT_tok, vnew_tok], w=[("psg", bo)])
                    Sc.act(lambda e, bo=bo, rs_=rs_: e.activation(out=osb[rs_, :], in_=pf(bo)[rs_, :], func=AF.Copy), r=[("psg", bo)], w=[osb_tok])
                    bs = nb()
                    Sc.pe(lambda e, bs=bs, kd=kd: e.matmul(out=pf(bs), lhsT=kd, rhs=vnew, start=True, stop=True), r=[kd_tok, vnew_tok], w=[("psg", bs)])
                    sdcol = EG[:, 64 * c + 63:64 * c + 64]
                    Sc.dve(lambda e, bs=bs, sdcol=sdcol: e.scalar_tensor_tensor(out=Sb, in0=Sst, scalar=sdcol, in1=pf(bs), op0=ALU.mult, op1=ALU.add),
                           r=[Sst_tok, EG_tok, ("psg", bs)], w=[Sb_tok])
                    Sc.dve(lambda e, bs=bs, sdcol=sdcol: e.scalar_tensor_tensor(out=Sst, in0=Sst, scalar=sdcol, in1=pf(bs), op0=ALU.mult, op1=ALU.add),
                           r=[Sst_tok, EG_tok, ("psg", bs)], w=[Sst_tok])
                    yield
                Sc.dma("sp", o_gdn[r0:r0 + 128, hs], osb, r=[osb_tok])

        for h0 in range(0, H, G):
            recs = []
            for gi in range(min(G, H - h0)):
                sr = SRec(gi)
                for _ in head_gen(streams[gi], h0 + gi, sr, sr.nb):
                    pass
                recs.append(sr.lst)
            idx = [0] * len(recs)
            remaining = sum(len(l) for l in recs)
            while remaining:
                for gi, l in enumerate(recs):
                    if idx[gi] < len(l):
                        eng, call, r_, w_, dma_ = l[idx[gi]]; idx[gi] += 1; remaining -= 1
                        Sc.op(eng, call, r_, w_, dma=dma_)

    def resid_linear(wname, prov, hT3, hT_tok, TG, resid_src, dst, ntg):
        NRB = 8
        rbufs = [AR.f32(512, "rbuf%d" % i) for i in range(NRB)]
        cnt = [0]
        cur = {}
        ntt_ = TG // 128

        def pre(tg, bi_):
            segs = WREG[wname]["blocks"][bi_]
            c0 = segs[0][0]; ncols = segs[0][1]
            lst = []
            for tt in range(ntt_):
                i2 = cnt[0] % NRB; cnt[0] += 1
                rb, rb_tok = rbufs[i2]
                r0 = tg * TG + tt * 128
                Sc.dma("sp", rb[:, 0:ncols], resid_src[r0:r0 + 128, c0:c0 + ncols], w=[rb_tok])
                lst.append((rb, rb_tok))
            cur[(tg, bi_)] = lst

        def epi(tg, bi_, segs, accs):
            c0 = segs[0][0]; ncols = segs[0][1]
            lst = cur.pop((tg, bi_))
            for tt, b in enumerate(accs):
                rb, rb_tok = lst[tt]
                r0 = tg * TG + tt * 128
                Sc.dve(lambda e, rb=rb, b=b: e.tensor_tensor(out=rb[:, 0:ncols], in0=rb[:, 0:ncols], in1=bank_f32(b)[:, 0:ncols], op=ALU.add),
                       r=[rb_tok, ("ps", b)], w=[rb_tok])
                Sc.dma("sp", dst[r0:r0 + 128, c0:c0 + ncols], rb[:, 0:ncols], r=[rb_tok])
        nb_ = len(WREG[wname]["blocks"])
        for tg in range(ntg):
            prov(tg)
            linear_one_tg(wname, range(nb_), hT3, hT_tok, TG, "tm", epi, tg, pre_block=pre)

    def phase_C():
        AR.reset()
        NSLV[0] = 2
        TG = 512 if S_ >= 512 else S_
        prol = rms_prologue_factory(a_out_norm_t, D, group=128)
        prov, hT3, hT_tok = make_prov_tm(o_gdn, D, TG, prol, extra_src=z_tm)
        resid_linear("a_w_out", prov, hT3, hT_tok, TG, x_in, x1, S_ // TG)

    def phase_FFN(layer, src, dst):
        AR.reset()
        NSLV[0] = 3
        TG = 512 if S_ >= 512 else S_
        ntg = S_ // TG
        prol = rms_prologue_factory(ffn_norm[layer:layer + 1, :], D)
        prov, hT3, hT_tok = make_prov_tm(src, D, TG, prol)
        sgb = [AR.f32(TG, "sg%d" % i) for i in range(2)]
        acb = [AR.bf16(TG, "ac%d" % i) for i in range(2)]
        cnt = [0]

        def epi(tg, bi_, segs, accs):
            f0 = segs[0][0]; nf = segs[0][1]
            nch = nf // 128
            for ci in range(nch):
                i2 = cnt[0] % 2; cnt[0] += 1
                (sg, sg_tok), (ac, ac_tok) = sgb[i2], acb[i2]
                bg_, bu_ = accs[ci], accs[nch + ci]
                Sc.act(lambda e, sg=sg, bg_=bg_: e.activation(out=sg, in_=bank_f32(bg_)[:, 0:TG], func=AF.Silu), r=[("ps", bg_)], w=[sg_tok])
                Sc.dve(lambda e, sg=sg, ac=ac, bu_=bu_: e.tensor_tensor(out=ac, in0=sg, in1=bank_f32(bu_)[:, 0:TG], op=ALU.mult),
                       r=[sg_tok, ("ps", bu_)], w=[ac_tok])
                fr = f0 + ci * 128
                Sc.dma("sp", actT[fr:fr + 128, tg * TG:(tg + 1) * TG], ac, r=[ac_tok])
        nb_ = len(WREG["gu%d" % layer]["blocks"])
        for tg in range(ntg):
            prov(tg)
            linear_one_tg("gu%d" % layer, range(nb_), hT3, hT_tok, TG, "fm", epi, tg)
        Sc.barrier()
        AR.reset()
        TG2 = 512 if S_ >= 512 else S_
        prov2, hT3b, hT_tokb = make_prov_fm(actT, F, TG2)
        resid_linear("dn%d" % layer, prov2, hT3b, hT_tokb, TG2, src, dst, S_ // TG2)

    def phase_E():
        TG = 512 if S_ >= 512 else S_
        ntg = S_ // TG

        def run(norm_row, wname, ncols_total, gain_row, dstT, do_v):
            AR.reset()
            prol = rms_prologue_factory(norm_row, D)
            prov, hT3, hT_tok = make_prov_tm(x2, D, TG, prol)
            gn, gn_tok = AR.f32(128, "gn")
            Sc.dma("sp", gn, bcast_row(gain_row, 128), w=[gn_tok])
            sqb, sqb_tok = AR.f32(512, "sqb")
            stt, stt_tok = AR.f32(16, "stt")
            NSLV[0] = 3
            knb = [AR.bf16(512, "kn%d" % i) for i in range(8)]
            kcnt = [0]
            kTb = [AR.bf16(512, "kTb%d" % i) for i in range(2)]
            vbb = [AR.bf16(512, "vbb%d" % i) for i in range(2)]
            cnt = [0]

            def epi(tg, bi_, segs, accs):
                c0 = segs[0][0]
                stage2 = []
                for tt, b in enumerate(accs):
                    i2 = cnt[0] % 2; cnt[0] += 1
                    r0 = tg * TG + tt * 128
                    pb = bank_f32(b)
                    if c0 < D:
                        kTs, kTs_tok = kTb[i2]
                        kn, kn_tok = knb[kcnt[0] % 8]; kcnt[0] += 1
                        Sc.act(lambda e, pb=pb: e.activation(out=sqb, in_=pb[:, 0:512], func=AF.Square), r=[("ps", b)], w=[sqb_tok])
                        Sc.dve(lambda e: e.tensor_reduce(out=stt[:, 0:4], in_=sqb.rearrange("p (g c) -> p g c", c=128), axis=AX.X, op=ALU.add), r=[sqb_tok], w=[stt_tok])
                        Sc.dve(lambda e: e.tensor_scalar(out=stt[:, 4:8], in0=stt[:, 0:4], scalar1=1.0 / 128, scalar2=RMS_EPS, op0=ALU.mult, op1=ALU.add), r=[stt_tok], w=[stt_tok])
                        Sc.act(lambda e: e.activation(out=stt[:, 4:8], in_=stt[:, 4:8], func=AF.Sqrt), r=[stt_tok], w=[stt_tok])
                        Sc.dve(lambda e: e.reciprocal(out=stt[:, 4:8], in_=stt[:, 4:8]), r=[stt_tok], w=[stt_tok])
                        for g in range(4):
                            Sc.dve(lambda e, kn=kn, pb=pb, g=g: e.scalar_tensor_tensor(out=kn[:, g * 128:(g + 1) * 128], in0=pb[:, g * 128:(g + 1) * 128],
                                   scalar=stt[:, 4 + g:5 + g], in1=gn, op0=ALU.mult, op1=ALU.mult), r=[("ps", b), stt_tok, gn_tok], w=[kn_tok])
                        def st2(kn=kn, kn_tok=kn_tok, kTs=kTs, kTs_tok=kTs_tok, c0=c0, r0=r0):
                            xb_ = next_aux()
                            for g in range(4):
                                Sc.pe(lambda e, kn=kn, g=g, xb_=xb_: e.transpose(out=bank_bf16(xb_)[:, g * 128:(g + 1) * 128], in_=kn[:, g * 128:(g + 1) * 128], identity=IDENTB),
                                      r=[kn_tok, "cstb"], w=[("ps", xb_)])
                            Sc.act(lambda e, kTs=kTs, xb_=xb_: e.activation(out=kTs, in_=bank_bf16(xb_)[:, 0:512], func=AF.Copy), r=[("ps", xb_)], w=[kTs_tok])
                            g0 = c0 // 128
                            Sc.dma("sp", dstT[g0:g0 + 4, :, r0:r0 + 128].rearrange("g p t -> p g t"), kTs.rearrange("p (g t) -> p g t", t=128), r=[kTs_tok])
                        stage2.append(st2)
                    else:
                        vbt, vbt_tok = vbb[i2]
                        Sc.act(lambda e, vbt=vbt, pb=pb: e.activation(out=vbt, in_=pb[:, 0:512], func=AF.Copy), r=[("ps", b)], w=[vbt_tok])
                        Sc.dma("sp", vb_d[r0:r0 + 128, c0 - D:c0 - D + 512], vbt, r=[vbt_tok])
                return (lambda: [f() for f in stage2]) if stage2 else None
            nb_ = len(WREG[wname]["blocks"])
            for tg in range(ntg):
                prov(tg)
                linear_one_tg(wname, range(nb_), hT3, hT_tok, TG, "tm", epi, tg)
            Sc.barrier()
        run(kv_norm, "w_kv", 2 * D, k_norm, kT_d, True)
        run(b_norm, "b_w_q", D, b_q_norm, qT_d, False)

    def phase_F(lam_init):
        AR.reset()
        scale = 128.0 ** -0.5
        QG = 512 if S_ >= 512 else S_
        nqg = S_ // QG
        ntq = QG // 128
        lp, lp_tok = AR.f32(512, "lp")
        Sc.dma("sp", lp, bcast_row(b_lambda, 512), w=[lp_tok])
        lt, lt_tok = AR.f32(272, "lt")
        Sc.dve(lambda e: e.tensor_tensor(out=lt[:, 0:128], in0=lp[:, 0:128], in1=lp[:, 128:256], op=ALU.mult), r=[lp_tok], w=[lt_tok])
        Sc.dve(lambda e: e.tensor_tensor(out=lt[:, 128:256], in0=lp[:, 256:384], in1=lp[:, 384:512], op=ALU.mult), r=[lp_tok], w=[lt_tok])
        Sc.dve(lambda e: e.tensor_reduce(out=lt[:, 256:258], in_=lt[:, 0:256].rearrange("p (g c) -> p g c", c=128), axis=AX.X, op=ALU.add), r=[lt_tok], w=[lt_tok])
        Sc.act(lambda e: e.activation(out=lt[:, 258:260], in_=lt[:, 256:258], func=AF.Exp), r=[lt_tok], w=[lt_tok])
        Sc.dve(lambda e: e.tensor_tensor(out=lt[:, 260:261], in0=lt[:, 259:260], in1=lt[:, 258:259], op=ALU.subtract), r=[lt_tok], w=[lt_tok])
        Sc.dve(lambda e: e.tensor_scalar(out=lt[:, 261:262], in0=lt[:, 260:261], scalar1=-lam_init, scalar2=None, op0=ALU.add), r=[lt_tok], w=[lt_tok])
        neglam = lt[:, 261:262]
        kTs = [AR.bf16(S_, "kTs%d" % m) for m in range(2)]
        qTs = [AR.bf16(S_, "qTs%d" % m) for m in range(2)]
        vx, vx_tok = AR.bf16(NT * 258, "vx")
        vx3 = vx.rearrange("p (t c) -> p t c", c=258)
        Sc.dve(lambda e: e.memset(vx, 1.0), w=[vx_tok])
        Eb = [AR.bf16(QG, "E%d" % i) for i in range(3)]
        Em = [AR.bf16(128, "Em%d" % i) for i in range(2)]
        AMB, AMB_tok = AR.bf16(128, "AMB")
        Sc.dve(lambda e: e.tensor_copy(out=AMB, in_=AMASK), r=["cst"], w=[AMB_tok])
        Osb = [[AR.f32(257, "O%d_%d" % (m, tt)) for tt in range(ntq)] for m in range(2)]
        outb = [AR.f32(256, "ob%d" % i) for i in range(2)]
        rcp, rcp_tok = AR.f32(8, "rcp")
        ecnt = [0]; ocnt = [0]
        for h in range(DH):
            for m in range(2):
                Sc.dma("sp", kTs[m][0], kT_d[2 * h + m], w=[kTs[m][1]])
                Sc.dma("sp", qTs[m][0], qT_d[2 * h + m], w=[qTs[m][1]])
            Sc.dma("sp", vx3[:, :, 0:256], vb_d[:, h * 256:(h + 1) * 256].rearrange("(t p) c -> p t c", p=128), w=[vx_tok])
            for qg in range(nqg):
                for m in range(2):
                    kTm, kT_tok = kTs[m]; qTm, qT_tok = qTs[m]
                    obanks = [0, 1, 2, 3][:ntq]
                    nkt = (qg + 1) * ntq
                    def issue_st(kt_):
                        sb_ = 4 + (ecnt[0] % 3)
                        E, E_tok = Eb[ecnt[0] % 3]; ecnt[0] += 1
                        Sc.pe(lambda e, sb_=sb_, kt_=kt_, kTm=kTm, qTm=qTm, qg=qg: e.matmul(out=bank_f32(sb_)[:, 0:QG], lhsT=kTm[:, kt_ * 128:(kt_ + 1) * 128],
                              rhs=qTm[:, qg * QG:(qg + 1) * QG], start=True, stop=True), r=[kT_tok, qT_tok], w=[("ps", sb_)])
                        Sc.act(lambda e, sb_=sb_, E=E: e.activation(out=E, in_=bank_f32(sb_)[:, 0:QG], func=AF.Exp, scale=scale), r=[("ps", sb_)], w=[E_tok])
                        return E, E_tok
                    pend = [issue_st(0)]
                    if nkt > 1:
                        pend.append(issue_st(1))
                    for kt_ in range(nkt):
                        E, E_tok = pend.pop(0)
                        if kt_ + 2 < nkt:
                            pend.append(issue_st(kt_ + 2))
                        for tt in range(ntq):
                            qt_ = qg * ntq + tt
                            if qt_ < kt_:
                                continue
                            lhs = E[:, tt * 128:(tt + 1) * 128]; lhs_tok = E_tok
                            if qt_ == kt_:
                                Emm, Emm_tok = Em[tt % 2]
                                Sc.dve(lambda e, Emm=Emm, lhs=lhs: e.tensor_tensor(out=Emm, in0=lhs, in1=AMB, op=ALU.mult), r=[E_tok, AMB_tok], w=[Emm_tok])
                                lhs, lhs_tok = Emm, Emm_tok
                            Sc.pe(lambda e, tt=tt, lhs=lhs, kt_=kt_, qt_=qt_: e.matmul(out=bank_f32(obanks[tt])[:, 0:257], lhsT=lhs, rhs=vx3[:, kt_, 0:257],
                                  start=(kt_ == 0), stop=(kt_ == qt_)), r=[lhs_tok, vx_tok], w=[("ps", obanks[tt])])
                    for tt in range(ntq):
                        O, O_tok = Osb[m][tt]
                        if tt % 2 == 0:
                            Sc.act(lambda e, O=O, tt=tt: e.activation(out=O, in_=bank_f32(obanks[tt])[:, 0:257], func=AF.Copy), r=[("ps", obanks[tt])], w=[O_tok])
                        else:
                            Sc.dve(lambda e, O=O, tt=tt: e.tensor_copy(out=O, in_=bank_f32(obanks[tt])[:, 0:257]), r=[("ps", obanks[tt])], w=[O_tok])
                for tt in range(ntq):
                    (O0, O0_tok), (O1, O1_tok) = Osb[0][tt], Osb[1][tt]
                    ob, ob_tok = outb[ocnt[0] % 2]; ocnt[0] += 1
                    Sc.dve(lambda e, O0=O0: e.reciprocal(out=rcp[:, 0:1], in_=O0[:, 256:257]), r=[O0_tok], w=[rcp_tok])
                    Sc.dve(lambda e, O1=O1: e.reciprocal(out=rcp[:, 1:2], in_=O1[:, 256:257]), r=[O1_tok], w=[rcp_tok])
                    Sc.dve(lambda e: e.tensor_tensor(out=rcp[:, 2:3], in0=rcp[:, 1:2], in1=neglam, op=ALU.mult), r=[rcp_tok, lt_tok], w=[rcp_tok])
                    Sc.dve(lambda e, ob=ob, O0=O0: e.tensor_scalar(out=ob, in0=O0[:, 0:256], scalar1=rcp[:, 0:1], scalar2=None, op0=ALU.mult), r=[O0_tok, rcp_tok], w=[ob_tok])
                    Sc.dve(lambda e, ob=ob, O1=O1: e.scalar_tensor_tensor(out=ob, in0=O1[:, 0:256], scalar=rcp[:, 2:3], in1=ob, op0=ALU.mult, op1=ALU.add),
                           r=[O1_tok, rcp_tok, ob_tok], w=[ob_tok])
                    r0 = qg * QG + tt * 128
                    Sc.dma("sp", o_att[r0:r0 + 128, h * 256:(h + 1) * 256], ob, r=[ob_tok])

    def phase_G(lam_init):
        AR.reset()
        NSLV[0] = 3
        TG = 512 if S_ >= 512 else S_
        prol = rms_prologue_factory(b_sub_norm_t, D, group=256, post_scale=(1.0 - lam_init))
        prov, hT3, hT_tok = make_prov_tm(o_att, D, TG, prol)
        resid_linear("b_w_out", prov, hT3, hT_tok, TG, x2, x3, S_ // TG)

    _orig_barrier = Sc.barrier
    def _barrier():
        flush_pending()
        _orig_barrier()
    Sc.barrier = _barrier
    lam_init = 0.8 - 0.6 * math.exp(-0.3 * 1)
    blkD = [[(c, 512)] for c in range(0, D, 512)]
    reg_weight("a_w_in", a_w_in, D, [[(c, 512)] for c in range(0, 4 * D, 512)] + [[(4 * D, 2 * H)]])
    reg_weight("a_w_out", a_w_out, D, blkD)
    reg_weight("gu0", ffn_w_gu[0], D, [[(f, 256), (F + f, 256)] for f in range(0, F, 256)])
    reg_weight("dn0", ffn_w_down[0], F, blkD)
    reg_weight("w_kv", w_kv, D, [[(c, 512)] for c in range(0, 2 * D, 512)])
    reg_weight("b_w_q", b_w_q, D, blkD)
    reg_weight("b_w_out", b_w_out, D, blkD)
    reg_weight("gu1", ffn_w_gu[1], D, [[(f, 256), (F + f, 256)] for f in range(0, F, 256)])
    reg_weight("dn1", ffn_w_down[1], F, blkD)
    issue_conv(len(WREG["a_w_in"]["blocks"]))
    phase_A(); Sc.barrier()
    phase_B(); Sc.barrier()
    phase_C(); Sc.barrier()
    phase_FFN(0, x1, x2); Sc.barrier()
    phase_E()
    phase_F(lam_init); Sc.barrier()
    phase_G(lam_init); Sc.barrier()
    phase_FFN(1, x3, out_d)
    Sc.finish()
    Sc.emit(nc, stack)
    stack.close()
    return nc


def prep_inputs(inputs, cfg, b):
    f = lambda a: np.ascontiguousarray(np.asarray(a, dtype=np.float32))
    H = cfg.H
    m = {
        "x": f(inputs["x"][b]),
        "a_norm": f(inputs["a_norm"]).reshape(1, -1),
        "a_w_in": f(inputs["a_w_in"][0]),
        "a_conv_t": f(np.asarray(inputs["a_conv"][0]).T),
        "a_A_log": f(inputs["a_A_log"]).reshape(1, -1),
        "a_dt_bias": f(inputs["a_dt_bias"]).reshape(1, -1),
        "a_out_norm_t": f(np.tile(np.asarray(inputs["a_out_norm"][0]), H)).reshape(1, -1),
        "a_w_out": f(inputs["a_w_out"][0]),
        "kv_norm": f(inputs["kv_norm"]).reshape(1, -1),
        "w_kv": f(inputs["w_kv"]),
        "k_norm": f(inputs["k_norm"]).reshape(1, -1),
        "b_norm": f(inputs["b_norm"]).reshape(1, -1),
        "b_w_q": f(inputs["b_w_q"][0]),
        "b_q_norm": f(inputs["b_q_norm"]).reshape(1, -1),
        "b_lambda": f(inputs["b_lambda"][0]).reshape(1, -1),
        "b_sub_norm_t": f(np.tile(np.asarray(inputs["b_sub_norm"][0]), cfg.DH)).reshape(1, -1),
        "b_w_out": f(inputs["b_w_out"][0]),
        "ffn_norm": f(inputs["ffn_norm"]),
        "ffn_w_gate_up": f(inputs["ffn_w_gate_up"]),
        "ffn_w_down": f(inputs["ffn_w_down"]),
        "consts": make_consts(),
    }
    return m


def run(inputs, cfg, debug=False, trace=False):
    nc = build(cfg, debug=debug)
    B = np.asarray(inputs["x"]).shape[0]
    in_maps = [prep_inputs(inputs, cfg, c) for c in range(B)]
    res = run_bass_kernel_spmd(nc, in_maps, core_ids=list(range(B)), **({"trace": True} if trace else {}))
    return res


def kernel(**inputs):
    cfg = Cfg(4096, 4096, 11008)
    res = run(inputs, cfg)
    B = np.asarray(inputs["x"]).shape[0]
    return np.stack([np.asarray(res.results[b]["out"], dtype=np.float32) for b in range(B)], axis=0)
```
